# Optimizing a Trainium2 kernel written in Bass

```python
import math
import jax, jax.numpy as jnp
from jax import lax
import numpy as np

D_MODEL = 1024
BATCH = 16
SEQ = 2048
DEPTH = 1
DEC_BATCH = 2
DEC_SEQ = 16384
PAST_LEN = 128

N_META = 16
HEAD_DIM = 64
N_HEADS = 8
N_KV_HEADS = 2
GQA_GROUP = N_HEADS // N_KV_HEADS
W_ATT = N_HEADS * HEAD_DIM
W_KV = N_KV_HEADS * HEAD_DIM
W_CONV = 512
D_MIX = W_ATT + W_CONV
WINDOW = 128
BLOCK = 128
CONV_K = 31
CONV_PAD = CONV_K // 2
NORM_EPS = 1e-6
LN_EPS = 1e-5
SPLIT_SIZES = (W_ATT, W_KV, W_KV, W_ATT, W_CONV, W_CONV, W_CONV)
IN_DIM = sum(SPLIT_SIZES)
SPLIT_IDX = tuple(int(i) for i in np.cumsum(SPLIT_SIZES)[:-1])

kernel_name = "hymba_conformer_swa_encoder"


def rmsnorm(x, w):
    xf = x.astype(jnp.float32)
    y = xf * lax.rsqrt(jnp.mean(xf * xf, axis=-1, keepdims=True) + NORM_EPS)
    return (y * w.astype(jnp.float32)).astype(x.dtype)


def layernorm(x, w, b):
    xf = x.astype(jnp.float32)
    mu = jnp.mean(xf, axis=-1, keepdims=True)
    var = jnp.mean(jnp.square(xf - mu), axis=-1, keepdims=True)
    y = (xf - mu) * lax.rsqrt(var + LN_EPS)
    return (y * w.astype(jnp.float32) + b.astype(jnp.float32)).astype(x.dtype)


def alibi_slopes():
    h = jnp.arange(1, N_HEADS + 1, dtype=jnp.float32)
    return jnp.exp2(-8.0 * h / N_HEADS).reshape(N_KV_HEADS, GQA_GROUP, 1, 1)


def windowed_gqa(q, k, v, sink):
    B, L, H, hd = q.shape
    S = L - N_META
    nb = S // BLOCK
    scale = hd ** -0.5
    slopes = alibi_slopes()
    qg = q.reshape(B, L, N_KV_HEADS, GQA_GROUP, hd)
    qm, qr = qg[:, :N_META], qg[:, N_META:]
    km, kr = k[:, :N_META], k[:, N_META:]
    vm, vr = v[:, :N_META], v[:, N_META:]

    qb = qr.reshape(B, nb, BLOCK, N_KV_HEADS, GQA_GROUP, hd)
    pad = ((0, 0), (BLOCK, BLOCK), (0, 0), (0, 0))
    kp = jnp.pad(kr, pad).reshape(B, nb + 2, BLOCK, N_KV_HEADS, hd)
    vp = jnp.pad(vr, pad).reshape(B, nb + 2, BLOCK, N_KV_HEADS, hd)
    kw = jnp.concatenate([kp[:, :-2], kp[:, 1:-1], kp[:, 2:]], axis=2)
    vw = jnp.concatenate([vp[:, :-2], vp[:, 1:-1], vp[:, 2:]], axis=2)

    rel = jnp.arange(3 * BLOCK)[None, :] - BLOCK - jnp.arange(BLOCK)[:, None]
    dist = jnp.abs(rel).astype(jnp.float32)
    key_idx = jnp.arange(nb)[:, None] * BLOCK + jnp.arange(3 * BLOCK)[None, :] - BLOCK
    in_range = (key_idx >= 0) & (key_idx < S)
    valid = (jnp.abs(rel) <= WINDOW)[None] & in_range[:, None, :]

    s_win = jnp.einsum('bnqkgd,bnskd->bnkgqs', qb, kw,
                       preferred_element_type=jnp.float32) * scale
    s_win = s_win - (slopes * dist)[None, None]
    s_win = jnp.where(valid[None, :, None, None], s_win, -jnp.inf)
    s_meta = jnp.einsum('bnqkgd,bmkd->bnkgqm', qb, km,
                        preferred_element_type=jnp.float32) * scale
    sink_f = sink.astype(jnp.float32).reshape(N_KV_HEADS, GQA_GROUP, 1, 1)
    sink_b = jnp.broadcast_to(sink_f, (B, nb, N_KV_HEADS, GQA_GROUP, BLOCK, 1))
    p = jax.nn.softmax(jnp.concatenate([sink_b, s_meta, s_win], axis=-1), axis=-1)
    p = p.astype(v.dtype)
    o = (jnp.einsum('bnkgqm,bmkd->bnqkgd', p[..., 1:1 + N_META], vm)
         + jnp.einsum('bnkgqs,bnskd->bnqkgd', p[..., 1 + N_META:], vw))
    o_real = o.reshape(B, S, H, hd)

    kh = k[:, :N_META + BLOCK]
    vh = v[:, :N_META + BLOCK]
    rel_h = jnp.arange(N_META + BLOCK)[None, :] - jnp.arange(N_META)[:, None]
    dist_h = jnp.abs(rel_h).astype(jnp.float32)
    s_h = jnp.einsum('btkgd,bskd->bkgts', qm, kh,
                     preferred_element_type=jnp.float32) * scale
    s_h = s_h - slopes * dist_h
    s_h = jnp.where((jnp.abs(rel_h) <= WINDOW)[None, None, None], s_h, -jnp.inf)
    sink_h = jnp.broadcast_to(sink_f, (B, N_KV_HEADS, GQA_GROUP, N_META, 1))
    p_h = jax.nn.softmax(jnp.concatenate([sink_h, s_h], axis=-1), axis=-1).astype(v.dtype)
    o_meta = jnp.einsum('bkgts,bskd->btkgd', p_h[..., 1:], vh).reshape(B, N_META, H, hd)
    return jnp.concatenate([o_meta, o_real], axis=1)


def conformer_conv(c_val, c_glu, conv_w, conv_b, ln_w, ln_b):
    u = c_val * jax.nn.sigmoid(c_glu)
    C = u.shape[-1]
    y = lax.conv_general_dilated(
        u, conv_w.astype(u.dtype)[:, None, :], window_strides=(1,),
        padding=[(CONV_PAD, CONV_PAD)], dimension_numbers=('NWC', 'WIO', 'NWC'),
        feature_group_count=C) + conv_b.astype(u.dtype)
    return jax.nn.silu(layernorm(y, ln_w, ln_b))


def hybrid_layer(z, norm_w, w_in, q_norm_w, k_norm_w, sink, conv_w, conv_b, ln_w, ln_b, w_out):
    B, L, _ = z.shape
    h = rmsnorm(z, norm_w)
    proj = jnp.einsum('bld,de->ble', h, w_in)
    q, k, v, g_att, c_val, c_glu, g_conv = jnp.split(proj, SPLIT_IDX, axis=-1)
    q = rmsnorm(q.reshape(B, L, N_HEADS, HEAD_DIM), q_norm_w)
    k = rmsnorm(k.reshape(B, L, N_KV_HEADS, HEAD_DIM), k_norm_w)
    v = v.reshape(B, L, N_KV_HEADS, HEAD_DIM)
    att = windowed_gqa(q, k, v, sink).reshape(B, L, W_ATT) * jax.nn.silu(g_att)
    cnv = conformer_conv(c_val, c_glu, conv_w, conv_b, ln_w, ln_b) * jax.nn.silu(g_conv)
    mixed = jnp.concatenate([att, cnv], axis=-1)
    return z + jnp.einsum('ble,ed->bld', mixed, w_out)


def setup_inputs(seed: int = 0) -> dict:
    key = jax.random.key(seed)
    ks = jax.random.split(key, 16)
    f32 = jnp.float32
    return {
        "x_prompt": jax.random.normal(ks[0], (BATCH, SEQ, D_MODEL), f32),
        "x_sample": jax.random.normal(ks[1], (DEC_BATCH, DEC_SEQ, D_MODEL), f32),
        "meta_tokens": jax.random.normal(ks[2], (N_META, D_MODEL), f32),
        "norm_w": 1.0 + 0.05 * jax.random.normal(ks[3], (DEPTH, D_MODEL), f32),
        "w_in": jax.random.normal(ks[4], (DEPTH, D_MODEL, IN_DIM), f32) * D_MODEL ** -0.5,
        "q_norm_w": 1.0 + 0.05 * jax.random.normal(ks[5], (DEPTH, HEAD_DIM), f32),
        "k_norm_w": 1.0 + 0.05 * jax.random.normal(ks[6], (DEPTH, HEAD_DIM), f32),
        "sink_logits": 0.5 * jax.random.normal(ks[7], (DEPTH, N_HEADS), f32),
        "conv_w": jax.random.normal(ks[8], (DEPTH, CONV_K, W_CONV), f32) * CONV_K ** -0.5,
        "conv_b": 0.01 * jax.random.normal(ks[9], (DEPTH, W_CONV), f32),
        "conv_ln_w": 1.0 + 0.05 * jax.random.normal(ks[10], (DEPTH, W_CONV), f32),
        "conv_ln_b": 0.01 * jax.random.normal(ks[11], (DEPTH, W_CONV), f32),
        "w_out": jax.random.normal(ks[12], (DEPTH, D_MIX, D_MODEL), f32) * D_MIX ** -0.5,
    }


def reference(x_prompt, x_sample, meta_tokens, norm_w, w_in, q_norm_w, k_norm_w, sink_logits,
              conv_w, conv_b, conv_ln_w, conv_ln_b, w_out):
    def run(x):
        B = x.shape[0]
        meta = jnp.broadcast_to(meta_tokens.astype(x.dtype)[None], (B, N_META, D_MODEL))
        z = jnp.concatenate([meta, x], axis=1)
        for l in range(DEPTH):
            z = hybrid_layer(z, norm_w[l], w_in[l], q_norm_w[l], k_norm_w[l], sink_logits[l],
                             conv_w[l], conv_b[l], conv_ln_w[l], conv_ln_b[l], w_out[l])
        return z[:, N_META:]

    y_prompt = run(x_prompt)
    y_sample = run(x_sample)
    return (y_prompt, y_sample)
```

```python
import math
from contextlib import ExitStack

import numpy as np
import ml_dtypes
import concourse.bass as bass
import concourse.mybir as mybir
from concourse.bass_utils import run_bass_kernel_spmd

F32 = mybir.dt.float32
BF16 = mybir.dt.bfloat16
AF = mybir.ActivationFunctionType
ALU = mybir.AluOpType

D_MODEL = 1024
N_META = 16
N_HEADS = 8
CONV_K = 31
NEG = -30000.0

Q0, KD0, KD1, V0, GA0, CV0, CG0, GC0, WCOLS = 0, 512, 640, 768, 896, 1408, 1920, 2432, 2944


class _Op:
    __slots__ = ("eng", "fn", "deps", "signal", "sem", "val", "dma", "waits", "prev_val")

    def __init__(self, eng, fn, dma):
        self.eng = eng
        self.fn = fn
        self.dma = dma
        self.deps = set()
        self.signal = False
        self.sem = None
        self.val = 0
        self.prev_val = 0
        self.waits = []


class Sched:
    ENGS = ("pe", "act", "dve", "pool", "sp")

    def __init__(self):
        self.ops = {e: [] for e in self.ENGS}
        self.last_writer = {}
        self.readers = {}

    def add(self, eng, fn, reads=(), writes=(), dma=False):
        op = _Op(eng, fn, dma)
        deps = op.deps
        for r in reads:
            w = self.last_writer.get(r)
            if w is not None:
                deps.add(w)
        for w_ in writes:
            w = self.last_writer.get(w_)
            if w is not None:
                deps.add(w)
            for rd in self.readers.get(w_, ()):
                deps.add(rd)
        for r in reads:
            self.readers.setdefault(r, []).append(op)
        for w_ in writes:
            self.last_writer[w_] = op
            self.readers[w_] = []
        deps.discard(op)
        if eng == "pe" and not dma:
            op.deps = {d for d in deps if not (d.eng == "pe" and not d.dma)}
        for d in op.deps:
            d.signal = True
        self.ops[eng].append(op)
        return op

    def finalize(self, eng_sems, dma_sems):
        for e in self.ENGS:
            cnt = 0
            dcount = {}
            k = 0
            pool = dma_sems.get(e, [])
            for op in self.ops[e]:
                if op.dma:
                    s = pool[k % len(pool)]
                    k += 1
                    prev = dcount.get(id(s), 0)
                    op.sem = s
                    op.prev_val = prev
                    op.val = prev + 16
                    dcount[id(s)] = op.val
                elif op.signal:
                    cnt += 1
                    op.sem = eng_sems[e]
                    op.val = cnt
        for e in self.ENGS:
            waited = {}
            for op in self.ops[e]:
                need = {}
                for d in op.deps:
                    key = id(d.sem)
                    if d.val > need.get(key, (None, 0))[1]:
                        need[key] = (d.sem, d.val)
                if op.dma and op.prev_val > 0:
                    key = id(op.sem)
                    if op.prev_val > need.get(key, (None, 0))[1]:
                        need[key] = (op.sem, op.prev_val)
                for key, (s, v) in need.items():
                    if v > waited.get(key, 0):
                        waited[key] = v
                        op.waits.append((s, v))

    def emit(self, eng_name, e):
        for op in self.ops[eng_name]:
            for (s, v) in op.waits:
                e.wait_ge(s, v)
            ins = op.fn(e)
            if op.dma:
                ins.then_inc(op.sem, 16)
            elif op.signal:
                ins.then_inc(op.sem, 1)

    def final_waits(self, e):
        last = {}
        for q in self.ENGS:
            for op in self.ops[q]:
                if op.dma:
                    last[id(op.sem)] = (op.sem, op.val)
        for s, v in last.values():
            e.wait_ge(s, v)


def _build(NSEG, NB, STOP=9):
    assert NB % 4 == 0
    NST = NB // 4
    NBX = NB + 2
    TX = NBX * 128
    nc = bass.Bass("TRN2", target_bir_lowering=False)

    def din(name, shape, dt=F32):
        return nc.dram_tensor(name, shape, dt, kind="ExternalInput").ap()

    xe = din("xe", [NSEG, TX, D_MODEL])
    xmeta = din("xmeta", [128, D_MODEL])
    w_in = din("w_in", [D_MODEL, 2816])
    w_out = din("w_out", [D_MODEL, D_MODEL])
    normw_pk = din("normw_pk", [128, 8])
    qkw = din("qkw", [128, 2])
    convw_pk = din("convw_pk", [128, 4 * CONV_K])
    vec4 = din("vec4", [128, 12])
    sink_b = din("sink_b", [128, 8])
    hb_in = din("hb", [128, 2 * NSEG])
    ident_c = din("ident_c", [128, 128], BF16)
    bd_c = din("bd_c", [128, 128], BF16)
    onesln_c = din("onesln_c", [128, 128], BF16)
    bias_c = din("bias_c", [128, 3 * 8 * 128], BF16)
    y = nc.dram_tensor("y", [NSEG, NB * 128, D_MODEL], F32, kind="ExternalOutput").ap()

    S = Sched()
    with ExitStack() as st:
        E = st.enter_context

        def sb(name, shape, dt=F32):
            return E(nc.sbuf_tensor("sb_" + name, shape, dt))

        W = sb("W", [128, 8, WCOLS], BF16)
        WO = sb("WO", [128, 8, D_MODEL], BF16)
        DG = sb("DG", [128, 4, 15, 128], BF16)
        BIAS = sb("BIAS", [128, 3, 8, 128], BF16)
        IDN = sb("IDN", [128, 128], BF16)
        BD = sb("BD", [128, 128], BF16)
        ONL = sb("ONL", [128, 128], BF16)
        nwp = sb("nwp", [128, 8])
        qkv = sb("qkv", [128, 2])
        qks = sb("qks", [128, 2])
        cwp = sb("cwp", [128, 4 * CONV_K])
        v4 = sb("v4", [128, 12])
        v4h = sb("v4h", [128, 8])
        snk = sb("snk", [128, 8])
        esink = sb("esink", [128, 8])
        hb = sb("hb", [128, 2 * NSEG])
        mhalf = sb("mhalf", [128, 1])
        epsc = sb("epsc", [128, 1])
        kTd = sb("kTd", [128, 2, TX], BF16)
        vaug = sb("vaug", [128, NBX, 2, 66], BF16)
        uT = sb("uT", [128, 4, TX], BF16)
        kTm = sb("kTm", [128, 2, 128], BF16)
        vm = sb("vm", [128, 1, 2, 66], BF16)
        kTmZ = sb("kTmZ", [128, 2, 2, 48], BF16)
        hT = sb("hT", [128, 8, 512], BF16)
        qT = [sb(f"qT{i}", [128, 4, 4, 128], BF16) for i in range(2)]
        gaT = [sb(f"gaT{i}", [128, 4, 512], BF16) for i in range(2)]
        gcT = [sb(f"gcT{i}", [128, 4, 512], BF16) for i in range(2)]
        mixT = sb("mixT", [128, 4, 512], BF16)
        xt = [sb(f"xt{i}", [128, D_MODEL]) for i in range(3)]
        xres = xt
        htm = [sb(f"htm{i}", [128, D_MODEL], BF16) for i in range(4)]
        ss = [sb(f"ss{i}", [128, 1]) for i in range(4)]
        sq = [sb(f"sq{i}", [128, 512], BF16) for i in range(2)]
        tq = [sb(f"tq{i}", [128, 512]) for i in range(2)]
        th = [sb(f"th{i}", [128, 512]) for i in range(2)]
        NPT = 8
        PT = [sb(f"PT{i}", [128, 512], BF16) for i in range(NPT)]
        PTm = [sb(f"PTm{i}", [128, 512], BF16) for i in range(2)]
        On = [sb(f"On{i}", [128, 512], BF16) for i in range(2)]
        den = [sb(f"den{i}", [128, 8]) for i in range(2)]
        yb = sb("yb", [128, 4, 512], BF16)
        acc = sb("acc", [128, 4, 512])
        ysq = [sb(f"ysq{i}", [128, 512], BF16) for i in range(4)]
        lnr = sb("lnr", [128, 512])
        m2 = sb("m2", [128, 512])
        nmr = m2
        t2 = sb("t2", [128, 512])
        ta = sb("ta", [128, 512])

        banks = [E(nc.psum_tensor(f"bank{i}", [128, 512], F32)) for i in range(8)]
        sems = {e: E(nc.semaphore("s_" + e)) for e in Sched.ENGS}
        dsems = {"sp": [E(nc.semaphore(f"d{i}")) for i in range(24)],
                 "pool": [E(nc.semaphore(f"dp{i}")) for i in range(8)]}

        def BK(i):
            return ("bank", i)

        cnt = {"htm": 0, "xt": 0, "xres": 0, "sq": 0, "th": 0, "pt": 0, "ptm": 0, "on": 0, "ysq": 0, "proj": 0,
               "st2": 0, "cv": 0, "qslot": 0}

        def nxt(k, n):
            v = cnt[k] % n
            cnt[k] += 1
            return v

        def dma(out_ap, in_ap, reads=(), writes=()):
            S.add("sp", lambda e: e.dma_start(out=out_ap, in_=in_ap), reads=reads, writes=writes, dma=True)

        dma(nwp[:], normw_pk, writes=["nwp"])
        dma(qkv[:], qkw, writes=["qkv"])
        dma(cwp[:], convw_pk, writes=["cwp"])
        dma(v4[:], vec4, writes=["v4"])
        dma(snk[:], sink_b, writes=["snk"])
        dma(hb[:], hb_in, writes=["hb"])
        dma(IDN[:], ident_c, writes=["IDN"])
        dma(BD[:], bd_c, writes=["BD"])
        dma(ONL[:], onesln_c, writes=["ONL"])
        dma(BIAS[:].rearrange("p a h q -> p (a h q)"), bias_c, writes=["BIAS"])

        S.add("pool", lambda e: e.memset(mhalf[:], -0.5), writes=["mhalf"])
        S.add("pool", lambda e: e.memset(epsc[:], 1e-6), writes=["epsc"])
        S.add("pool", lambda e: e.memset(vaug[:, :, :, 64:66], 1.0), writes=[("vaug1",)])
        S.add("pool", lambda e: e.memset(vm[:, :, :, 64:66], 1.0), writes=[("vm1",)])
        for i in range(8):
            S.add("dve", (lambda i: lambda e: e.memset(banks[i][:], 0.0))(i), writes=[BK(i)])
        S.add("dve", lambda e: e.tensor_scalar(out=qks[:, 0:1], in0=qkv[:, 0:1], scalar1=0.125, scalar2=None,
                                               op0=ALU.mult), reads=["qkv"], writes=["qks0"])
        S.add("dve", lambda e: e.tensor_copy(out=qks[:, 1:2], in_=qkv[:, 1:2]), reads=["qkv"], writes=["qks1"])
        S.add("dve", lambda e: e.tensor_scalar(out=cwp[:], in0=cwp[:], scalar1=0.5, scalar2=None, op0=ALU.mult),
              reads=["cwp"], writes=["cwp"])
        S.add("dve", lambda e: e.tensor_scalar(out=v4h[:], in0=v4[:, 4:12], scalar1=0.5, scalar2=None,
                                               op0=ALU.mult), reads=["v4"], writes=["v4h"])
        S.add("act", lambda e: e.activation(out=esink[:], in_=snk[:], func=AF.Exp), reads=["snk"], writes=["esink"])

        def stage_view(slot):
            return xt[slot][:].rearrange("p (k n) -> p k n", k=8)

        nwb = nwp[:].unsqueeze(2).to_broadcast([128, 8, 128])

        def wpiece(c0):
            slot = nxt("xt", 3)
            sv = stage_view(slot)
            dma(sv, w_in[:, c0:c0 + 128].rearrange("(k p) n -> p k n", p=128), writes=[("xt", slot)])
            if c0 < 512:
                dsts = [(Q0 + c0, 0, 128)]
            elif c0 == 512:
                dsts = [(KD0, 0, 64), (KD0 + 64, 0, 64), (KD1, 64, 64), (KD1 + 64, 64, 64)]
            elif c0 == 640:
                dsts = [(V0, 0, 128)]
            elif c0 < 1280:
                dsts = [(GA0 + c0 - 768, 0, 128)]
            elif c0 < 1792:
                dsts = [(CV0 + c0 - 1280, 0, 128)]
            elif c0 < 2304:
                dsts = [(CG0 + c0 - 1792, 0, 128)]
            else:
                dsts = [(GC0 + c0 - 2304, 0, 128)]
            for (d0, s0, n) in dsts:
                S.add("dve", (lambda d0, s0, n, sv: lambda e: e.tensor_tensor(
                    out=W[:, :, d0:d0 + n], in0=sv[:, :, s0:s0 + n],
                    in1=nwp[:].unsqueeze(2).to_broadcast([128, 8, n]), op=ALU.mult))(d0, s0, n, sv),
                    reads=[("xt", slot), "nwp"], writes=[("W", d0)])

        for c0 in range(0, 2816, 128):
            wpiece(c0)

        def wopiece(c0):
            slot = nxt("xt", 3)
            sv = stage_view(slot)
            dma(sv, w_out[:, c0:c0 + 128].rearrange("(k p) n -> p k n", p=128), writes=[("xt", slot)])
            S.add("dve", lambda e: e.tensor_scalar(out=WO[:, 0:4, c0:c0 + 128], in0=sv[:, 0:4, :], scalar1=0.5,
                                                   scalar2=None, op0=ALU.mult),
                  reads=[("xt", slot)], writes=[("WO", c0, 0)])
            S.add("dve", lambda e: e.tensor_scalar(out=WO[:, 4:8, c0:c0 + 128], in0=sv[:, 4:8, :], scalar1=0.25,
                                                   scalar2=None, op0=ALU.mult),
                  reads=[("xt", slot)], writes=[("WO", c0, 1)])

        for c0 in range(0, D_MODEL, 128):
            wopiece(c0)

        def dgs(e):
            ins = None
            for cc in range(4):
                for k in range(16, CONV_K):
                    j = cc * CONV_K + k
                    ins = e.tensor_scalar(out=DG[:, cc, k - 16, :], in0=IDN[:], scalar1=cwp[:, j:j + 1], scalar2=None,
                                          op0=ALU.mult)
            return ins
        S.add("dve", dgs, reads=["IDN", "cwp"], writes=["DG"])
        WALL = [("W", d) for d in ([Q0 + i for i in range(0, 512, 128)] + [KD0, KD0 + 64, KD1, KD1 + 64, V0]
                                   + [b + i for b in (GA0, CV0, CG0, GC0) for i in range(0, 512, 128)])]
        WOALL = [("WO", c0, h) for c0 in range(0, D_MODEL, 128) for h in range(2)]

        def norm_pre(src, nblk):
            hs = []
            for j in range(nblk):
                slot = nxt("xt", 3)
                h_ = nxt("htm", 4)
                hs.append(h_)
                dma(xt[slot][:], src(j), writes=[("xt", slot)])
                S.add("act", (lambda slot, h_: lambda e: e.activation(out=htm[h_][:], in_=xt[slot][:], func=AF.Square,
                                                                       accum_out=ss[h_][:]))(slot, h_),
                      reads=[("xt", slot)], writes=[("htm", h_), ("ss", h_)])
                S.add("dve", (lambda h_: lambda e: e.tensor_scalar(out=ss[h_][:], in0=ss[h_][:], scalar1=1.0 / D_MODEL,
                                                                    scalar2=1e-6, op0=ALU.mult, op1=ALU.add))(h_),
                      reads=[("ss", h_)], writes=[("ss", h_)])
                S.add("pool", (lambda h_: lambda e: e.tensor_tensor(out=ss[h_][:], in0=ss[h_][:], in1=mhalf[:, 0:1],
                                                                     op=ALU.pow))(h_),
                      reads=[("ss", h_), "mhalf"], writes=[("ss", h_)])
                S.add("dve", (lambda slot, h_: lambda e: e.tensor_scalar(out=htm[h_][:], in0=xt[slot][:], scalar1=ss[h_][:],
                                                                          scalar2=None, op0=ALU.mult))(slot, h_),
                      reads=[("xt", slot), ("ss", h_)], writes=[("htm", h_)])
            return hs

        def norm_T(hs):
            for j, h_ in enumerate(hs):
                tb = 4 + (j % 2)
                trb = banks[tb][:].bitcast(BF16)

                def tr(e, h_=h_, trb=trb):
                    ins = None
                    for k in range(8):
                        ins = e.transpose(out=trb[:, k * 128:(k + 1) * 128], in_=htm[h_][:, k * 128:(k + 1) * 128],
                                          identity=IDN[:])
                    return ins
                S.add("pe", tr, reads=[("htm", h_), "IDN"], writes=[BK(tb)])
                S.add("act", (lambda j, trb: lambda e: e.activation(out=hT[:, :, j * 128:(j + 1) * 128],
                                                                     in_=trb.rearrange("p (k t) -> p k t", k=8),
                                                                     func=AF.Copy))(j, trb),
                      reads=[BK(tb)], writes=[("hT", j)])

        def proj_chunk(wcol, N, lo=0):
            b = (0, 1, 2, 4)[nxt("proj", 4)]

            def mm(e):
                ins = None
                for k in range(8):
                    ins = e.matmul(banks[b][:, 0:N], lhsT=W[:, k, wcol:wcol + 128], rhs=hT[:, k, lo:lo + N],
                                   start=(k == 0), stop=(k == 7))
                return ins
            S.add("pe", mm, reads=[("hT", j) for j in range((lo + N + 127) // 128)] + WALL, writes=[BK(b)])
            return b

        def silu2(b, N, out_ap, out_res):
            t = nxt("th", 2)
            S.add("act", lambda e: e.activation(out=th[t][:, 0:N], in_=banks[b][:, 0:N], func=AF.Tanh, scale=0.5),
                  reads=[BK(b)], writes=[("th", t)])
            S.add("dve", lambda e: e.scalar_tensor_tensor(out=out_ap, in0=th[t][:, 0:N], scalar=1.0,
                                                          in1=banks[b][:, 0:N], op0=ALU.add, op1=ALU.mult),
                  reads=[("th", t), BK(b)], writes=out_res)

        def qknorm(b, N, scol, out_ap, out_res, view3=False):
            s_ = nxt("sq", 2)
            stb = 3 if s_ == 0 else 5
            S.add("act", lambda e: e.activation(out=sq[s_][:, 0:N], in_=banks[b][:, 0:N], func=AF.Square),
                  reads=[BK(b)], writes=[("sq", s_)])
            S.add("pe", lambda e: e.matmul(banks[stb][:, 0:N], lhsT=BD[:], rhs=sq[s_][:, 0:N], start=True, stop=True),
                  reads=[("sq", s_), "BD"], writes=[BK(stb)])
            S.add("act", lambda e: e.activation(out=tq[s_][:, 0:N], in_=banks[stb][:, 0:N], func=AF.Ln, bias=epsc[:, 0:1]),
                  reads=[BK(stb), "epsc"], writes=[("tq", s_)])
            S.add("act", lambda e: e.activation(out=tq[s_][:, 0:N], in_=tq[s_][:, 0:N], func=AF.Exp, scale=-0.5),
                  reads=[("tq", s_)], writes=[("tq", s_)])
            v3 = (lambda ap: ap.rearrange("p (j t) -> p j t", t=128)) if view3 else (lambda ap: ap)
            S.add("dve", lambda e: e.scalar_tensor_tensor(out=out_ap, in0=v3(banks[b][:, 0:N]), scalar=qks[:, scol:scol + 1],
                                                          in1=v3(tq[s_][:, 0:N]), op0=ALU.mult, op1=ALU.mult),
                  reads=[BK(b), ("tq", s_), "qks0", "qks1"], writes=out_res)

        u_ready = set()

        def projA_jobs(b0, nblk, mode, qslot=None, seg=None):
            N = nblk * 128
            if mode == "meta":
                kdst, vdst, kres, vres = kTm, vm, lambda j: ("kTm",), lambda j: ("vm",)
                c0 = 0
            else:
                kdst, vdst, kres, vres = kTd, vaug, lambda j: ("kTd", b0 + j), lambda j: ("vaug", b0 + j)
                c0 = b0 * 128
            blks = range(nblk)
            jobs = []

            def job_v():
                def vmm(e):
                    ins = None
                    for j in range(nblk):
                        for k in range(8):
                            ins = e.matmul(banks[5][:, j * 128:(j + 1) * 128], lhsT=hT[:, k, j * 128:(j + 1) * 128],
                                           rhs=W[:, k, V0:V0 + 128], start=(k == 0), stop=(k == 7))
                    return ins
                S.add("pe", vmm, reads=[("hT", j) for j in blks] + WALL, writes=[BK(5)])
                vb0 = 0 if mode == "meta" else b0
                S.add("act", lambda e: e.activation(out=vdst[:, vb0:vb0 + nblk, :, 0:64],
                                                    in_=banks[5][:, 0:N].rearrange("p (j g d) -> p j g d", j=nblk, g=2),
                                                    func=AF.Copy),
                      reads=[BK(5)], writes=[vres(j) for j in blks])
            jobs.append(job_v)

            def job_k(g):
                b = proj_chunk(KD0 + 128 * g, N)
                qknorm(b, N, 1, kdst[:, g, c0:c0 + N], [kres(j) for j in blks])

            if mode == "meta":
                for g in range(2):
                    jobs.append((lambda g: lambda: job_k(g))(g))
                return jobs

            def job_glu(j):
                if mode == "halo":
                    lo = 96 if b0 == 0 else 0
                    n_ = 32
                else:
                    lo, n_ = 0, N
                bg = proj_chunk(CG0 + 128 * j, n_, lo)
                t = nxt("th", 2)
                S.add("act", lambda e: e.activation(out=th[t][:, 0:n_], in_=banks[bg][:, 0:n_], func=AF.Tanh, scale=0.5),
                      reads=[BK(bg)], writes=[("th", t)])
                bv = proj_chunk(CV0 + 128 * j, n_, lo)
                S.add("dve", lambda e: e.scalar_tensor_tensor(out=uT[:, j, c0 + lo:c0 + lo + n_], in0=th[t][:, 0:n_],
                                                              scalar=1.0, in1=banks[bv][:, 0:n_], op0=ALU.add, op1=ALU.mult),
                      reads=[("th", t), BK(bv)], writes=[("uT", b0 + jj) for jj in blks])
                if j == 3:
                    for jj in blks:
                        u_ready.add((seg, b0 + jj))

            def job_ga(j):
                b = proj_chunk(GA0 + 128 * j, N)
                silu2(b, N, gaT[qslot][:, j, 0:N], [("gaT", qslot, jj) for jj in blks])

            def job_gc(j):
                b = proj_chunk(GC0 + 128 * j, N)
                silu2(b, N, gcT[qslot][:, j, 0:N], [("gcT", qslot)])

            def job_q(j):
                b = proj_chunk(Q0 + 128 * j, N)
                qknorm(b, N, 0, qT[qslot][:, 0:nblk, j, :], [("qT", qslot, jj) for jj in blks], view3=True)

            for j in range(4):
                jobs.append((lambda j: lambda: job_glu(j))(j))
            if mode == "full":
                for j in range(4):
                    jobs.append((lambda j: lambda: job_ga(j))(j))
                for j in range(4):
                    jobs.append((lambda j: lambda: job_gc(j))(j))
            for g in range(2):
                jobs.append((lambda g: lambda: job_k(g))(g))
            if mode == "full":
                for j in range(4):
                    jobs.append((lambda j: lambda: job_q(j))(j))
            return jobs

        def att_scores(seg, qslot, jq, bq):
            qc = jq * 128
            ptiles = {}
            for g in range(2):
                for ty, kb in enumerate((bq - 1, bq, bq + 1)):
                    sb_ = 6 + nxt("st2", 2)

                    def smm(e, g=g, ty=ty, kb=kb, sb_=sb_):
                        e.matmul(banks[sb_][:, 0:256], lhsT=kTd[0:64, g, kb * 128:(kb + 1) * 128],
                                 rhs=qT[qslot][0:64, jq, 2 * g:2 * g + 2, :], start=True, stop=False,
                                 skip_group_check=True)
                        e.matmul(banks[sb_][:, 0:512], lhsT=IDN[:], rhs=BIAS[:, ty, 4 * g:4 * g + 4, :],
                                 start=False, stop=False, skip_group_check=True)
                        return e.matmul(banks[sb_][:, 256:512], lhsT=kTd[64:128, g, kb * 128:(kb + 1) * 128],
                                        rhs=qT[qslot][64:128, jq, 2 * g:2 * g + 2, :], start=False, stop=True,
                                        skip_group_check=True)
                    S.add("pe", smm, reads=[("kTd", kb), ("qT", qslot, jq), "IDN", "BIAS"], writes=[BK(sb_)])
                    p = nxt("pt", NPT)
                    ptiles[(g, ty)] = p
                    if kb == 0 or kb == NB + 1:
                        col = 2 * seg + (0 if kb == 0 else 1)
                        S.add("act", (lambda sb_, p, col: lambda e: e.activation(
                            out=PT[p][:], in_=banks[sb_][:], func=AF.Exp, bias=hb[:, col:col + 1]))(sb_, p, col),
                            reads=[BK(sb_), "hb"], writes=[("PT", p)])
                    else:
                        S.add("act", (lambda sb_, p: lambda e: e.activation(out=PT[p][:], in_=banks[sb_][:],
                                                                             func=AF.Exp))(sb_, p),
                              reads=[BK(sb_)], writes=[("PT", p)])
            sbm = 6 + nxt("st2", 2)

            def mmm(e):
                ins = None
                for g in range(2):
                    for a in range(2):
                        ins = e.matmul(banks[sbm][32 * g:32 * g + 16, a * 256:(a + 1) * 256],
                                       lhsT=kTmZ[:, g, a, 32 * g:32 * g + 16],
                                       rhs=qT[qslot][:, jq, 2 * g:2 * g + 2, :], start=True, stop=True)
                return ins
            S.add("pe", mmm, reads=[("kTmZ",), ("qT", qslot, jq)], writes=[BK(sbm)])
            pm = nxt("ptm", 2)
            S.add("act", lambda e: e.activation(out=PTm[pm][0:48, :], in_=banks[sbm][0:48, :], func=AF.Exp),
                  reads=[BK(sbm)], writes=[("PTm", pm)])
            return dict(seg=seg, qslot=qslot, jq=jq, bq=bq, qc=qc, ptiles=ptiles, pm=pm)

        def att_pv(cx):
            qslot, jq, bq, qc, ptiles, pm = cx["qslot"], cx["jq"], cx["bq"], cx["qc"], cx["ptiles"], cx["pm"]
            for g in range(2):
                pvb = g

                def pv(e, g=g, pvb=pvb):
                    ins = None
                    for hl in range(4):
                        a, j = hl % 2, hl // 2
                        c_ = a * 256 + j * 128
                        for ty, kb in enumerate((bq - 1, bq, bq + 1)):
                            e.matmul(banks[pvb][:, hl * 65:hl * 65 + 65], lhsT=PT[ptiles[(g, ty)]][:, c_:c_ + 128],
                                     rhs=vaug[:, kb, g, 0:65], start=(ty == 0), stop=False)
                        ins = e.matmul(banks[pvb][:, hl * 65:hl * 65 + 65], lhsT=PTm[pm][32 * g:32 * g + 16, c_:c_ + 128],
                                       rhs=vm[32 * g:32 * g + 16, 0, g, 0:65], start=False, stop=True)
                    return ins
                S.add("pe", pv, reads=[("PT", ptiles[(g, ty)]) for ty in range(3)] + [("PTm", pm), ("vm",), ("vm1",),
                                                                                      ("vaug1",)]
                      + [("vaug", kb) for kb in (bq - 1, bq, bq + 1)], writes=[BK(pvb)])
            o = nxt("on", 2)
            for g in range(2):
                S.add("dve", (lambda g: lambda e: e.tensor_tensor(
                    out=den[o][:, 4 * g:4 * g + 4], in0=banks[g][:, 64:260:65], in1=esink[:, 4 * g:4 * g + 4],
                    op=ALU.add))(g), reads=[BK(g), "esink"], writes=[("den", o, g)])
            S.add("dve", lambda e: e.reciprocal(out=den[o][:], in_=den[o][:]),
                  reads=[("den", o, 0), ("den", o, 1)], writes=[("den", o, 0), ("den", o, 1)])
            for g in range(2):
                S.add("dve", (lambda g: lambda e: e.tensor_tensor(
                    out=On[o][:, 256 * g:256 * g + 256].rearrange("p (h d) -> p h d", h=4),
                    in0=banks[g][:, 0:260].rearrange("p (h d) -> p h d", h=4)[:, :, 0:64],
                    in1=den[o][:, 4 * g:4 * g + 4].unsqueeze(2).to_broadcast([128, 4, 64]), op=ALU.mult))(g),
                    reads=[BK(g), ("den", o, 0), ("den", o, 1)], writes=[("On", o, g)])
            cx["o"] = o

        def att_out(cx):
            qslot, jq, qc, o = cx["qslot"], cx["jq"], cx["qc"], cx["o"]
            otb = banks[4][:].bitcast(BF16)

            def otr(e):
                ins = None
                for j in range(4):
                    ins = e.transpose(out=otb[:, j * 128:(j + 1) * 128], in_=On[o][:, j * 128:(j + 1) * 128],
                                      identity=IDN[:])
                return ins
            S.add("pe", otr, reads=[("On", o, 0), ("On", o, 1), "IDN"], writes=[BK(4)])
            S.add("dve", lambda e: e.tensor_tensor(out=mixT[:, 0:4, qc:qc + 128],
                                                   in0=otb[:, 0:512].rearrange("p (j t) -> p j t", j=4),
                                                   in1=gaT[qslot][:, 0:4, qc:qc + 128], op=ALU.mult),
                  reads=[BK(4), ("gaT", qslot, jq)], writes=[("mixA", jq)])

        def conv_mm(b0, cc):
            ublk = [("uT", b) for b in range(b0, b0 + 5)]
            c0 = b0 * 128
            cb = 6 + nxt("cv", 2)

            def cmm(e, cc=cc, cb=cb):
                ins = None
                for k in range(16, CONV_K):
                    ins = e.matmul(banks[cb][:], lhsT=DG[:, cc, k - 16, :], rhs=uT[:, cc, c0 + k - 15:c0 + k - 15 + 512],
                                   start=(k == 16), stop=(k == CONV_K - 1))
                return ins
            S.add("pe", cmm, reads=ublk + ["DG"], writes=[BK(cb)])
            S.add("dve", (lambda cc, cb: lambda e: e.tensor_tensor(out=acc[:, cc, :], in0=banks[cb][:], in1=acc[:, cc, :],
                                                                    op=ALU.add))(cc, cb),
                  reads=[BK(cb), ("acc", cc)], writes=[("acc", cc)])
            S.add("act", (lambda cc: lambda e: e.activation(out=yb[:, cc, :], in_=acc[:, cc, :], func=AF.Identity,
                                                             bias=v4[:, cc:cc + 1]))(cc),
                  reads=[("acc", cc), "v4"], writes=[("yb", cc)])
            ys = nxt("ysq", 4)
            S.add("act", (lambda cc, ys: lambda e: e.activation(out=ysq[ys][:], in_=acc[:, cc, :], func=AF.Square,
                                                                 bias=v4[:, cc:cc + 1]))(cc, ys),
                  reads=[("acc", cc), "v4"], writes=[("ysq", ys)])
            pending_stat.append((cc, ys))

        mac_q = []
        in_drain = [False]

        def conv_queue(seg, b0):
            c0 = b0 * 128
            for k in range(0, 16):
                for cc in range(4):
                    lo = c0 + k - 15
                    need = {(seg, b) for b in range(lo // 128, (lo + 511) // 128 + 1)}
                    j = cc * CONV_K + k
                    rd = [("uT", b) for (_, b) in need] + [("acc", cc), "cwp"]
                    if k == 0:
                        fn = (lambda cc, lo, j: lambda e: e.tensor_scalar(out=acc[:, cc, :], in0=uT[:, cc, lo:lo + 512],
                                                                           scalar1=cwp[:, j:j + 1], scalar2=None,
                                                                           op0=ALU.mult))(cc, lo, j)
                    else:
                        fn = (lambda cc, lo, j: lambda e: e.scalar_tensor_tensor(out=acc[:, cc, :], in0=uT[:, cc, lo:lo + 512],
                                                                                  scalar=cwp[:, j:j + 1], in1=acc[:, cc, :],
                                                                                  op0=ALU.mult, op1=ALU.add))(cc, lo, j)
                    mac_q.append((need, fn, rd, cc))

        def drain(n):
            if in_drain[0]:
                return
            in_drain[0] = True
            while n > 0 and mac_q and mac_q[0][0] <= u_ready:
                need, fn, rd, cc = mac_q.pop(0)
                S.add("dve", fn, reads=rd, writes=[("acc", cc)])
                n -= 1
            in_drain[0] = False

        _add = S.add

        def add_hook(eng, fn, reads=(), writes=(), dma=False):
            op = _add(eng, fn, reads=reads, writes=writes, dma=dma)
            if eng == "dve" and not dma and not in_drain[0]:
                drain(1)
            return op
        S.add = add_hook

        pending_stat = []

        def conv_statmm():
            while pending_stat:
                cc, ys = pending_stat.pop(0)
                S.add("pe", (lambda cc: lambda e: e.matmul(banks[2][:], lhsT=ONL[:], rhs=yb[:, cc, :], start=(cc == 0),
                                                            stop=(cc == 3)))(cc),
                      reads=[("yb", cc), "ONL"], writes=[BK(2)])
                S.add("pe", (lambda cc, ys: lambda e: e.matmul(banks[3][:], lhsT=ONL[:], rhs=ysq[ys][:], start=(cc == 0),
                                                                stop=(cc == 3)))(cc, ys),
                      reads=[("ysq", ys), "ONL"], writes=[BK(3)])

        def conv_stats():
            S.add("act", lambda e: e.activation(out=m2[:], in_=banks[2][:], func=AF.Square), reads=[BK(2)], writes=["m2"])
            S.add("dve", lambda e: e.scalar_tensor_tensor(out=lnr[:], in0=banks[3][:], scalar=1e-5, in1=m2[:],
                                                          op0=ALU.add, op1=ALU.subtract),
                  reads=[BK(3), "m2"], writes=["lnr"])
            S.add("act", lambda e: e.activation(out=lnr[:], in_=lnr[:], func=AF.Ln), reads=["lnr"], writes=["lnr"])
            S.add("act", lambda e: e.activation(out=lnr[:], in_=lnr[:], func=AF.Exp, scale=-0.5), reads=["lnr"], writes=["lnr"])
            S.add("dve", lambda e: e.scalar_tensor_tensor(out=nmr[:], in0=banks[2][:], scalar=-1.0, in1=lnr[:],
                                                          op0=ALU.mult, op1=ALU.mult),
                  reads=[BK(2), "lnr", "m2"], writes=["m2"])

        def conv_epi(qslot, cc):
            if True:
                S.add("dve", (lambda cc: lambda e: e.tensor_tensor(out=t2[:], in0=yb[:, cc, :], in1=lnr[:],
                                                                    op=ALU.mult))(cc),
                      reads=[("yb", cc), "lnr"], writes=["t2"])
                S.add("dve", lambda e: e.tensor_tensor(out=t2[:], in0=t2[:], in1=nmr[:], op=ALU.add),
                      reads=["t2", "m2"], writes=["t2"])
                t = nxt("th", 2)
                S.add("act", (lambda cc: lambda e: e.activation(out=ta[:], in_=t2[:], func=AF.Identity,
                                                                 scale=v4[:, 4 + cc:5 + cc], bias=v4[:, 8 + cc:9 + cc]))(cc),
                      reads=["t2", "v4"], writes=["ta"])
                S.add("act", (lambda cc, t: lambda e: e.activation(out=th[t][:], in_=t2[:], func=AF.Tanh,
                                                                    scale=v4h[:, cc:cc + 1], bias=v4h[:, 4 + cc:5 + cc]))(cc, t),
                      reads=["t2", "v4h"], writes=[("th", t)])
                S.add("dve", (lambda t: lambda e: e.scalar_tensor_tensor(out=ta[:], in0=th[t][:], scalar=1.0, in1=ta[:],
                                                                          op0=ALU.add, op1=ALU.mult))(t),
                      reads=[("th", t), "ta"], writes=["ta"])
                S.add("dve", (lambda cc: lambda e: e.tensor_tensor(out=yb[:, cc, :], in0=ta[:],
                                                                    in1=gcT[qslot][:, cc, :], op=ALU.mult))(cc),
                      reads=["ta", ("gcT", qslot)], writes=[("yb", cc)])

        def outproj_load(seg, bq):
            r = nxt("xt", 3)
            dma(xres[r][:], xe[seg, bq * 128:(bq + 1) * 128, :], writes=[("xt", r)])
            return r

        def outproj_block(seg, jq, bq, r):
            qc = jq * 128
            for half in range(2):
                ob = 6 + half

                def omm(e, half=half, ob=ob):
                    ins = None
                    for k in range(8):
                        lt = mixT[:, k, qc:qc + 128] if k < 4 else yb[:, k - 4, qc:qc + 128]
                        ins = e.matmul(banks[ob][:], lhsT=lt,
                                       rhs=WO[:, k, half * 512:(half + 1) * 512], start=(k == 0), stop=(k == 7))
                    return ins
                S.add("pe", omm, reads=[("mixA", jq)] + [("yb", cc) for cc in range(4)] + WOALL, writes=[BK(ob)])
                S.add("dve", (lambda half, ob: lambda e: e.tensor_tensor(
                    out=xres[r][:, half * 512:(half + 1) * 512], in0=banks[ob][:],
                    in1=xres[r][:, half * 512:(half + 1) * 512], op=ALU.add))(half, ob),
                    reads=[BK(ob), ("xt", r)], writes=[("xt", r)])
            S.add("pool", lambda e: e.dma_start(out=y[seg, (bq - 1) * 128:bq * 128, :], in_=xres[r][:]),
                  reads=[("xt", r)], dma=True)

        if STOP >= 1:
            norm_T(norm_pre(lambda j: xmeta, 1))
            for jb in projA_jobs(0, 1, "meta"):
                jb()
        S.add("dve", lambda e: e.memset(kTmZ[:], 0.0), writes=[("kTmZ",)])
        for g in range(2):
            for a in range(2):
                S.add("dve", (lambda g, a: lambda e: e.tensor_copy(out=kTmZ[64 * a:64 * a + 64, g, a, :],
                                                                   in_=kTm[64 * a:64 * a + 64, g, 0:48]))(g, a),
                      reads=[("kTm",)], writes=[("kTmZ",)])

        A_list = []
        for seg in range(NSEG if STOP >= 2 else 0):
            A_list.append((seg, 0, "halo", 0, 1))
            for i in range(NST):
                A_list.append((seg, i + 1, "full", 1 + 4 * i, 4))
            A_list.append((seg, NST + 1, "halo", NB + 1, 1))
        B_list = [(seg, i) for seg in range(NSEG if STOP >= 2 else 0) for i in range(NST)]
        pa = [0]
        slot_of = {}

        def src_of(entry):
            seg, k, mode, b0, nblk = entry
            return (lambda seg, b0: lambda j: xe[seg, (b0 + j) * 128:(b0 + j + 1) * 128, :])(seg, b0)

        pre_slots = {}

        def do_pre(idx):
            if idx < len(A_list) and idx not in pre_slots:
                pre_slots[idx] = norm_pre(src_of(A_list[idx]), A_list[idx][4])

        def A_jobs(idx):
            seg, k, mode, b0, nblk = A_list[idx]
            qs = None
            if mode == "full":
                qs = nxt("qslot", 2)
                slot_of[(seg, k)] = qs
            pj = projA_jobs(b0, nblk, mode, qs, seg)

            def first():
                do_pre(idx)
                norm_T(pre_slots[idx])
            return [first, pj[0], lambda: do_pre(idx + 1)] + pj[1:]

        def need_idx(seg, i):
            return seg * (NST + 2) + i + 2

        def first_slot(entry, seg, b0):
            eseg, k, mode, eb0, nblk = entry
            if eseg == seg:
                return 0
            last = -1
            for jq in range(4):
                rd = range(b0 + jq - 1, b0 + jq + 2)
                if any(eb0 <= r < eb0 + nblk for r in rd):
                    last = jq
            return last + 1

        for bi, (seg, i) in enumerate(B_list):
            while pa[0] <= need_idx(seg, i):
                for jb in A_jobs(pa[0]):
                    jb()
                pa[0] += 1
            qslot = slot_of[(seg, i + 1)]
            b0 = 1 + 4 * i
            slots = [[] for _ in range(5)]
            if bi + 1 < len(B_list):
                nseg, ni = B_list[bi + 1]
                tgt = need_idx(nseg, ni)
                pend = []
                while pa[0] <= tgt:
                    pend.append(pa[0])
                    pa[0] += 1
                queue = []
                post = False
                for idx in pend:
                    fs = first_slot(A_list[idx], seg, b0)
                    post = post or fs >= 4
                    for jb in A_jobs(idx):
                        queue.append((4 if post else fs, jb))
                n_in = sum(1 for fs, _ in queue if fs < 4)
                quota = max(1, -(-n_in // 4))
                cur, cnt_ = 0, 0
                for fs, jb in queue:
                    if fs >= 4:
                        slots[4].append(jb)
                        continue
                    if fs > cur:
                        cur, cnt_ = fs, 0
                    if cnt_ >= quota and cur < 3:
                        cur, cnt_ = cur + 1, 0
                    slots[cur].append(jb)
                    cnt_ += 1
            if bi == 0:
                conv_queue(seg, b0)
                drain(10 ** 9)
                assert not mac_q
            for cc in range(4):
                conv_mm(b0, cc)
                if cc >= 1:
                    pass
                if cc >= 1:
                    last = pending_stat.pop()
                    conv_statmm()
                    pending_stat.append(last)
            if bi + 1 < len(B_list):
                nseg, ni = B_list[bi + 1]
                conv_queue(nseg, 1 + 4 * ni)
            cxs = []
            for jq in range(4):
                cx = att_scores(seg, qslot, jq, b0 + jq)
                cxs.append(cx)
                if jq == 0:
                    conv_statmm()
                    conv_stats()
                if jq >= 1:
                    att_out(cxs[jq - 1])
                for jb in slots[jq]:
                    jb()
                conv_epi(qslot, jq)
                att_pv(cx)
            rs_ = {0: outproj_load(seg, b0), 1: outproj_load(seg, b0 + 1)}
            outproj_block(seg, 0, b0, rs_[0])
            rs_[2] = outproj_load(seg, b0 + 2)
            outproj_block(seg, 1, b0 + 1, rs_[1])
            att_out(cxs[3])
            for jb in slots[4]:
                jb()
            rs_[3] = outproj_load(seg, b0 + 3)
            outproj_block(seg, 2, b0 + 2, rs_[2])
            outproj_block(seg, 3, b0 + 3, rs_[3])
            drain(10 ** 9)
            assert not mac_q, "conv MACs left whose u blocks were never produced"

        S.finalize(sems, dsems)
        with nc.Block() as block:
            @block.tensor
            def _(e):
                S.emit("pe", e)

            @block.scalar
            def _(e):
                S.emit("act", e)

            @block.vector
            def _(e):
                S.emit("dve", e)

            @block.gpsimd
            def _(e):
                S.emit("pool", e)

            @block.sync
            def _(e):
                S.emit("sp", e)
                S.final_waits(e)
    return nc


def _const_tables():
    i = np.arange(128)
    jj, ii = np.meshgrid(i, i, indexing="ij")
    slopes = np.exp2(-8.0 * np.arange(1, N_HEADS + 1) / N_HEADS).astype(np.float32)
    tab = np.empty((128, 3, N_HEADS, 128), np.float32)
    dl, vl = 128 + ii - jj, jj >= ii
    dm = np.abs(ii - jj)
    dr, vr = 128 + jj - ii, jj <= ii
    order = [0, 2, 1, 3, 4, 6, 5, 7]
    for sl, h in enumerate(order):
        tab[:, 0, sl, :] = np.where(vl, -slopes[h] * dl, NEG)
        tab[:, 1, sl, :] = -slopes[h] * dm
        tab[:, 2, sl, :] = np.where(vr, -slopes[h] * dr, NEG)
    bf = ml_dtypes.bfloat16
    ident = np.eye(128, dtype=np.float32).astype(bf)
    bd = np.kron(np.eye(2, dtype=np.float32), np.full((64, 64), 1.0 / 64, np.float32)).astype(bf)
    onesln = np.full((128, 128), 1.0 / 512, np.float32).astype(bf)
    return tab.reshape(128, -1).astype(bf), ident, bd, onesln


def _segment(xseq, meta, t0, n):
    S_ = xseq.shape[0]
    if t0 > 0:
        left, hl = xseq[t0 - 128:t0], 0.0
    else:
        left = np.concatenate([np.zeros((128 - N_META, D_MODEL), np.float32), meta], 0)
        hl = NEG
    if t0 + n < S_:
        right, hr = xseq[t0 + n:t0 + n + 128], 0.0
    else:
        right, hr = np.zeros((128, D_MODEL), np.float32), NEG
    return np.concatenate([left, xseq[t0:t0 + n], right], 0), hl, hr


def _core_inputs(segs, meta, weights, consts):
    xs, hbv = [], []
    for (xseq, t0, n) in segs:
        x_, hl, hr = _segment(xseq, meta, t0, n)
        xs.append(x_)
        hbv += [hl, hr]
    xmeta = np.zeros((128, D_MODEL), np.float32)
    xmeta[0:N_META] = meta
    xmeta[32:32 + N_META] = meta
    d = dict(weights)
    d.update(consts)
    d["xe"] = np.ascontiguousarray(np.stack(xs, 0))
    d["xmeta"] = xmeta
    d["hb"] = np.ascontiguousarray(np.broadcast_to(np.asarray(hbv, np.float32)[None, :], (128, len(hbv))))
    return d


def _pack_weights(norm_w, w_in, q_norm_w, k_norm_w, sink_logits, conv_w, conv_b, conv_ln_w, conv_ln_b, w_out):
    f = np.float32
    return {
        "w_in": np.ascontiguousarray(w_in[0], f),
        "w_out": np.ascontiguousarray(w_out[0], f),
        "normw_pk": np.ascontiguousarray(norm_w[0].reshape(8, 128).T, f),
        "qkw": np.ascontiguousarray(np.stack([np.tile(q_norm_w[0], 2), np.tile(k_norm_w[0], 2)], 1), f),
        "convw_pk": np.ascontiguousarray(conv_w[0].reshape(CONV_K, 4, 128).transpose(2, 1, 0).reshape(128, 4 * CONV_K), f),
        "vec4": np.ascontiguousarray(np.concatenate([conv_b[0].reshape(4, 128).T, conv_ln_w[0].reshape(4, 128).T,
                                                     conv_ln_b[0].reshape(4, 128).T], 1), f),
        "sink_b": np.ascontiguousarray(np.broadcast_to(sink_logits[0][None, :], (128, N_HEADS)), f),
    }


_NC_CACHE = {}


def kernel(x_prompt, x_sample, meta_tokens, norm_w, w_in, q_norm_w, k_norm_w, sink_logits,
           conv_w, conv_b, conv_ln_w, conv_ln_b, w_out):
    NB, NSEG, NCORE = 8, 8, 8
    n = NB * 128
    xp = np.asarray(x_prompt, np.float32)
    xs = np.asarray(x_sample, np.float32)
    meta = np.asarray(meta_tokens, np.float32)
    weights = _pack_weights(*[np.asarray(a, np.float32) for a in
                              (norm_w, w_in, q_norm_w, k_norm_w, sink_logits, conv_w, conv_b, conv_ln_w, conv_ln_b, w_out)])
    tab, ident, bd, onesln = _const_tables()
    consts = {"bias_c": tab, "ident_c": ident, "bd_c": bd, "onesln_c": onesln}
    in_maps, place = [], []
    for c in range(NCORE):
        segs, pl = [], []
        for sq_ in (2 * c, 2 * c + 1):
            for t0 in (0, n):
                segs.append((xp[sq_], t0, n))
                pl.append((0, sq_, t0))
        sseq, base = c // 4, (c % 4) * 4 * n
        for q in range(4):
            segs.append((xs[sseq], base + q * n, n))
            pl.append((1, sseq, base + q * n))
        in_maps.append(_core_inputs(segs, meta, weights, consts))
        place.append(pl)
    key = (NSEG, NB)
    if key not in _NC_CACHE:
        _NC_CACHE[key] = _build(NSEG, NB)
    res = run_bass_kernel_spmd(_NC_CACHE[key], in_maps, core_ids=list(range(NCORE)))
    yp = np.empty(xp.shape, np.float32)
    ys = np.empty(xs.shape, np.float32)
    for c in range(NCORE):
        yc = res.results[c]["y"]
        for s_, (which, sq_, t0) in enumerate(place[c]):
            (yp if which == 0 else ys)[sq_, t0:t0 + n] = yc[s_]
    return (yp, ys)
```

```python
import math
from contextlib import ExitStack

import numpy as np
import ml_dtypes
import concourse.bass as bass
import concourse.mybir as mybir
from concourse.bass_utils import run_bass_kernel_spmd

F32 = mybir.dt.float32
BF16 = mybir.dt.bfloat16
AF = mybir.ActivationFunctionType
ALU = mybir.AluOpType

D_MODEL = 1024
N_META = 16
N_HEADS = 8
CONV_K = 31
NEG = -30000.0

Q0, KD0, KD1, V0, GA0, CV0, CG0, GC0, WCOLS = 0, 512, 640, 768, 896, 1408, 1920, 2432, 2944


class _Op:
    __slots__ = ("eng", "fn", "deps", "signal", "sem", "val", "dma", "waits", "prev_val")

    def __init__(self, eng, fn, dma):
        self.eng = eng
        self.fn = fn
        self.dma = dma
        self.deps = set()
        self.signal = False
        self.sem = None
        self.val = 0
        self.prev_val = 0
        self.waits = []


class Sched:
    ENGS = ("pe", "act", "dve", "pool", "sp")

    def __init__(self):
        self.ops = {e: [] for e in self.ENGS}
        self.last_writer = {}
        self.readers = {}

    def add(self, eng, fn, reads=(), writes=(), dma=False):
        op = _Op(eng, fn, dma)
        deps = op.deps
        for r in reads:
            w = self.last_writer.get(r)
            if w is not None:
                deps.add(w)
        for w_ in writes:
            w = self.last_writer.get(w_)
            if w is not None:
                deps.add(w)
            for rd in self.readers.get(w_, ()):
                deps.add(rd)
        for r in reads:
            self.readers.setdefault(r, []).append(op)
        for w_ in writes:
            self.last_writer[w_] = op
            self.readers[w_] = []
        deps.discard(op)
        if eng == "pe" and not dma:
            op.deps = {d for d in deps if not (d.eng == "pe" and not d.dma)}
        for d in op.deps:
            d.signal = True
        self.ops[eng].append(op)
        return op

    def finalize(self, eng_sems, dma_sems):
        for e in self.ENGS:
            cnt = 0
            dcount = {}
            k = 0
            pool = dma_sems.get(e, [])
            for op in self.ops[e]:
                if op.dma:
                    s = pool[k % len(pool)]
                    k += 1
                    prev = dcount.get(id(s), 0)
                    op.sem = s
                    op.prev_val = prev
                    op.val = prev + 16
                    dcount[id(s)] = op.val
                elif op.signal:
                    cnt += 1
                    op.sem = eng_sems[e]
                    op.val = cnt
        for e in self.ENGS:
            waited = {}
            for op in self.ops[e]:
                need = {}
                for d in op.deps:
                    key = id(d.sem)
                    if d.val > need.get(key, (None, 0))[1]:
                        need[key] = (d.sem, d.val)
                if op.dma and op.prev_val > 0:
                    key = id(op.sem)
                    if op.prev_val > need.get(key, (None, 0))[1]:
                        need[key] = (op.sem, op.prev_val)
                for key, (s, v) in need.items():
                    if v > waited.get(key, 0):
                        waited[key] = v
                        op.waits.append((s, v))

    def emit(self, eng_name, e):
        for op in self.ops[eng_name]:
            for (s, v) in op.waits:
                e.wait_ge(s, v)
            ins = op.fn(e)
            if op.dma:
                ins.then_inc(op.sem, 16)
            elif op.signal:
                ins.then_inc(op.sem, 1)

    def final_waits(self, e):
        last = {}
        for q in self.ENGS:
            for op in self.ops[q]:
                if op.dma:
                    last[id(op.sem)] = (op.sem, op.val)
        for s, v in last.values():
            e.wait_ge(s, v)


def _build(NSEG, NB, STOP=9):
    assert NB % 4 == 0
    NST = NB // 4
    NBX = NB + 2
    TX = NBX * 128
    nc = bass.Bass("TRN2", target_bir_lowering=False)

    def din(name, shape, dt=F32):
        return nc.dram_tensor(name, shape, dt, kind="ExternalInput").ap()

    xe = din("xe", [NSEG, TX, D_MODEL])
    xmeta = din("xmeta", [128, D_MODEL])
    w_in = din("w_in", [D_MODEL, 2816])
    w_out = din("w_out", [D_MODEL, D_MODEL])
    normw_pk = din("normw_pk", [128, 8])
    qkw = din("qkw", [128, 2])
    convw_pk = din("convw_pk", [128, 4 * CONV_K])
    vec4 = din("vec4", [128, 12])
    sink_b = din("sink_b", [128, 8])
    hb_in = din("hb", [128, 2 * NSEG])
    ident_c = din("ident_c", [128, 128], BF16)
    bd_c = din("bd_c", [128, 128], BF16)
    onesln_c = din("onesln_c", [128, 128], BF16)
    bias_c = din("bias_c", [128, 3 * 8 * 128], BF16)
    y = nc.dram_tensor("y", [NSEG, NB * 128, D_MODEL], F32, kind="ExternalOutput").ap()

    S = Sched()
    with ExitStack() as st:
        E = st.enter_context

        def sb(name, shape, dt=F32):
            return E(nc.sbuf_tensor("sb_" + name, shape, dt))

        W = sb("W", [128, 8, WCOLS], BF16)
        WO = sb("WO", [128, 8, D_MODEL], BF16)
        DG = sb("DG", [128, 4, 15, 128], BF16)
        BIAS = sb("BIAS", [128, 3, 8, 128], BF16)
        IDN = sb("IDN", [128, 128], BF16)
        BD = sb("BD", [128, 128], BF16)
        ONL = sb("ONL", [128, 128], BF16)
        nwp = sb("nwp", [128, 8])
        qkv = sb("qkv", [128, 2])
        qks = sb("qks", [128, 2])
        cwp = sb("cwp", [128, 4 * CONV_K])
        v4 = sb("v4", [128, 12])
        v4h = sb("v4h", [128, 8])
        snk = sb("snk", [128, 8])
        esink = sb("esink", [128, 8])
        hb = sb("hb", [128, 2 * NSEG])
        mhalf = sb("mhalf", [128, 1])
        epsc = sb("epsc", [128, 1])
        kTd = sb("kTd", [128, 2, TX], BF16)
        vaug = sb("vaug", [128, NBX, 2, 66], BF16)
        uT = sb("uT", [128, 4, TX], BF16)
        kTm = sb("kTm", [128, 2, 128], BF16)
        vm = sb("vm", [128, 1, 2, 66], BF16)
        kTmZ = sb("kTmZ", [128, 2, 2, 48], BF16)
        hT = sb("hT", [128, 8, 512], BF16)
        qT = [sb(f"qT{i}", [128, 4, 4, 128], BF16) for i in range(2)]
        gaT = [sb(f"gaT{i}", [128, 4, 512], BF16) for i in range(2)]
        gcT = [sb(f"gcT{i}", [128, 4, 512], BF16) for i in range(2)]
        mixT = sb("mixT", [128, 4, 512], BF16)
        xt = [sb(f"xt{i}", [128, D_MODEL]) for i in range(3)]
        xres = xt
        htm = [sb(f"htm{i}", [128, D_MODEL], BF16) for i in range(4)]
        ss = [sb(f"ss{i}", [128, 1]) for i in range(4)]
        sq = [sb(f"sq{i}", [128, 512], BF16) for i in range(2)]
        tq = [sb(f"tq{i}", [128, 512]) for i in range(2)]
        th = [sb(f"th{i}", [128, 512]) for i in range(2)]
        NPT = 8
        PT = [sb(f"PT{i}", [128, 512], BF16) for i in range(NPT)]
        PTm = [sb(f"PTm{i}", [128, 512], BF16) for i in range(2)]
        On = [sb(f"On{i}", [128, 512], BF16) for i in range(2)]
        den = [sb(f"den{i}", [128, 8]) for i in range(2)]
        yb = sb("yb", [128, 4, 512], BF16)
        acc = sb("acc", [128, 4, 512])
        ysq = [sb(f"ysq{i}", [128, 512], BF16) for i in range(4)]
        lnr = sb("lnr", [128, 512])
        m2 = sb("m2", [128, 512])
        nmr = m2
        t2 = sb("t2", [128, 512])
        ta = sb("ta", [128, 512])

        banks = [E(nc.psum_tensor(f"bank{i}", [128, 512], F32)) for i in range(8)]
        sems = {e: E(nc.semaphore("s_" + e)) for e in Sched.ENGS}
        dsems = {"sp": [E(nc.semaphore(f"d{i}")) for i in range(24)],
                 "pool": [E(nc.semaphore(f"dp{i}")) for i in range(8)]}

        def BK(i):
            return ("bank", i)

        cnt = {"htm": 0, "xt": 0, "xres": 0, "sq": 0, "th": 0, "pt": 0, "ptm": 0, "on": 0, "ysq": 0, "proj": 0,
               "st2": 0, "cv": 0, "qslot": 0}

        def nxt(k, n):
            v = cnt[k] % n
            cnt[k] += 1
            return v

        def dma(out_ap, in_ap, reads=(), writes=()):
            S.add("sp", lambda e: e.dma_start(out=out_ap, in_=in_ap), reads=reads, writes=writes, dma=True)

        dma(nwp[:], normw_pk, writes=["nwp"])
        dma(qkv[:], qkw, writes=["qkv"])
        dma(cwp[:], convw_pk, writes=["cwp"])
        dma(v4[:], vec4, writes=["v4"])
        dma(snk[:], sink_b, writes=["snk"])
        dma(hb[:], hb_in, writes=["hb"])
        dma(IDN[:], ident_c, writes=["IDN"])
        dma(BD[:], bd_c, writes=["BD"])
        dma(ONL[:], onesln_c, writes=["ONL"])
        dma(BIAS[:].rearrange("p a h q -> p (a h q)"), bias_c, writes=["BIAS"])

        S.add("pool", lambda e: e.memset(mhalf[:], -0.5), writes=["mhalf"])
        S.add("pool", lambda e: e.memset(epsc[:], 1e-6), writes=["epsc"])
        S.add("pool", lambda e: e.memset(vaug[:, :, :, 64:66], 1.0), writes=[("vaug1",)])
        S.add("pool", lambda e: e.memset(vm[:, :, :, 64:66], 1.0), writes=[("vm1",)])
        for i in range(8):
            S.add("dve", (lambda i: lambda e: e.memset(banks[i][:], 0.0))(i), writes=[BK(i)])
        S.add("dve", lambda e: e.tensor_scalar(out=qks[:, 0:1], in0=qkv[:, 0:1], scalar1=0.125, scalar2=None,
                                               op0=ALU.mult), reads=["qkv"], writes=["qks0"])
        S.add("dve", lambda e: e.tensor_copy(out=qks[:, 1:2], in_=qkv[:, 1:2]), reads=["qkv"], writes=["qks1"])
        S.add("dve", lambda e: e.tensor_scalar(out=cwp[:], in0=cwp[:], scalar1=0.5, scalar2=None, op0=ALU.mult),
              reads=["cwp"], writes=["cwp"])
        S.add("dve", lambda e: e.tensor_scalar(out=v4h[:], in0=v4[:, 4:12], scalar1=0.5, scalar2=None,
                                               op0=ALU.mult), reads=["v4"], writes=["v4h"])
        S.add("act", lambda e: e.activation(out=esink[:], in_=snk[:], func=AF.Exp), reads=["snk"], writes=["esink"])

        def stage_view(slot):
            return xt[slot][:].rearrange("p (k n) -> p k n", k=8)

        nwb = nwp[:].unsqueeze(2).to_broadcast([128, 8, 128])

        def wpiece(c0):
            slot = nxt("xt", 3)
            sv = stage_view(slot)
            dma(sv, w_in[:, c0:c0 + 128].rearrange("(k p) n -> p k n", p=128), writes=[("xt", slot)])
            if c0 < 512:
                dsts = [(Q0 + c0, 0, 128)]
            elif c0 == 512:
                dsts = [(KD0, 0, 64), (KD0 + 64, 0, 64), (KD1, 64, 64), (KD1 + 64, 64, 64)]
            elif c0 == 640:
                dsts = [(V0, 0, 128)]
            elif c0 < 1280:
                dsts = [(GA0 + c0 - 768, 0, 128)]
            elif c0 < 1792:
                dsts = [(CV0 + c0 - 1280, 0, 128)]
            elif c0 < 2304:
                dsts = [(CG0 + c0 - 1792, 0, 128)]
            else:
                dsts = [(GC0 + c0 - 2304, 0, 128)]
            for (d0, s0, n) in dsts:
                S.add("dve", (lambda d0, s0, n, sv: lambda e: e.tensor_tensor(
                    out=W[:, :, d0:d0 + n], in0=sv[:, :, s0:s0 + n],
                    in1=nwp[:].unsqueeze(2).to_broadcast([128, 8, n]), op=ALU.mult))(d0, s0, n, sv),
                    reads=[("xt", slot), "nwp"], writes=[("W", d0)])

        for c0 in range(0, 2816, 128):
            wpiece(c0)

        def wopiece(c0):
            slot = nxt("xt", 3)
            sv = stage_view(slot)
            dma(sv, w_out[:, c0:c0 + 128].rearrange("(k p) n -> p k n", p=128), writes=[("xt", slot)])
            S.add("dve", lambda e: e.tensor_scalar(out=WO[:, 0:4, c0:c0 + 128], in0=sv[:, 0:4, :], scalar1=0.5,
                                                   scalar2=None, op0=ALU.mult),
                  reads=[("xt", slot)], writes=[("WO", c0, 0)])
            S.add("dve", lambda e: e.tensor_scalar(out=WO[:, 4:8, c0:c0 + 128], in0=sv[:, 4:8, :], scalar1=0.25,
                                                   scalar2=None, op0=ALU.mult),
                  reads=[("xt", slot)], writes=[("WO", c0, 1)])

        for c0 in range(0, D_MODEL, 128):
            wopiece(c0)

        def dgs(e):
            ins = None
            for cc in range(4):
                for k in range(16, CONV_K):
                    j = cc * CONV_K + k
                    ins = e.tensor_scalar(out=DG[:, cc, k - 16, :], in0=IDN[:], scalar1=cwp[:, j:j + 1], scalar2=None,
                                          op0=ALU.mult)
            return ins
        S.add("dve", dgs, reads=["IDN", "cwp"], writes=["DG"])
        WALL = [("W", d) for d in ([Q0 + i for i in range(0, 512, 128)] + [KD0, KD0 + 64, KD1, KD1 + 64, V0]
                                   + [b + i for b in (GA0, CV0, CG0, GC0) for i in range(0, 512, 128)])]
        WOALL = [("WO", c0, h) for c0 in range(0, D_MODEL, 128) for h in range(2)]

        def norm_pre(src, nblk):
            hs = []
            for j in range(nblk):
                slot = nxt("xt", 3)
                h_ = nxt("htm", 4)
                hs.append(h_)
                dma(xt[slot][:], src(j), writes=[("xt", slot)])
                S.add("act", (lambda slot, h_: lambda e: e.activation(out=htm[h_][:], in_=xt[slot][:], func=AF.Square,
                                                                       accum_out=ss[h_][:]))(slot, h_),
                      reads=[("xt", slot)], writes=[("htm", h_), ("ss", h_)])
                S.add("dve", (lambda h_: lambda e: e.tensor_scalar(out=ss[h_][:], in0=ss[h_][:], scalar1=1.0 / D_MODEL,
                                                                    scalar2=1e-6, op0=ALU.mult, op1=ALU.add))(h_),
                      reads=[("ss", h_)], writes=[("ss", h_)])
                S.add("pool", (lambda h_: lambda e: e.tensor_tensor(out=ss[h_][:], in0=ss[h_][:], in1=mhalf[:, 0:1],
                                                                     op=ALU.pow))(h_),
                      reads=[("ss", h_), "mhalf"], writes=[("ss", h_)])
                S.add("dve", (lambda slot, h_: lambda e: e.tensor_scalar(out=htm[h_][:], in0=xt[slot][:], scalar1=ss[h_][:],
                                                                          scalar2=None, op0=ALU.mult))(slot, h_),
                      reads=[("xt", slot), ("ss", h_)], writes=[("htm", h_)])
            return hs

        def norm_T(hs):
            for j, h_ in enumerate(hs):
                tb = 4 + (j % 2)
                trb = banks[tb][:].bitcast(BF16)

                def tr(e, h_=h_, trb=trb):
                    ins = None
                    for k in range(8):
                        ins = e.transpose(out=trb[:, k * 128:(k + 1) * 128], in_=htm[h_][:, k * 128:(k + 1) * 128],
                                          identity=IDN[:])
                    return ins
                S.add("pe", tr, reads=[("htm", h_), "IDN"], writes=[BK(tb)])
                S.add("act", (lambda j, trb: lambda e: e.activation(out=hT[:, :, j * 128:(j + 1) * 128],
                                                                     in_=trb.rearrange("p (k t) -> p k t", k=8),
                                                                     func=AF.Copy))(j, trb),
                      reads=[BK(tb)], writes=[("hT", j)])

        def proj_chunk(wcol, N, lo=0):
            b = nxt("proj", 3)

            def mm(e):
                ins = None
                for k in range(8):
                    ins = e.matmul(banks[b][:, 0:N], lhsT=W[:, k, wcol:wcol + 128], rhs=hT[:, k, lo:lo + N],
                                   start=(k == 0), stop=(k == 7))
                return ins
            S.add("pe", mm, reads=[("hT", j) for j in range((lo + N + 127) // 128)] + WALL, writes=[BK(b)])
            return b

        def silu2(b, N, out_ap, out_res):
            t = nxt("th", 2)
            S.add("act", lambda e: e.activation(out=th[t][:, 0:N], in_=banks[b][:, 0:N], func=AF.Tanh, scale=0.5),
                  reads=[BK(b)], writes=[("th", t)])
            S.add("dve", lambda e: e.scalar_tensor_tensor(out=out_ap, in0=th[t][:, 0:N], scalar=1.0,
                                                          in1=banks[b][:, 0:N], op0=ALU.add, op1=ALU.mult),
                  reads=[("th", t), BK(b)], writes=out_res)

        def qknorm(b, N, scol, out_ap, out_res, view3=False):
            s_ = nxt("sq", 2)
            stb = 3 if s_ == 0 else 5
            S.add("act", lambda e: e.activation(out=sq[s_][:, 0:N], in_=banks[b][:, 0:N], func=AF.Square),
                  reads=[BK(b)], writes=[("sq", s_)])
            S.add("pe", lambda e: e.matmul(banks[stb][:, 0:N], lhsT=BD[:], rhs=sq[s_][:, 0:N], start=True, stop=True),
                  reads=[("sq", s_), "BD"], writes=[BK(stb)])
            S.add("act", lambda e: e.activation(out=tq[s_][:, 0:N], in_=banks[stb][:, 0:N], func=AF.Ln, bias=epsc[:, 0:1]),
                  reads=[BK(stb), "epsc"], writes=[("tq", s_)])
            S.add("act", lambda e: e.activation(out=tq[s_][:, 0:N], in_=tq[s_][:, 0:N], func=AF.Exp, scale=-0.5),
                  reads=[("tq", s_)], writes=[("tq", s_)])
            v3 = (lambda ap: ap.rearrange("p (j t) -> p j t", t=128)) if view3 else (lambda ap: ap)
            S.add("dve", lambda e: e.scalar_tensor_tensor(out=out_ap, in0=v3(banks[b][:, 0:N]), scalar=qks[:, scol:scol + 1],
                                                          in1=v3(tq[s_][:, 0:N]), op0=ALU.mult, op1=ALU.mult),
                  reads=[BK(b), ("tq", s_), "qks0", "qks1"], writes=out_res)

        u_ready = set()

        def projA_jobs(b0, nblk, mode, qslot=None, seg=None):
            N = nblk * 128
            if mode == "meta":
                kdst, vdst, kres, vres = kTm, vm, lambda j: ("kTm",), lambda j: ("vm",)
                c0 = 0
            else:
                kdst, vdst, kres, vres = kTd, vaug, lambda j: ("kTd", b0 + j), lambda j: ("vaug", b0 + j)
                c0 = b0 * 128
            blks = range(nblk)
            jobs = []

            def job_v():
                def vmm(e):
                    ins = None
                    for j in range(nblk):
                        for k in range(8):
                            ins = e.matmul(banks[5][:, j * 128:(j + 1) * 128], lhsT=hT[:, k, j * 128:(j + 1) * 128],
                                           rhs=W[:, k, V0:V0 + 128], start=(k == 0), stop=(k == 7))
                    return ins
                S.add("pe", vmm, reads=[("hT", j) for j in blks] + WALL, writes=[BK(5)])
                vb0 = 0 if mode == "meta" else b0
                S.add("act", lambda e: e.activation(out=vdst[:, vb0:vb0 + nblk, :, 0:64],
                                                    in_=banks[5][:, 0:N].rearrange("p (j g d) -> p j g d", j=nblk, g=2),
                                                    func=AF.Copy),
                      reads=[BK(5)], writes=[vres(j) for j in blks])
            jobs.append(job_v)

            def job_k(g):
                b = proj_chunk(KD0 + 128 * g, N)
                qknorm(b, N, 1, kdst[:, g, c0:c0 + N], [kres(j) for j in blks])

            if mode == "meta":
                for g in range(2):
                    jobs.append((lambda g: lambda: job_k(g))(g))
                return jobs

            def job_glu(j):
                if mode == "halo":
                    lo = 96 if b0 == 0 else 0
                    n_ = 32
                else:
                    lo, n_ = 0, N
                bg = proj_chunk(CG0 + 128 * j, n_, lo)
                t = nxt("th", 2)
                S.add("act", lambda e: e.activation(out=th[t][:, 0:n_], in_=banks[bg][:, 0:n_], func=AF.Tanh, scale=0.5),
                      reads=[BK(bg)], writes=[("th", t)])
                bv = proj_chunk(CV0 + 128 * j, n_, lo)
                S.add("dve", lambda e: e.scalar_tensor_tensor(out=uT[:, j, c0 + lo:c0 + lo + n_], in0=th[t][:, 0:n_],
                                                              scalar=1.0, in1=banks[bv][:, 0:n_], op0=ALU.add, op1=ALU.mult),
                      reads=[("th", t), BK(bv)], writes=[("uT", b0 + jj) for jj in blks])
                if j == 3:
                    for jj in blks:
                        u_ready.add((seg, b0 + jj))

            def job_ga(j):
                b = proj_chunk(GA0 + 128 * j, N)
                silu2(b, N, gaT[qslot][:, j, 0:N], [("gaT", qslot, jj) for jj in blks])

            def job_gc(j):
                b = proj_chunk(GC0 + 128 * j, N)
                silu2(b, N, gcT[qslot][:, j, 0:N], [("gcT", qslot)])

            def job_q(j):
                b = proj_chunk(Q0 + 128 * j, N)
                qknorm(b, N, 0, qT[qslot][:, 0:nblk, j, :], [("qT", qslot, jj) for jj in blks], view3=True)

            for j in range(4):
                jobs.append((lambda j: lambda: job_glu(j))(j))
            if mode == "full":
                for j in range(4):
                    jobs.append((lambda j: lambda: job_ga(j))(j))
                for j in range(4):
                    jobs.append((lambda j: lambda: job_gc(j))(j))
            for g in range(2):
                jobs.append((lambda g: lambda: job_k(g))(g))
            if mode == "full":
                for j in range(4):
                    jobs.append((lambda j: lambda: job_q(j))(j))
            return jobs

        def att_scores(seg, qslot, jq, bq):
            qc = jq * 128
            ptiles = {}
            for g in range(2):
                for ty, kb in enumerate((bq - 1, bq, bq + 1)):
                    sb_ = 6 + nxt("st2", 2)

                    def smm(e, g=g, ty=ty, kb=kb, sb_=sb_):
                        e.matmul(banks[sb_][:, 0:256], lhsT=kTd[0:64, g, kb * 128:(kb + 1) * 128],
                                 rhs=qT[qslot][0:64, jq, 2 * g:2 * g + 2, :], start=True, stop=False,
                                 skip_group_check=True)
                        e.matmul(banks[sb_][:, 0:512], lhsT=IDN[:], rhs=BIAS[:, ty, 4 * g:4 * g + 4, :],
                                 start=False, stop=False, skip_group_check=True)
                        return e.matmul(banks[sb_][:, 256:512], lhsT=kTd[64:128, g, kb * 128:(kb + 1) * 128],
                                        rhs=qT[qslot][64:128, jq, 2 * g:2 * g + 2, :], start=False, stop=True,
                                        skip_group_check=True)
                    S.add("pe", smm, reads=[("kTd", kb), ("qT", qslot, jq), "IDN", "BIAS"], writes=[BK(sb_)])
                    p = nxt("pt", NPT)
                    ptiles[(g, ty)] = p
                    if kb == 0 or kb == NB + 1:
                        col = 2 * seg + (0 if kb == 0 else 1)
                        S.add("act", (lambda sb_, p, col: lambda e: e.activation(
                            out=PT[p][:], in_=banks[sb_][:], func=AF.Exp, bias=hb[:, col:col + 1]))(sb_, p, col),
                            reads=[BK(sb_), "hb"], writes=[("PT", p)])
                    else:
                        S.add("act", (lambda sb_, p: lambda e: e.activation(out=PT[p][:], in_=banks[sb_][:],
                                                                             func=AF.Exp))(sb_, p),
                              reads=[BK(sb_)], writes=[("PT", p)])
            sbm = 6 + nxt("st2", 2)

            def mmm(e):
                ins = None
                for g in range(2):
                    for a in range(2):
                        ins = e.matmul(banks[sbm][32 * g:32 * g + 16, a * 256:(a + 1) * 256],
                                       lhsT=kTmZ[:, g, a, 32 * g:32 * g + 16],
                                       rhs=qT[qslot][:, jq, 2 * g:2 * g + 2, :], start=True, stop=True)
                return ins
            S.add("pe", mmm, reads=[("kTmZ",), ("qT", qslot, jq)], writes=[BK(sbm)])
            pm = nxt("ptm", 2)
            S.add("act", lambda e: e.activation(out=PTm[pm][0:48, :], in_=banks[sbm][0:48, :], func=AF.Exp),
                  reads=[BK(sbm)], writes=[("PTm", pm)])
            return dict(seg=seg, qslot=qslot, jq=jq, bq=bq, qc=qc, ptiles=ptiles, pm=pm)

        def att_pv(cx):
            qslot, jq, bq, qc, ptiles, pm = cx["qslot"], cx["jq"], cx["bq"], cx["qc"], cx["ptiles"], cx["pm"]
            for g in range(2):
                pvb = g

                def pv(e, g=g, pvb=pvb):
                    ins = None
                    for hl in range(4):
                        a, j = hl % 2, hl // 2
                        c_ = a * 256 + j * 128
                        for ty, kb in enumerate((bq - 1, bq, bq + 1)):
                            e.matmul(banks[pvb][:, hl * 65:hl * 65 + 65], lhsT=PT[ptiles[(g, ty)]][:, c_:c_ + 128],
                                     rhs=vaug[:, kb, g, 0:65], start=(ty == 0), stop=False)
                        ins = e.matmul(banks[pvb][:, hl * 65:hl * 65 + 65], lhsT=PTm[pm][32 * g:32 * g + 16, c_:c_ + 128],
                                       rhs=vm[32 * g:32 * g + 16, 0, g, 0:65], start=False, stop=True)
                    return ins
                S.add("pe", pv, reads=[("PT", ptiles[(g, ty)]) for ty in range(3)] + [("PTm", pm), ("vm",), ("vm1",),
                                                                                      ("vaug1",)]
                      + [("vaug", kb) for kb in (bq - 1, bq, bq + 1)], writes=[BK(pvb)])
            o = nxt("on", 2)
            for g in range(2):
                S.add("dve", (lambda g: lambda e: e.tensor_tensor(
                    out=den[o][:, 4 * g:4 * g + 4], in0=banks[g][:, 64:260:65], in1=esink[:, 4 * g:4 * g + 4],
                    op=ALU.add))(g), reads=[BK(g), "esink"], writes=[("den", o, g)])
            S.add("dve", lambda e: e.reciprocal(out=den[o][:], in_=den[o][:]),
                  reads=[("den", o, 0), ("den", o, 1)], writes=[("den", o, 0), ("den", o, 1)])
            for g in range(2):
                S.add("dve", (lambda g: lambda e: e.tensor_tensor(
                    out=On[o][:, 256 * g:256 * g + 256].rearrange("p (h d) -> p h d", h=4),
                    in0=banks[g][:, 0:260].rearrange("p (h d) -> p h d", h=4)[:, :, 0:64],
                    in1=den[o][:, 4 * g:4 * g + 4].unsqueeze(2).to_broadcast([128, 4, 64]), op=ALU.mult))(g),
                    reads=[BK(g), ("den", o, 0), ("den", o, 1)], writes=[("On", o, g)])
            cx["o"] = o

        def att_out(cx):
            qslot, jq, qc, o = cx["qslot"], cx["jq"], cx["qc"], cx["o"]
            otb = banks[4][:].bitcast(BF16)

            def otr(e):
                ins = None
                for j in range(4):
                    ins = e.transpose(out=otb[:, j * 128:(j + 1) * 128], in_=On[o][:, j * 128:(j + 1) * 128],
                                      identity=IDN[:])
                return ins
            S.add("pe", otr, reads=[("On", o, 0), ("On", o, 1), "IDN"], writes=[BK(4)])
            S.add("dve", lambda e: e.tensor_tensor(out=mixT[:, 0:4, qc:qc + 128],
                                                   in0=otb[:, 0:512].rearrange("p (j t) -> p j t", j=4),
                                                   in1=gaT[qslot][:, 0:4, qc:qc + 128], op=ALU.mult),
                  reads=[BK(4), ("gaT", qslot, jq)], writes=[("mixA", jq)])

        def conv_mm(b0, cc):
            ublk = [("uT", b) for b in range(b0, b0 + 5)]
            c0 = b0 * 128
            cb = 6 + nxt("cv", 2)

            def cmm(e, cc=cc, cb=cb):
                ins = None
                for k in range(16, CONV_K):
                    ins = e.matmul(banks[cb][:], lhsT=DG[:, cc, k - 16, :], rhs=uT[:, cc, c0 + k - 15:c0 + k - 15 + 512],
                                   start=(k == 16), stop=(k == CONV_K - 1))
                return ins
            S.add("pe", cmm, reads=ublk + ["DG"], writes=[BK(cb)])
            S.add("dve", (lambda cc, cb: lambda e: e.tensor_tensor(out=acc[:, cc, :], in0=banks[cb][:], in1=acc[:, cc, :],
                                                                    op=ALU.add))(cc, cb),
                  reads=[BK(cb), ("acc", cc)], writes=[("acc", cc)])
            S.add("act", (lambda cc: lambda e: e.activation(out=yb[:, cc, :], in_=acc[:, cc, :], func=AF.Identity,
                                                             bias=v4[:, cc:cc + 1]))(cc),
                  reads=[("acc", cc), "v4"], writes=[("yb", cc)])
            ys = nxt("ysq", 4)
            S.add("act", (lambda cc, ys: lambda e: e.activation(out=ysq[ys][:], in_=acc[:, cc, :], func=AF.Square,
                                                                 bias=v4[:, cc:cc + 1]))(cc, ys),
                  reads=[("acc", cc), "v4"], writes=[("ysq", ys)])
            pending_stat.append((cc, ys))

        mac_q = []
        in_drain = [False]

        def conv_queue(seg, b0):
            c0 = b0 * 128
            for k in range(0, 16):
                for cc in range(4):
                    lo = c0 + k - 15
                    need = {(seg, b) for b in range(lo // 128, (lo + 511) // 128 + 1)}
                    j = cc * CONV_K + k
                    rd = [("uT", b) for (_, b) in need] + [("acc", cc), "cwp"]
                    if k == 0:
                        fn = (lambda cc, lo, j: lambda e: e.tensor_scalar(out=acc[:, cc, :], in0=uT[:, cc, lo:lo + 512],
                                                                           scalar1=cwp[:, j:j + 1], scalar2=None,
                                                                           op0=ALU.mult))(cc, lo, j)
                    else:
                        fn = (lambda cc, lo, j: lambda e: e.scalar_tensor_tensor(out=acc[:, cc, :], in0=uT[:, cc, lo:lo + 512],
                                                                                  scalar=cwp[:, j:j + 1], in1=acc[:, cc, :],
                                                                                  op0=ALU.mult, op1=ALU.add))(cc, lo, j)
                    mac_q.append((need, fn, rd, cc))

        def drain(n):
            if in_drain[0]:
                return
            in_drain[0] = True
            while n > 0 and mac_q and mac_q[0][0] <= u_ready:
                need, fn, rd, cc = mac_q.pop(0)
                S.add("dve", fn, reads=rd, writes=[("acc", cc)])
                n -= 1
            in_drain[0] = False

        _add = S.add

        def add_hook(eng, fn, reads=(), writes=(), dma=False):
            op = _add(eng, fn, reads=reads, writes=writes, dma=dma)
            if eng == "dve" and not dma and not in_drain[0]:
                drain(1)
            return op
        S.add = add_hook

        pending_stat = []

        def conv_statmm():
            while pending_stat:
                cc, ys = pending_stat.pop(0)
                S.add("pe", (lambda cc: lambda e: e.matmul(banks[2][:], lhsT=ONL[:], rhs=yb[:, cc, :], start=(cc == 0),
                                                            stop=(cc == 3)))(cc),
                      reads=[("yb", cc), "ONL"], writes=[BK(2)])
                S.add("pe", (lambda cc, ys: lambda e: e.matmul(banks[3][:], lhsT=ONL[:], rhs=ysq[ys][:], start=(cc == 0),
                                                                stop=(cc == 3)))(cc, ys),
                      reads=[("ysq", ys), "ONL"], writes=[BK(3)])

        def conv_stats():
            S.add("act", lambda e: e.activation(out=m2[:], in_=banks[2][:], func=AF.Square), reads=[BK(2)], writes=["m2"])
            S.add("dve", lambda e: e.scalar_tensor_tensor(out=lnr[:], in0=banks[3][:], scalar=1e-5, in1=m2[:],
                                                          op0=ALU.add, op1=ALU.subtract),
                  reads=[BK(3), "m2"], writes=["lnr"])
            S.add("act", lambda e: e.activation(out=lnr[:], in_=lnr[:], func=AF.Ln), reads=["lnr"], writes=["lnr"])
            S.add("act", lambda e: e.activation(out=lnr[:], in_=lnr[:], func=AF.Exp, scale=-0.5), reads=["lnr"], writes=["lnr"])
            S.add("dve", lambda e: e.scalar_tensor_tensor(out=nmr[:], in0=banks[2][:], scalar=-1.0, in1=lnr[:],
                                                          op0=ALU.mult, op1=ALU.mult),
                  reads=[BK(2), "lnr", "m2"], writes=["m2"])

        def conv_epi(qslot, cc):
            if True:
                S.add("dve", (lambda cc: lambda e: e.tensor_tensor(out=t2[:], in0=yb[:, cc, :], in1=lnr[:],
                                                                    op=ALU.mult))(cc),
                      reads=[("yb", cc), "lnr"], writes=["t2"])
                S.add("dve", lambda e: e.tensor_tensor(out=t2[:], in0=t2[:], in1=nmr[:], op=ALU.add),
                      reads=["t2", "m2"], writes=["t2"])
                t = nxt("th", 2)
                S.add("act", (lambda cc: lambda e: e.activation(out=ta[:], in_=t2[:], func=AF.Identity,
                                                                 scale=v4[:, 4 + cc:5 + cc], bias=v4[:, 8 + cc:9 + cc]))(cc),
                      reads=["t2", "v4"], writes=["ta"])
                S.add("act", (lambda cc, t: lambda e: e.activation(out=th[t][:], in_=t2[:], func=AF.Tanh,
                                                                    scale=v4h[:, cc:cc + 1], bias=v4h[:, 4 + cc:5 + cc]))(cc, t),
                      reads=["t2", "v4h"], writes=[("th", t)])
                S.add("dve", (lambda t: lambda e: e.scalar_tensor_tensor(out=ta[:], in0=th[t][:], scalar=1.0, in1=ta[:],
                                                                          op0=ALU.add, op1=ALU.mult))(t),
                      reads=[("th", t), "ta"], writes=["ta"])
                S.add("dve", (lambda cc: lambda e: e.tensor_tensor(out=yb[:, cc, :], in0=ta[:],
                                                                    in1=gcT[qslot][:, cc, :], op=ALU.mult))(cc),
                      reads=["ta", ("gcT", qslot)], writes=[("yb", cc)])

        def outproj_block(seg, jq, bq):
            qc = jq * 128
            r = nxt("xt", 3)
            dma(xres[r][:], xe[seg, bq * 128:(bq + 1) * 128, :], writes=[("xt", r)])
            for half in range(2):
                ob = 6 + half

                def omm(e, half=half, ob=ob):
                    ins = None
                    for k in range(8):
                        lt = mixT[:, k, qc:qc + 128] if k < 4 else yb[:, k - 4, qc:qc + 128]
                        ins = e.matmul(banks[ob][:], lhsT=lt,
                                       rhs=WO[:, k, half * 512:(half + 1) * 512], start=(k == 0), stop=(k == 7))
                    return ins
                S.add("pe", omm, reads=[("mixA", jq)] + [("yb", cc) for cc in range(4)] + WOALL, writes=[BK(ob)])
                S.add("dve", (lambda half, ob: lambda e: e.tensor_tensor(
                    out=xres[r][:, half * 512:(half + 1) * 512], in0=banks[ob][:],
                    in1=xres[r][:, half * 512:(half + 1) * 512], op=ALU.add))(half, ob),
                    reads=[BK(ob), ("xt", r)], writes=[("xt", r)])
            S.add("pool", lambda e: e.dma_start(out=y[seg, (bq - 1) * 128:bq * 128, :], in_=xres[r][:]),
                  reads=[("xt", r)], dma=True)

        if STOP >= 1:
            norm_T(norm_pre(lambda j: xmeta, 1))
            for jb in projA_jobs(0, 1, "meta"):
                jb()
        S.add("dve", lambda e: e.memset(kTmZ[:], 0.0), writes=[("kTmZ",)])
        for g in range(2):
            for a in range(2):
                S.add("dve", (lambda g, a: lambda e: e.tensor_copy(out=kTmZ[64 * a:64 * a + 64, g, a, :],
                                                                   in_=kTm[64 * a:64 * a + 64, g, 0:48]))(g, a),
                      reads=[("kTm",)], writes=[("kTmZ",)])

        A_list = []
        for seg in range(NSEG if STOP >= 2 else 0):
            A_list.append((seg, 0, "halo", 0, 1))
            for i in range(NST):
                A_list.append((seg, i + 1, "full", 1 + 4 * i, 4))
            A_list.append((seg, NST + 1, "halo", NB + 1, 1))
        B_list = [(seg, i) for seg in range(NSEG if STOP >= 2 else 0) for i in range(NST)]
        pa = [0]
        slot_of = {}

        def src_of(entry):
            seg, k, mode, b0, nblk = entry
            return (lambda seg, b0: lambda j: xe[seg, (b0 + j) * 128:(b0 + j + 1) * 128, :])(seg, b0)

        pre_slots = {}

        def do_pre(idx):
            if idx < len(A_list) and idx not in pre_slots:
                pre_slots[idx] = norm_pre(src_of(A_list[idx]), A_list[idx][4])

        def A_jobs(idx):
            seg, k, mode, b0, nblk = A_list[idx]
            qs = None
            if mode == "full":
                qs = nxt("qslot", 2)
                slot_of[(seg, k)] = qs
            pj = projA_jobs(b0, nblk, mode, qs, seg)

            def first():
                do_pre(idx)
                norm_T(pre_slots[idx])
            return [first, pj[0]] + pj[1:5] + [lambda: do_pre(idx + 1)] + pj[5:]

        def need_idx(seg, i):
            return seg * (NST + 2) + i + 2

        def first_slot(entry, seg, b0):
            eseg, k, mode, eb0, nblk = entry
            if eseg == seg:
                return 0
            last = -1
            for jq in range(4):
                rd = range(b0 + jq - 1, b0 + jq + 2)
                if any(eb0 <= r < eb0 + nblk for r in rd):
                    last = jq
            return last + 1

        for bi, (seg, i) in enumerate(B_list):
            while pa[0] <= need_idx(seg, i):
                for jb in A_jobs(pa[0]):
                    jb()
                pa[0] += 1
            qslot = slot_of[(seg, i + 1)]
            b0 = 1 + 4 * i
            slots = [[] for _ in range(5)]
            if bi + 1 < len(B_list):
                nseg, ni = B_list[bi + 1]
                tgt = need_idx(nseg, ni)
                pend = []
                while pa[0] <= tgt:
                    pend.append(pa[0])
                    pa[0] += 1
                queue = []
                post = False
                for idx in pend:
                    fs = first_slot(A_list[idx], seg, b0)
                    post = post or fs >= 4
                    for jb in A_jobs(idx):
                        queue.append((4 if post else fs, jb))
                n_in = sum(1 for fs, _ in queue if fs < 4)
                quota = max(1, -(-n_in // 4))
                cur, cnt_ = 0, 0
                for fs, jb in queue:
                    if fs >= 4:
                        slots[4].append(jb)
                        continue
                    if fs > cur:
                        cur, cnt_ = fs, 0
                    if cnt_ >= quota and cur < 3:
                        cur, cnt_ = cur + 1, 0
                    slots[cur].append(jb)
                    cnt_ += 1
            if bi == 0:
                conv_queue(seg, b0)
                drain(10 ** 9)
                assert not mac_q
            for cc in range(4):
                conv_mm(b0, cc)
                if cc >= 1:
                    pass
                if cc >= 1:
                    last = pending_stat.pop()
                    conv_statmm()
                    pending_stat.append(last)
            if bi + 1 < len(B_list):
                nseg, ni = B_list[bi + 1]
                conv_queue(nseg, 1 + 4 * ni)
            cxs = []
            for jq in range(4):
                cx = att_scores(seg, qslot, jq, b0 + jq)
                cxs.append(cx)
                if jq == 0:
                    conv_statmm()
                    conv_stats()
                if jq >= 1:
                    att_out(cxs[jq - 1])
                for jb in slots[jq]:
                    jb()
                conv_epi(qslot, jq)
                att_pv(cx)
            for jq in range(2):
                outproj_block(seg, jq, b0 + jq)
            att_out(cxs[3])
            for jb in slots[4]:
                jb()
            for jq in range(2, 4):
                outproj_block(seg, jq, b0 + jq)
            drain(10 ** 9)
            assert not mac_q, "conv MACs left whose u blocks were never produced"

        S.finalize(sems, dsems)
        with nc.Block() as block:
            @block.tensor
            def _(e):
                S.emit("pe", e)

            @block.scalar
            def _(e):
                S.emit("act", e)

            @block.vector
            def _(e):
                S.emit("dve", e)

            @block.gpsimd
            def _(e):
                S.emit("pool", e)

            @block.sync
            def _(e):
                S.emit("sp", e)
                S.final_waits(e)
    return nc


def _const_tables():
    i = np.arange(128)
    jj, ii = np.meshgrid(i, i, indexing="ij")
    slopes = np.exp2(-8.0 * np.arange(1, N_HEADS + 1) / N_HEADS).astype(np.float32)
    tab = np.empty((128, 3, N_HEADS, 128), np.float32)
    dl, vl = 128 + ii - jj, jj >= ii
    dm = np.abs(ii - jj)
    dr, vr = 128 + jj - ii, jj <= ii
    order = [0, 2, 1, 3, 4, 6, 5, 7]
    for sl, h in enumerate(order):
        tab[:, 0, sl, :] = np.where(vl, -slopes[h] * dl, NEG)
        tab[:, 1, sl, :] = -slopes[h] * dm
        tab[:, 2, sl, :] = np.where(vr, -slopes[h] * dr, NEG)
    bf = ml_dtypes.bfloat16
    ident = np.eye(128, dtype=np.float32).astype(bf)
    bd = np.kron(np.eye(2, dtype=np.float32), np.full((64, 64), 1.0 / 64, np.float32)).astype(bf)
    onesln = np.full((128, 128), 1.0 / 512, np.float32).astype(bf)
    return tab.reshape(128, -1).astype(bf), ident, bd, onesln


def _segment(xseq, meta, t0, n):
    S_ = xseq.shape[0]
    if t0 > 0:
        left, hl = xseq[t0 - 128:t0], 0.0
    else:
        left = np.concatenate([np.zeros((128 - N_META, D_MODEL), np.float32), meta], 0)
        hl = NEG
    if t0 + n < S_:
        right, hr = xseq[t0 + n:t0 + n + 128], 0.0
    else:
        right, hr = np.zeros((128, D_MODEL), np.float32), NEG
    return np.concatenate([left, xseq[t0:t0 + n], right], 0), hl, hr


def _core_inputs(segs, meta, weights, consts):
    xs, hbv = [], []
    for (xseq, t0, n) in segs:
        x_, hl, hr = _segment(xseq, meta, t0, n)
        xs.append(x_)
        hbv += [hl, hr]
    xmeta = np.zeros((128, D_MODEL), np.float32)
    xmeta[0:N_META] = meta
    xmeta[32:32 + N_META] = meta
    d = dict(weights)
    d.update(consts)
    d["xe"] = np.ascontiguousarray(np.stack(xs, 0))
    d["xmeta"] = xmeta
    d["hb"] = np.ascontiguousarray(np.broadcast_to(np.asarray(hbv, np.float32)[None, :], (128, len(hbv))))
    return d


def _pack_weights(norm_w, w_in, q_norm_w, k_norm_w, sink_logits, conv_w, conv_b, conv_ln_w, conv_ln_b, w_out):
    f = np.float32
    return {
        "w_in": np.ascontiguousarray(w_in[0], f),
        "w_out": np.ascontiguousarray(w_out[0], f),
        "normw_pk": np.ascontiguousarray(norm_w[0].reshape(8, 128).T, f),
        "qkw": np.ascontiguousarray(np.stack([np.tile(q_norm_w[0], 2), np.tile(k_norm_w[0], 2)], 1), f),
        "convw_pk": np.ascontiguousarray(conv_w[0].reshape(CONV_K, 4, 128).transpose(2, 1, 0).reshape(128, 4 * CONV_K), f),
        "vec4": np.ascontiguousarray(np.concatenate([conv_b[0].reshape(4, 128).T, conv_ln_w[0].reshape(4, 128).T,
                                                     conv_ln_b[0].reshape(4, 128).T], 1), f),
        "sink_b": np.ascontiguousarray(np.broadcast_to(sink_logits[0][None, :], (128, N_HEADS)), f),
    }


_NC_CACHE = {}


def kernel(x_prompt, x_sample, meta_tokens, norm_w, w_in, q_norm_w, k_norm_w, sink_logits,
           conv_w, conv_b, conv_ln_w, conv_ln_b, w_out):
    NB, NSEG, NCORE = 8, 8, 8
    n = NB * 128
    xp = np.asarray(x_prompt, np.float32)
    xs = np.asarray(x_sample, np.float32)
    meta = np.asarray(meta_tokens, np.float32)
    weights = _pack_weights(*[np.asarray(a, np.float32) for a in
                              (norm_w, w_in, q_norm_w, k_norm_w, sink_logits, conv_w, conv_b, conv_ln_w, conv_ln_b, w_out)])
    tab, ident, bd, onesln = _const_tables()
    consts = {"bias_c": tab, "ident_c": ident, "bd_c": bd, "onesln_c": onesln}
    in_maps, place = [], []
    for c in range(NCORE):
        segs, pl = [], []
        for sq_ in (2 * c, 2 * c + 1):
            for t0 in (0, n):
                segs.append((xp[sq_], t0, n))
                pl.append((0, sq_, t0))
        sseq, base = c // 4, (c % 4) * 4 * n
        for q in range(4):
            segs.append((xs[sseq], base + q * n, n))
            pl.append((1, sseq, base + q * n))
        in_maps.append(_core_inputs(segs, meta, weights, consts))
        place.append(pl)
    key = (NSEG, NB)
    if key not in _NC_CACHE:
        _NC_CACHE[key] = _build(NSEG, NB)
    res = run_bass_kernel_spmd(_NC_CACHE[key], in_maps, core_ids=list(range(NCORE)))
    yp = np.empty(xp.shape, np.float32)
    ys = np.empty(xs.shape, np.float32)
    for c in range(NCORE):
        yc = res.results[c]["y"]
        for s_, (which, sq_, t0) in enumerate(place[c]):
            (yp if which == 0 else ys)[sq_, t0:t0 + n] = yc[s_]
    return (yp, ys)
```

```python
import math
from contextlib import ExitStack

import numpy as np
import ml_dtypes
import concourse.bass as bass
import concourse.mybir as mybir
from concourse.bass_utils import run_bass_kernel_spmd

F32 = mybir.dt.float32
BF16 = mybir.dt.bfloat16
AF = mybir.ActivationFunctionType
ALU = mybir.AluOpType

D_MODEL = 1024
N_META = 16
N_HEADS = 8
CONV_K = 31
NEG = -30000.0

Q0, KD0, KD1, V0, GA0, CV0, CG0, GC0, WCOLS = 0, 512, 640, 768, 896, 1408, 1920, 2432, 2944


class _Op:
    __slots__ = ("eng", "fn", "deps", "signal", "sem", "val", "dma", "waits", "prev_val")

    def __init__(self, eng, fn, dma):
        self.eng = eng
        self.fn = fn
        self.dma = dma
        self.deps = set()
        self.signal = False
        self.sem = None
        self.val = 0
        self.prev_val = 0
        self.waits = []


class Sched:
    ENGS = ("pe", "act", "dve", "pool", "sp")

    def __init__(self):
        self.ops = {e: [] for e in self.ENGS}
        self.last_writer = {}
        self.readers = {}

    def add(self, eng, fn, reads=(), writes=(), dma=False):
        op = _Op(eng, fn, dma)
        deps = op.deps
        for r in reads:
            w = self.last_writer.get(r)
            if w is not None:
                deps.add(w)
        for w_ in writes:
            w = self.last_writer.get(w_)
            if w is not None:
                deps.add(w)
            for rd in self.readers.get(w_, ()):
                deps.add(rd)
        for r in reads:
            self.readers.setdefault(r, []).append(op)
        for w_ in writes:
            self.last_writer[w_] = op
            self.readers[w_] = []
        deps.discard(op)
        if eng == "pe" and not dma:
            op.deps = {d for d in deps if not (d.eng == "pe" and not d.dma)}
        for d in op.deps:
            d.signal = True
        self.ops[eng].append(op)
        return op

    def finalize(self, eng_sems, dma_sems):
        for e in self.ENGS:
            cnt = 0
            dcount = {}
            k = 0
            pool = dma_sems.get(e, [])
            for op in self.ops[e]:
                if op.dma:
                    s = pool[k % len(pool)]
                    k += 1
                    prev = dcount.get(id(s), 0)
                    op.sem = s
                    op.prev_val = prev
                    op.val = prev + 16
                    dcount[id(s)] = op.val
                elif op.signal:
                    cnt += 1
                    op.sem = eng_sems[e]
                    op.val = cnt
        for e in self.ENGS:
            waited = {}
            for op in self.ops[e]:
                need = {}
                for d in op.deps:
                    key = id(d.sem)
                    if d.val > need.get(key, (None, 0))[1]:
                        need[key] = (d.sem, d.val)
                if op.dma and op.prev_val > 0:
                    key = id(op.sem)
                    if op.prev_val > need.get(key, (None, 0))[1]:
                        need[key] = (op.sem, op.prev_val)
                for key, (s, v) in need.items():
                    if v > waited.get(key, 0):
                        waited[key] = v
                        op.waits.append((s, v))

    def emit(self, eng_name, e):
        for op in self.ops[eng_name]:
            for (s, v) in op.waits:
                e.wait_ge(s, v)
            ins = op.fn(e)
            if op.dma:
                ins.then_inc(op.sem, 16)
            elif op.signal:
                ins.then_inc(op.sem, 1)

    def final_waits(self, e):
        last = {}
        for q in self.ENGS:
            for op in self.ops[q]:
                if op.dma:
                    last[id(op.sem)] = (op.sem, op.val)
        for s, v in last.values():
            e.wait_ge(s, v)


def _build(NSEG, NB, STOP=9):
    assert NB % 4 == 0
    NST = NB // 4
    NBX = NB + 2
    TX = NBX * 128
    nc = bass.Bass("TRN2", target_bir_lowering=False)

    def din(name, shape, dt=F32):
        return nc.dram_tensor(name, shape, dt, kind="ExternalInput").ap()

    xe = din("xe", [NSEG, TX, D_MODEL])
    xmeta = din("xmeta", [128, D_MODEL])
    w_in = din("w_in", [D_MODEL, 2816])
    w_out = din("w_out", [D_MODEL, D_MODEL])
    normw_pk = din("normw_pk", [128, 8])
    qkw = din("qkw", [128, 2])
    convw_pk = din("convw_pk", [128, 4 * CONV_K])
    vec4 = din("vec4", [128, 12])
    sink_b = din("sink_b", [128, 8])
    hb_in = din("hb", [128, 2 * NSEG])
    ident_c = din("ident_c", [128, 128], BF16)
    bd_c = din("bd_c", [128, 128], BF16)
    onesln_c = din("onesln_c", [128, 128], BF16)
    bias_c = din("bias_c", [128, 3 * 8 * 128], BF16)
    y = nc.dram_tensor("y", [NSEG, NB * 128, D_MODEL], F32, kind="ExternalOutput").ap()

    S = Sched()
    with ExitStack() as st:
        E = st.enter_context

        def sb(name, shape, dt=F32):
            return E(nc.sbuf_tensor("sb_" + name, shape, dt))

        W = sb("W", [128, 8, WCOLS], BF16)
        WO = sb("WO", [128, 8, D_MODEL], BF16)
        DG = sb("DG", [128, 4, 15, 128], BF16)
        BIAS = sb("BIAS", [128, 3, 8, 128], BF16)
        IDN = sb("IDN", [128, 128], BF16)
        BD = sb("BD", [128, 128], BF16)
        ONL = sb("ONL", [128, 128], BF16)
        nwp = sb("nwp", [128, 8])
        qkv = sb("qkv", [128, 2])
        qks = sb("qks", [128, 2])
        cwp = sb("cwp", [128, 4 * CONV_K])
        v4 = sb("v4", [128, 12])
        v4h = sb("v4h", [128, 8])
        snk = sb("snk", [128, 8])
        esink = sb("esink", [128, 8])
        hb = sb("hb", [128, 2 * NSEG])
        mhalf = sb("mhalf", [128, 1])
        epsc = sb("epsc", [128, 1])
        kTd = sb("kTd", [128, 2, TX], BF16)
        vaug = sb("vaug", [128, NBX, 2, 66], BF16)
        uT = sb("uT", [128, 4, TX], BF16)
        kTm = sb("kTm", [128, 2, 128], BF16)
        vm = sb("vm", [128, 1, 2, 66], BF16)
        kTmZ = sb("kTmZ", [128, 2, 2, 48], BF16)
        hT = sb("hT", [128, 8, 512], BF16)
        qT = [sb(f"qT{i}", [128, 4, 4, 128], BF16) for i in range(2)]
        gaT = [sb(f"gaT{i}", [128, 4, 512], BF16) for i in range(2)]
        gcT = [sb(f"gcT{i}", [128, 4, 512], BF16) for i in range(2)]
        mixT = sb("mixT", [128, 4, 512], BF16)
        xt = [sb(f"xt{i}", [128, D_MODEL]) for i in range(3)]
        xres = xt
        htm = [sb(f"htm{i}", [128, D_MODEL], BF16) for i in range(4)]
        ss = [sb(f"ss{i}", [128, 1]) for i in range(4)]
        sq = [sb(f"sq{i}", [128, 512], BF16) for i in range(2)]
        tq = [sb(f"tq{i}", [128, 512]) for i in range(2)]
        th = [sb(f"th{i}", [128, 512]) for i in range(2)]
        NPT = 8
        PT = [sb(f"PT{i}", [128, 512], BF16) for i in range(NPT)]
        PTm = [sb(f"PTm{i}", [128, 512], BF16) for i in range(2)]
        On = [sb(f"On{i}", [128, 512], BF16) for i in range(2)]
        den = [sb(f"den{i}", [128, 8]) for i in range(2)]
        yb = sb("yb", [128, 4, 512], BF16)
        acc = sb("acc", [128, 4, 512])
        ysq = [sb(f"ysq{i}", [128, 512], BF16) for i in range(4)]
        lnr = sb("lnr", [128, 512])
        m2 = sb("m2", [128, 512])
        nmr = m2
        t2 = sb("t2", [128, 512])
        ta = sb("ta", [128, 512])

        banks = [E(nc.psum_tensor(f"bank{i}", [128, 512], F32)) for i in range(8)]
        sems = {e: E(nc.semaphore("s_" + e)) for e in Sched.ENGS}
        dsems = {"sp": [E(nc.semaphore(f"d{i}")) for i in range(24)],
                 "pool": [E(nc.semaphore(f"dp{i}")) for i in range(8)]}

        def BK(i):
            return ("bank", i)

        cnt = {"htm": 0, "xt": 0, "xres": 0, "sq": 0, "th": 0, "pt": 0, "ptm": 0, "on": 0, "ysq": 0, "proj": 0,
               "st2": 0, "cv": 0, "qslot": 0}

        def nxt(k, n):
            v = cnt[k] % n
            cnt[k] += 1
            return v

        def dma(out_ap, in_ap, reads=(), writes=()):
            S.add("sp", lambda e: e.dma_start(out=out_ap, in_=in_ap), reads=reads, writes=writes, dma=True)

        dma(nwp[:], normw_pk, writes=["nwp"])
        dma(qkv[:], qkw, writes=["qkv"])
        dma(cwp[:], convw_pk, writes=["cwp"])
        dma(v4[:], vec4, writes=["v4"])
        dma(snk[:], sink_b, writes=["snk"])
        dma(hb[:], hb_in, writes=["hb"])
        dma(IDN[:], ident_c, writes=["IDN"])
        dma(BD[:], bd_c, writes=["BD"])
        dma(ONL[:], onesln_c, writes=["ONL"])
        dma(BIAS[:].rearrange("p a h q -> p (a h q)"), bias_c, writes=["BIAS"])

        S.add("pool", lambda e: e.memset(mhalf[:], -0.5), writes=["mhalf"])
        S.add("pool", lambda e: e.memset(epsc[:], 1e-6), writes=["epsc"])
        S.add("pool", lambda e: e.memset(vaug[:, :, :, 64:66], 1.0), writes=[("vaug1",)])
        S.add("pool", lambda e: e.memset(vm[:, :, :, 64:66], 1.0), writes=[("vm1",)])
        for i in range(8):
            S.add("dve", (lambda i: lambda e: e.memset(banks[i][:], 0.0))(i), writes=[BK(i)])
        S.add("dve", lambda e: e.tensor_scalar(out=qks[:, 0:1], in0=qkv[:, 0:1], scalar1=0.125, scalar2=None,
                                               op0=ALU.mult), reads=["qkv"], writes=["qks0"])
        S.add("dve", lambda e: e.tensor_copy(out=qks[:, 1:2], in_=qkv[:, 1:2]), reads=["qkv"], writes=["qks1"])
        S.add("dve", lambda e: e.tensor_scalar(out=cwp[:], in0=cwp[:], scalar1=0.5, scalar2=None, op0=ALU.mult),
              reads=["cwp"], writes=["cwp"])
        S.add("dve", lambda e: e.tensor_scalar(out=v4h[:], in0=v4[:, 4:12], scalar1=0.5, scalar2=None,
                                               op0=ALU.mult), reads=["v4"], writes=["v4h"])
        S.add("act", lambda e: e.activation(out=esink[:], in_=snk[:], func=AF.Exp), reads=["snk"], writes=["esink"])

        def stage_view(slot):
            return xt[slot][:].rearrange("p (k n) -> p k n", k=8)

        nwb = nwp[:].unsqueeze(2).to_broadcast([128, 8, 128])

        def wpiece(c0):
            slot = nxt("xt", 3)
            sv = stage_view(slot)
            dma(sv, w_in[:, c0:c0 + 128].rearrange("(k p) n -> p k n", p=128), writes=[("xt", slot)])
            if c0 < 512:
                dsts = [(Q0 + c0, 0, 128)]
            elif c0 == 512:
                dsts = [(KD0, 0, 64), (KD0 + 64, 0, 64), (KD1, 64, 64), (KD1 + 64, 64, 64)]
            elif c0 == 640:
                dsts = [(V0, 0, 128)]
            elif c0 < 1280:
                dsts = [(GA0 + c0 - 768, 0, 128)]
            elif c0 < 1792:
                dsts = [(CV0 + c0 - 1280, 0, 128)]
            elif c0 < 2304:
                dsts = [(CG0 + c0 - 1792, 0, 128)]
            else:
                dsts = [(GC0 + c0 - 2304, 0, 128)]
            for (d0, s0, n) in dsts:
                S.add("dve", (lambda d0, s0, n, sv: lambda e: e.tensor_tensor(
                    out=W[:, :, d0:d0 + n], in0=sv[:, :, s0:s0 + n],
                    in1=nwp[:].unsqueeze(2).to_broadcast([128, 8, n]), op=ALU.mult))(d0, s0, n, sv),
                    reads=[("xt", slot), "nwp"], writes=[("W", d0)])

        for c0 in range(0, 2816, 128):
            wpiece(c0)

        def wopiece(c0):
            slot = nxt("xt", 3)
            sv = stage_view(slot)
            dma(sv, w_out[:, c0:c0 + 128].rearrange("(k p) n -> p k n", p=128), writes=[("xt", slot)])
            S.add("dve", lambda e: e.tensor_scalar(out=WO[:, 0:4, c0:c0 + 128], in0=sv[:, 0:4, :], scalar1=0.5,
                                                   scalar2=None, op0=ALU.mult),
                  reads=[("xt", slot)], writes=[("WO", c0, 0)])
            S.add("dve", lambda e: e.tensor_scalar(out=WO[:, 4:8, c0:c0 + 128], in0=sv[:, 4:8, :], scalar1=0.25,
                                                   scalar2=None, op0=ALU.mult),
                  reads=[("xt", slot)], writes=[("WO", c0, 1)])

        for c0 in range(0, D_MODEL, 128):
            wopiece(c0)

        def dgs(e):
            ins = None
            for cc in range(4):
                for k in range(16, CONV_K):
                    j = cc * CONV_K + k
                    ins = e.tensor_scalar(out=DG[:, cc, k - 16, :], in0=IDN[:], scalar1=cwp[:, j:j + 1], scalar2=None,
                                          op0=ALU.mult)
            return ins
        S.add("dve", dgs, reads=["IDN", "cwp"], writes=["DG"])
        WALL = [("W", d) for d in ([Q0 + i for i in range(0, 512, 128)] + [KD0, KD0 + 64, KD1, KD1 + 64, V0]
                                   + [b + i for b in (GA0, CV0, CG0, GC0) for i in range(0, 512, 128)])]
        WOALL = [("WO", c0, h) for c0 in range(0, D_MODEL, 128) for h in range(2)]

        def norm_pre(src, nblk):
            hs = []
            for j in range(nblk):
                slot = nxt("xt", 3)
                h_ = nxt("htm", 4)
                hs.append(h_)
                dma(xt[slot][:], src(j), writes=[("xt", slot)])
                S.add("act", (lambda slot, h_: lambda e: e.activation(out=htm[h_][:], in_=xt[slot][:], func=AF.Square,
                                                                       accum_out=ss[h_][:]))(slot, h_),
                      reads=[("xt", slot)], writes=[("htm", h_), ("ss", h_)])
                S.add("dve", (lambda h_: lambda e: e.tensor_scalar(out=ss[h_][:], in0=ss[h_][:], scalar1=1.0 / D_MODEL,
                                                                    scalar2=1e-6, op0=ALU.mult, op1=ALU.add))(h_),
                      reads=[("ss", h_)], writes=[("ss", h_)])
                S.add("pool", (lambda h_: lambda e: e.tensor_tensor(out=ss[h_][:], in0=ss[h_][:], in1=mhalf[:, 0:1],
                                                                     op=ALU.pow))(h_),
                      reads=[("ss", h_), "mhalf"], writes=[("ss", h_)])
                S.add("dve", (lambda slot, h_: lambda e: e.tensor_scalar(out=htm[h_][:], in0=xt[slot][:], scalar1=ss[h_][:],
                                                                          scalar2=None, op0=ALU.mult))(slot, h_),
                      reads=[("xt", slot), ("ss", h_)], writes=[("htm", h_)])
            return hs

        def norm_T(hs):
            for j, h_ in enumerate(hs):
                tb = 4 + (j % 2)
                trb = banks[tb][:].bitcast(BF16)

                def tr(e, h_=h_, trb=trb):
                    ins = None
                    for k in range(8):
                        ins = e.transpose(out=trb[:, k * 128:(k + 1) * 128], in_=htm[h_][:, k * 128:(k + 1) * 128],
                                          identity=IDN[:])
                    return ins
                S.add("pe", tr, reads=[("htm", h_), "IDN"], writes=[BK(tb)])
                S.add("act", (lambda j, trb: lambda e: e.activation(out=hT[:, :, j * 128:(j + 1) * 128],
                                                                     in_=trb.rearrange("p (k t) -> p k t", k=8),
                                                                     func=AF.Copy))(j, trb),
                      reads=[BK(tb)], writes=[("hT", j)])

        def proj_chunk(wcol, N, lo=0):
            b = nxt("proj", 3)

            def mm(e):
                ins = None
                for k in range(8):
                    ins = e.matmul(banks[b][:, 0:N], lhsT=W[:, k, wcol:wcol + 128], rhs=hT[:, k, lo:lo + N],
                                   start=(k == 0), stop=(k == 7))
                return ins
            S.add("pe", mm, reads=[("hT", j) for j in range((lo + N + 127) // 128)] + WALL, writes=[BK(b)])
            return b

        def silu2(b, N, out_ap, out_res):
            t = nxt("th", 2)
            S.add("act", lambda e: e.activation(out=th[t][:, 0:N], in_=banks[b][:, 0:N], func=AF.Tanh, scale=0.5),
                  reads=[BK(b)], writes=[("th", t)])
            S.add("dve", lambda e: e.scalar_tensor_tensor(out=out_ap, in0=th[t][:, 0:N], scalar=1.0,
                                                          in1=banks[b][:, 0:N], op0=ALU.add, op1=ALU.mult),
                  reads=[("th", t), BK(b)], writes=out_res)

        def qknorm(b, N, scol, out_ap, out_res, view3=False):
            s_ = nxt("sq", 2)
            stb = 3 if s_ == 0 else 5
            S.add("act", lambda e: e.activation(out=sq[s_][:, 0:N], in_=banks[b][:, 0:N], func=AF.Square),
                  reads=[BK(b)], writes=[("sq", s_)])
            S.add("pe", lambda e: e.matmul(banks[stb][:, 0:N], lhsT=BD[:], rhs=sq[s_][:, 0:N], start=True, stop=True),
                  reads=[("sq", s_), "BD"], writes=[BK(stb)])
            S.add("act", lambda e: e.activation(out=tq[s_][:, 0:N], in_=banks[stb][:, 0:N], func=AF.Ln, bias=epsc[:, 0:1]),
                  reads=[BK(stb), "epsc"], writes=[("tq", s_)])
            S.add("act", lambda e: e.activation(out=tq[s_][:, 0:N], in_=tq[s_][:, 0:N], func=AF.Exp, scale=-0.5),
                  reads=[("tq", s_)], writes=[("tq", s_)])
            v3 = (lambda ap: ap.rearrange("p (j t) -> p j t", t=128)) if view3 else (lambda ap: ap)
            S.add("dve", lambda e: e.scalar_tensor_tensor(out=out_ap, in0=v3(banks[b][:, 0:N]), scalar=qks[:, scol:scol + 1],
                                                          in1=v3(tq[s_][:, 0:N]), op0=ALU.mult, op1=ALU.mult),
                  reads=[BK(b), ("tq", s_), "qks0", "qks1"], writes=out_res)

        u_ready = set()

        def projA_jobs(b0, nblk, mode, qslot=None, seg=None):
            N = nblk * 128
            if mode == "meta":
                kdst, vdst, kres, vres = kTm, vm, lambda j: ("kTm",), lambda j: ("vm",)
                c0 = 0
            else:
                kdst, vdst, kres, vres = kTd, vaug, lambda j: ("kTd", b0 + j), lambda j: ("vaug", b0 + j)
                c0 = b0 * 128
            blks = range(nblk)
            jobs = []

            def job_v():
                def vmm(e):
                    ins = None
                    for j in range(nblk):
                        for k in range(8):
                            ins = e.matmul(banks[5][:, j * 128:(j + 1) * 128], lhsT=hT[:, k, j * 128:(j + 1) * 128],
                                           rhs=W[:, k, V0:V0 + 128], start=(k == 0), stop=(k == 7))
                    return ins
                S.add("pe", vmm, reads=[("hT", j) for j in blks] + WALL, writes=[BK(5)])
                vb0 = 0 if mode == "meta" else b0
                S.add("act", lambda e: e.activation(out=vdst[:, vb0:vb0 + nblk, :, 0:64],
                                                    in_=banks[5][:, 0:N].rearrange("p (j g d) -> p j g d", j=nblk, g=2),
                                                    func=AF.Copy),
                      reads=[BK(5)], writes=[vres(j) for j in blks])
            jobs.append(job_v)

            def job_k(g):
                b = proj_chunk(KD0 + 128 * g, N)
                qknorm(b, N, 1, kdst[:, g, c0:c0 + N], [kres(j) for j in blks])

            if mode == "meta":
                for g in range(2):
                    jobs.append((lambda g: lambda: job_k(g))(g))
                return jobs

            def job_glu(j):
                if mode == "halo":
                    lo = 96 if b0 == 0 else 0
                    n_ = 32
                else:
                    lo, n_ = 0, N
                bg = proj_chunk(CG0 + 128 * j, n_, lo)
                t = nxt("th", 2)
                S.add("act", lambda e: e.activation(out=th[t][:, 0:n_], in_=banks[bg][:, 0:n_], func=AF.Tanh, scale=0.5),
                      reads=[BK(bg)], writes=[("th", t)])
                bv = proj_chunk(CV0 + 128 * j, n_, lo)
                S.add("dve", lambda e: e.scalar_tensor_tensor(out=uT[:, j, c0 + lo:c0 + lo + n_], in0=th[t][:, 0:n_],
                                                              scalar=1.0, in1=banks[bv][:, 0:n_], op0=ALU.add, op1=ALU.mult),
                      reads=[("th", t), BK(bv)], writes=[("uT", b0 + jj) for jj in blks])
                if j == 3:
                    for jj in blks:
                        u_ready.add((seg, b0 + jj))

            def job_ga(j):
                b = proj_chunk(GA0 + 128 * j, N)
                silu2(b, N, gaT[qslot][:, j, 0:N], [("gaT", qslot, jj) for jj in blks])

            def job_gc(j):
                b = proj_chunk(GC0 + 128 * j, N)
                silu2(b, N, gcT[qslot][:, j, 0:N], [("gcT", qslot)])

            def job_q(j):
                b = proj_chunk(Q0 + 128 * j, N)
                qknorm(b, N, 0, qT[qslot][:, 0:nblk, j, :], [("qT", qslot, jj) for jj in blks], view3=True)

            for j in range(4):
                jobs.append((lambda j: lambda: job_glu(j))(j))
            if mode == "full":
                for j in range(4):
                    jobs.append((lambda j: lambda: job_ga(j))(j))
                for j in range(4):
                    jobs.append((lambda j: lambda: job_gc(j))(j))
            for g in range(2):
                jobs.append((lambda g: lambda: job_k(g))(g))
            if mode == "full":
                for j in range(4):
                    jobs.append((lambda j: lambda: job_q(j))(j))
            return jobs

        def att_scores(seg, qslot, jq, bq):
            qc = jq * 128
            ptiles = {}
            for g in range(2):
                for ty, kb in enumerate((bq - 1, bq, bq + 1)):
                    sb_ = 6 + nxt("st2", 2)

                    def smm(e, g=g, ty=ty, kb=kb, sb_=sb_):
                        e.matmul(banks[sb_][:, 0:256], lhsT=kTd[0:64, g, kb * 128:(kb + 1) * 128],
                                 rhs=qT[qslot][0:64, jq, 2 * g:2 * g + 2, :], start=True, stop=False,
                                 skip_group_check=True)
                        e.matmul(banks[sb_][:, 0:512], lhsT=IDN[:], rhs=BIAS[:, ty, 4 * g:4 * g + 4, :],
                                 start=False, stop=False, skip_group_check=True)
                        return e.matmul(banks[sb_][:, 256:512], lhsT=kTd[64:128, g, kb * 128:(kb + 1) * 128],
                                        rhs=qT[qslot][64:128, jq, 2 * g:2 * g + 2, :], start=False, stop=True,
                                        skip_group_check=True)
                    S.add("pe", smm, reads=[("kTd", kb), ("qT", qslot, jq), "IDN", "BIAS"], writes=[BK(sb_)])
                    p = nxt("pt", NPT)
                    ptiles[(g, ty)] = p
                    if kb == 0 or kb == NB + 1:
                        col = 2 * seg + (0 if kb == 0 else 1)
                        S.add("act", (lambda sb_, p, col: lambda e: e.activation(
                            out=PT[p][:], in_=banks[sb_][:], func=AF.Exp, bias=hb[:, col:col + 1]))(sb_, p, col),
                            reads=[BK(sb_), "hb"], writes=[("PT", p)])
                    else:
                        S.add("act", (lambda sb_, p: lambda e: e.activation(out=PT[p][:], in_=banks[sb_][:],
                                                                             func=AF.Exp))(sb_, p),
                              reads=[BK(sb_)], writes=[("PT", p)])
            sbm = 6 + nxt("st2", 2)

            def mmm(e):
                ins = None
                for g in range(2):
                    for a in range(2):
                        ins = e.matmul(banks[sbm][32 * g:32 * g + 16, a * 256:(a + 1) * 256],
                                       lhsT=kTmZ[:, g, a, 32 * g:32 * g + 16],
                                       rhs=qT[qslot][:, jq, 2 * g:2 * g + 2, :], start=True, stop=True)
                return ins
            S.add("pe", mmm, reads=[("kTmZ",), ("qT", qslot, jq)], writes=[BK(sbm)])
            pm = nxt("ptm", 2)
            S.add("act", lambda e: e.activation(out=PTm[pm][0:48, :], in_=banks[sbm][0:48, :], func=AF.Exp),
                  reads=[BK(sbm)], writes=[("PTm", pm)])
            return dict(seg=seg, qslot=qslot, jq=jq, bq=bq, qc=qc, ptiles=ptiles, pm=pm)

        def att_pv(cx):
            qslot, jq, bq, qc, ptiles, pm = cx["qslot"], cx["jq"], cx["bq"], cx["qc"], cx["ptiles"], cx["pm"]
            for g in range(2):
                pvb = g

                def pv(e, g=g, pvb=pvb):
                    ins = None
                    for hl in range(4):
                        a, j = hl % 2, hl // 2
                        c_ = a * 256 + j * 128
                        for ty, kb in enumerate((bq - 1, bq, bq + 1)):
                            e.matmul(banks[pvb][:, hl * 65:hl * 65 + 65], lhsT=PT[ptiles[(g, ty)]][:, c_:c_ + 128],
                                     rhs=vaug[:, kb, g, 0:65], start=(ty == 0), stop=False)
                        ins = e.matmul(banks[pvb][:, hl * 65:hl * 65 + 65], lhsT=PTm[pm][32 * g:32 * g + 16, c_:c_ + 128],
                                       rhs=vm[32 * g:32 * g + 16, 0, g, 0:65], start=False, stop=True)
                    return ins
                S.add("pe", pv, reads=[("PT", ptiles[(g, ty)]) for ty in range(3)] + [("PTm", pm), ("vm",), ("vm1",),
                                                                                      ("vaug1",)]
                      + [("vaug", kb) for kb in (bq - 1, bq, bq + 1)], writes=[BK(pvb)])
            o = nxt("on", 2)
            for g in range(2):
                S.add("dve", (lambda g: lambda e: e.tensor_tensor(
                    out=den[o][:, 4 * g:4 * g + 4], in0=banks[g][:, 64:260:65], in1=esink[:, 4 * g:4 * g + 4],
                    op=ALU.add))(g), reads=[BK(g), "esink"], writes=[("den", o, g)])
            S.add("dve", lambda e: e.reciprocal(out=den[o][:], in_=den[o][:]),
                  reads=[("den", o, 0), ("den", o, 1)], writes=[("den", o, 0), ("den", o, 1)])
            for g in range(2):
                S.add("dve", (lambda g: lambda e: e.tensor_tensor(
                    out=On[o][:, 256 * g:256 * g + 256].rearrange("p (h d) -> p h d", h=4),
                    in0=banks[g][:, 0:260].rearrange("p (h d) -> p h d", h=4)[:, :, 0:64],
                    in1=den[o][:, 4 * g:4 * g + 4].unsqueeze(2).to_broadcast([128, 4, 64]), op=ALU.mult))(g),
                    reads=[BK(g), ("den", o, 0), ("den", o, 1)], writes=[("On", o, g)])
            cx["o"] = o

        def att_out(cx):
            qslot, jq, qc, o = cx["qslot"], cx["jq"], cx["qc"], cx["o"]
            otb = banks[4][:].bitcast(BF16)

            def otr(e):
                ins = None
                for j in range(4):
                    ins = e.transpose(out=otb[:, j * 128:(j + 1) * 128], in_=On[o][:, j * 128:(j + 1) * 128],
                                      identity=IDN[:])
                return ins
            S.add("pe", otr, reads=[("On", o, 0), ("On", o, 1), "IDN"], writes=[BK(4)])
            S.add("dve", lambda e: e.tensor_tensor(out=mixT[:, 0:4, qc:qc + 128],
                                                   in0=otb[:, 0:512].rearrange("p (j t) -> p j t", j=4),
                                                   in1=gaT[qslot][:, 0:4, qc:qc + 128], op=ALU.mult),
                  reads=[BK(4), ("gaT", qslot, jq)], writes=[("mixA", jq)])

        def conv_mm(b0, cc):
            ublk = [("uT", b) for b in range(b0, b0 + 5)]
            c0 = b0 * 128
            cb = 6 + nxt("cv", 2)

            def cmm(e, cc=cc, cb=cb):
                ins = None
                for k in range(16, CONV_K):
                    ins = e.matmul(banks[cb][:], lhsT=DG[:, cc, k - 16, :], rhs=uT[:, cc, c0 + k - 15:c0 + k - 15 + 512],
                                   start=(k == 16), stop=(k == CONV_K - 1))
                return ins
            S.add("pe", cmm, reads=ublk + ["DG"], writes=[BK(cb)])
            S.add("dve", (lambda cc, cb: lambda e: e.tensor_tensor(out=acc[:, cc, :], in0=banks[cb][:], in1=acc[:, cc, :],
                                                                    op=ALU.add))(cc, cb),
                  reads=[BK(cb), ("acc", cc)], writes=[("acc", cc)])
            S.add("act", (lambda cc: lambda e: e.activation(out=yb[:, cc, :], in_=acc[:, cc, :], func=AF.Identity,
                                                             bias=v4[:, cc:cc + 1]))(cc),
                  reads=[("acc", cc), "v4"], writes=[("yb", cc)])
            ys = nxt("ysq", 4)
            S.add("act", (lambda cc, ys: lambda e: e.activation(out=ysq[ys][:], in_=acc[:, cc, :], func=AF.Square,
                                                                 bias=v4[:, cc:cc + 1]))(cc, ys),
                  reads=[("acc", cc), "v4"], writes=[("ysq", ys)])
            pending_stat.append((cc, ys))

        mac_q = []
        in_drain = [False]

        def conv_queue(seg, b0):
            c0 = b0 * 128
            for k in range(0, 16):
                for cc in range(4):
                    lo = c0 + k - 15
                    need = {(seg, b) for b in range(lo // 128, (lo + 511) // 128 + 1)}
                    j = cc * CONV_K + k
                    rd = [("uT", b) for (_, b) in need] + [("acc", cc), "cwp"]
                    if k == 0:
                        fn = (lambda cc, lo, j: lambda e: e.tensor_scalar(out=acc[:, cc, :], in0=uT[:, cc, lo:lo + 512],
                                                                           scalar1=cwp[:, j:j + 1], scalar2=None,
                                                                           op0=ALU.mult))(cc, lo, j)
                    else:
                        fn = (lambda cc, lo, j: lambda e: e.scalar_tensor_tensor(out=acc[:, cc, :], in0=uT[:, cc, lo:lo + 512],
                                                                                  scalar=cwp[:, j:j + 1], in1=acc[:, cc, :],
                                                                                  op0=ALU.mult, op1=ALU.add))(cc, lo, j)
                    mac_q.append((need, fn, rd, cc))

        def drain(n):
            if in_drain[0]:
                return
            in_drain[0] = True
            while n > 0 and mac_q and mac_q[0][0] <= u_ready:
                need, fn, rd, cc = mac_q.pop(0)
                S.add("dve", fn, reads=rd, writes=[("acc", cc)])
                n -= 1
            in_drain[0] = False

        _add = S.add

        def add_hook(eng, fn, reads=(), writes=(), dma=False):
            op = _add(eng, fn, reads=reads, writes=writes, dma=dma)
            if eng == "dve" and not dma and not in_drain[0]:
                drain(1)
            return op
        S.add = add_hook

        pending_stat = []

        def conv_statmm():
            while pending_stat:
                cc, ys = pending_stat.pop(0)
                S.add("pe", (lambda cc: lambda e: e.matmul(banks[2][:], lhsT=ONL[:], rhs=yb[:, cc, :], start=(cc == 0),
                                                            stop=(cc == 3)))(cc),
                      reads=[("yb", cc), "ONL"], writes=[BK(2)])
                S.add("pe", (lambda cc, ys: lambda e: e.matmul(banks[3][:], lhsT=ONL[:], rhs=ysq[ys][:], start=(cc == 0),
                                                                stop=(cc == 3)))(cc, ys),
                      reads=[("ysq", ys), "ONL"], writes=[BK(3)])

        def conv_stats():
            S.add("act", lambda e: e.activation(out=m2[:], in_=banks[2][:], func=AF.Square), reads=[BK(2)], writes=["m2"])
            S.add("dve", lambda e: e.scalar_tensor_tensor(out=lnr[:], in0=banks[3][:], scalar=1e-5, in1=m2[:],
                                                          op0=ALU.add, op1=ALU.subtract),
                  reads=[BK(3), "m2"], writes=["lnr"])
            S.add("act", lambda e: e.activation(out=lnr[:], in_=lnr[:], func=AF.Ln), reads=["lnr"], writes=["lnr"])
            S.add("act", lambda e: e.activation(out=lnr[:], in_=lnr[:], func=AF.Exp, scale=-0.5), reads=["lnr"], writes=["lnr"])
            S.add("dve", lambda e: e.scalar_tensor_tensor(out=nmr[:], in0=banks[2][:], scalar=-1.0, in1=lnr[:],
                                                          op0=ALU.mult, op1=ALU.mult),
                  reads=[BK(2), "lnr", "m2"], writes=["m2"])

        def conv_epi(qslot, cc):
            if True:
                S.add("dve", (lambda cc: lambda e: e.tensor_tensor(out=t2[:], in0=yb[:, cc, :], in1=lnr[:],
                                                                    op=ALU.mult))(cc),
                      reads=[("yb", cc), "lnr"], writes=["t2"])
                S.add("dve", lambda e: e.tensor_tensor(out=t2[:], in0=t2[:], in1=nmr[:], op=ALU.add),
                      reads=["t2", "m2"], writes=["t2"])
                t = nxt("th", 2)
                S.add("act", (lambda cc: lambda e: e.activation(out=ta[:], in_=t2[:], func=AF.Identity,
                                                                 scale=v4[:, 4 + cc:5 + cc], bias=v4[:, 8 + cc:9 + cc]))(cc),
                      reads=["t2", "v4"], writes=["ta"])
                S.add("act", (lambda cc, t: lambda e: e.activation(out=th[t][:], in_=t2[:], func=AF.Tanh,
                                                                    scale=v4h[:, cc:cc + 1], bias=v4h[:, 4 + cc:5 + cc]))(cc, t),
                      reads=["t2", "v4h"], writes=[("th", t)])
                S.add("dve", (lambda t: lambda e: e.scalar_tensor_tensor(out=ta[:], in0=th[t][:], scalar=1.0, in1=ta[:],
                                                                          op0=ALU.add, op1=ALU.mult))(t),
                      reads=[("th", t), "ta"], writes=["ta"])
                S.add("dve", (lambda cc: lambda e: e.tensor_tensor(out=yb[:, cc, :], in0=ta[:],
                                                                    in1=gcT[qslot][:, cc, :], op=ALU.mult))(cc),
                      reads=["ta", ("gcT", qslot)], writes=[("yb", cc)])

        def outproj_block(seg, jq, bq):
            qc = jq * 128
            r = nxt("xt", 3)
            dma(xres[r][:], xe[seg, bq * 128:(bq + 1) * 128, :], writes=[("xt", r)])
            for half in range(2):
                ob = 6 + half

                def omm(e, half=half, ob=ob):
                    ins = None
                    for k in range(8):
                        lt = mixT[:, k, qc:qc + 128] if k < 4 else yb[:, k - 4, qc:qc + 128]
                        ins = e.matmul(banks[ob][:], lhsT=lt,
                                       rhs=WO[:, k, half * 512:(half + 1) * 512], start=(k == 0), stop=(k == 7))
                    return ins
                S.add("pe", omm, reads=[("mixA", jq)] + [("yb", cc) for cc in range(4)] + WOALL, writes=[BK(ob)])
                S.add("dve", (lambda half, ob: lambda e: e.tensor_tensor(
                    out=xres[r][:, half * 512:(half + 1) * 512], in0=banks[ob][:],
                    in1=xres[r][:, half * 512:(half + 1) * 512], op=ALU.add))(half, ob),
                    reads=[BK(ob), ("xt", r)], writes=[("xt", r)])
            S.add("pool", lambda e: e.dma_start(out=y[seg, (bq - 1) * 128:bq * 128, :], in_=xres[r][:]),
                  reads=[("xt", r)], dma=True)

        if STOP >= 1:
            norm_T(norm_pre(lambda j: xmeta, 1))
            for jb in projA_jobs(0, 1, "meta"):
                jb()
        S.add("dve", lambda e: e.memset(kTmZ[:], 0.0), writes=[("kTmZ",)])
        for g in range(2):
            for a in range(2):
                S.add("dve", (lambda g, a: lambda e: e.tensor_copy(out=kTmZ[64 * a:64 * a + 64, g, a, :],
                                                                   in_=kTm[64 * a:64 * a + 64, g, 0:48]))(g, a),
                      reads=[("kTm",)], writes=[("kTmZ",)])

        A_list = []
        for seg in range(NSEG if STOP >= 2 else 0):
            A_list.append((seg, 0, "halo", 0, 1))
            for i in range(NST):
                A_list.append((seg, i + 1, "full", 1 + 4 * i, 4))
            A_list.append((seg, NST + 1, "halo", NB + 1, 1))
        B_list = [(seg, i) for seg in range(NSEG if STOP >= 2 else 0) for i in range(NST)]
        pa = [0]
        slot_of = {}

        def src_of(entry):
            seg, k, mode, b0, nblk = entry
            return (lambda seg, b0: lambda j: xe[seg, (b0 + j) * 128:(b0 + j + 1) * 128, :])(seg, b0)

        pre_slots = {}

        def do_pre(idx):
            if idx < len(A_list) and idx not in pre_slots:
                pre_slots[idx] = norm_pre(src_of(A_list[idx]), A_list[idx][4])

        def A_jobs(idx):
            seg, k, mode, b0, nblk = A_list[idx]
            qs = None
            if mode == "full":
                qs = nxt("qslot", 2)
                slot_of[(seg, k)] = qs
            pj = projA_jobs(b0, nblk, mode, qs, seg)

            def first():
                do_pre(idx)
                norm_T(pre_slots[idx])
            return [first, pj[0]] + pj[1:5] + [lambda: do_pre(idx + 1)] + pj[5:]

        def need_idx(seg, i):
            return seg * (NST + 2) + i + 2

        def first_slot(entry, seg, b0):
            eseg, k, mode, eb0, nblk = entry
            if eseg == seg:
                return 0
            last = -1
            for jq in range(4):
                rd = range(b0 + jq - 1, b0 + jq + 2)
                if any(eb0 <= r < eb0 + nblk for r in rd):
                    last = jq
            return last + 1

        for bi, (seg, i) in enumerate(B_list):
            while pa[0] <= need_idx(seg, i):
                for jb in A_jobs(pa[0]):
                    jb()
                pa[0] += 1
            qslot = slot_of[(seg, i + 1)]
            b0 = 1 + 4 * i
            slots = [[] for _ in range(5)]
            if bi + 1 < len(B_list):
                nseg, ni = B_list[bi + 1]
                tgt = need_idx(nseg, ni)
                pend = []
                while pa[0] <= tgt:
                    pend.append(pa[0])
                    pa[0] += 1
                queue = []
                post = False
                for idx in pend:
                    fs = first_slot(A_list[idx], seg, b0)
                    post = post or fs >= 4
                    for jb in A_jobs(idx):
                        queue.append((4 if post else fs, jb))
                n_in = sum(1 for fs, _ in queue if fs < 4)
                quota = max(1, -(-n_in // 4))
                cur, cnt_ = 0, 0
                for fs, jb in queue:
                    if fs >= 4:
                        slots[4].append(jb)
                        continue
                    if fs > cur:
                        cur, cnt_ = fs, 0
                    if cnt_ >= quota and cur < 3:
                        cur, cnt_ = cur + 1, 0
                    slots[cur].append(jb)
                    cnt_ += 1
            if bi == 0:
                conv_queue(seg, b0)
                drain(10 ** 9)
                assert not mac_q
            for cc in range(4):
                conv_mm(b0, cc)
                if cc >= 1:
                    pass
                if cc >= 1:
                    last = pending_stat.pop()
                    conv_statmm()
                    pending_stat.append(last)
            if bi + 1 < len(B_list):
                nseg, ni = B_list[bi + 1]
                conv_queue(nseg, 1 + 4 * ni)
            cxs = []
            for jq in range(4):
                cx = att_scores(seg, qslot, jq, b0 + jq)
                cxs.append(cx)
                if jq == 0:
                    conv_statmm()
                    conv_stats()
                if jq >= 1:
                    att_out(cxs[jq - 1])
                if jq == 3:
                    conv_epi(qslot, jq)
                for jb in slots[jq]:
                    jb()
                if jq < 3:
                    conv_epi(qslot, jq)
                att_pv(cx)
            for jq in range(2):
                outproj_block(seg, jq, b0 + jq)
            att_out(cxs[3])
            for jb in slots[4]:
                jb()
            for jq in range(2, 4):
                outproj_block(seg, jq, b0 + jq)
            drain(10 ** 9)
            assert not mac_q, "conv MACs left whose u blocks were never produced"

        S.finalize(sems, dsems)
        with nc.Block() as block:
            @block.tensor
            def _(e):
                S.emit("pe", e)

            @block.scalar
            def _(e):
                S.emit("act", e)

            @block.vector
            def _(e):
                S.emit("dve", e)

            @block.gpsimd
            def _(e):
                S.emit("pool", e)

            @block.sync
            def _(e):
                S.emit("sp", e)
                S.final_waits(e)
    return nc


def _const_tables():
    i = np.arange(128)
    jj, ii = np.meshgrid(i, i, indexing="ij")
    slopes = np.exp2(-8.0 * np.arange(1, N_HEADS + 1) / N_HEADS).astype(np.float32)
    tab = np.empty((128, 3, N_HEADS, 128), np.float32)
    dl, vl = 128 + ii - jj, jj >= ii
    dm = np.abs(ii - jj)
    dr, vr = 128 + jj - ii, jj <= ii
    order = [0, 2, 1, 3, 4, 6, 5, 7]
    for sl, h in enumerate(order):
        tab[:, 0, sl, :] = np.where(vl, -slopes[h] * dl, NEG)
        tab[:, 1, sl, :] = -slopes[h] * dm
        tab[:, 2, sl, :] = np.where(vr, -slopes[h] * dr, NEG)
    bf = ml_dtypes.bfloat16
    ident = np.eye(128, dtype=np.float32).astype(bf)
    bd = np.kron(np.eye(2, dtype=np.float32), np.full((64, 64), 1.0 / 64, np.float32)).astype(bf)
    onesln = np.full((128, 128), 1.0 / 512, np.float32).astype(bf)
    return tab.reshape(128, -1).astype(bf), ident, bd, onesln


def _segment(xseq, meta, t0, n):
    S_ = xseq.shape[0]
    if t0 > 0:
        left, hl = xseq[t0 - 128:t0], 0.0
    else:
        left = np.concatenate([np.zeros((128 - N_META, D_MODEL), np.float32), meta], 0)
        hl = NEG
    if t0 + n < S_:
        right, hr = xseq[t0 + n:t0 + n + 128], 0.0
    else:
        right, hr = np.zeros((128, D_MODEL), np.float32), NEG
    return np.concatenate([left, xseq[t0:t0 + n], right], 0), hl, hr


def _core_inputs(segs, meta, weights, consts):
    xs, hbv = [], []
    for (xseq, t0, n) in segs:
        x_, hl, hr = _segment(xseq, meta, t0, n)
        xs.append(x_)
        hbv += [hl, hr]
    xmeta = np.zeros((128, D_MODEL), np.float32)
    xmeta[0:N_META] = meta
    xmeta[32:32 + N_META] = meta
    d = dict(weights)
    d.update(consts)
    d["xe"] = np.ascontiguousarray(np.stack(xs, 0))
    d["xmeta"] = xmeta
    d["hb"] = np.ascontiguousarray(np.broadcast_to(np.asarray(hbv, np.float32)[None, :], (128, len(hbv))))
    return d


def _pack_weights(norm_w, w_in, q_norm_w, k_norm_w, sink_logits, conv_w, conv_b, conv_ln_w, conv_ln_b, w_out):
    f = np.float32
    return {
        "w_in": np.ascontiguousarray(w_in[0], f),
        "w_out": np.ascontiguousarray(w_out[0], f),
        "normw_pk": np.ascontiguousarray(norm_w[0].reshape(8, 128).T, f),
        "qkw": np.ascontiguousarray(np.stack([np.tile(q_norm_w[0], 2), np.tile(k_norm_w[0], 2)], 1), f),
        "convw_pk": np.ascontiguousarray(conv_w[0].reshape(CONV_K, 4, 128).transpose(2, 1, 0).reshape(128, 4 * CONV_K), f),
        "vec4": np.ascontiguousarray(np.concatenate([conv_b[0].reshape(4, 128).T, conv_ln_w[0].reshape(4, 128).T,
                                                     conv_ln_b[0].reshape(4, 128).T], 1), f),
        "sink_b": np.ascontiguousarray(np.broadcast_to(sink_logits[0][None, :], (128, N_HEADS)), f),
    }


_NC_CACHE = {}


def kernel(x_prompt, x_sample, meta_tokens, norm_w, w_in, q_norm_w, k_norm_w, sink_logits,
           conv_w, conv_b, conv_ln_w, conv_ln_b, w_out):
    NB, NSEG, NCORE = 8, 8, 8
    n = NB * 128
    xp = np.asarray(x_prompt, np.float32)
    xs = np.asarray(x_sample, np.float32)
    meta = np.asarray(meta_tokens, np.float32)
    weights = _pack_weights(*[np.asarray(a, np.float32) for a in
                              (norm_w, w_in, q_norm_w, k_norm_w, sink_logits, conv_w, conv_b, conv_ln_w, conv_ln_b, w_out)])
    tab, ident, bd, onesln = _const_tables()
    consts = {"bias_c": tab, "ident_c": ident, "bd_c": bd, "onesln_c": onesln}
    in_maps, place = [], []
    for c in range(NCORE):
        segs, pl = [], []
        for sq_ in (2 * c, 2 * c + 1):
            for t0 in (0, n):
                segs.append((xp[sq_], t0, n))
                pl.append((0, sq_, t0))
        sseq, base = c // 4, (c % 4) * 4 * n
        for q in range(4):
            segs.append((xs[sseq], base + q * n, n))
            pl.append((1, sseq, base + q * n))
        in_maps.append(_core_inputs(segs, meta, weights, consts))
        place.append(pl)
    key = (NSEG, NB)
    if key not in _NC_CACHE:
        _NC_CACHE[key] = _build(NSEG, NB)
    res = run_bass_kernel_spmd(_NC_CACHE[key], in_maps, core_ids=list(range(NCORE)))
    yp = np.empty(xp.shape, np.float32)
    ys = np.empty(xs.shape, np.float32)
    for c in range(NCORE):
        yc = res.results[c]["y"]
        for s_, (which, sq_, t0) in enumerate(place[c]):
            (yp if which == 0 else ys)[sq_, t0:t0 + n] = yc[s_]
    return (yp, ys)
```

```python
import math
from contextlib import ExitStack

import numpy as np
import ml_dtypes
import concourse.bass as bass
import concourse.mybir as mybir
from concourse.bass_utils import run_bass_kernel_spmd

F32 = mybir.dt.float32
BF16 = mybir.dt.bfloat16
AF = mybir.ActivationFunctionType
ALU = mybir.AluOpType

D_MODEL = 1024
N_META = 16
N_HEADS = 8
CONV_K = 31
NEG = -30000.0

Q0, KD0, KD1, V0, GA0, CV0, CG0, GC0, WCOLS = 0, 512, 640, 768, 896, 1408, 1920, 2432, 2944


class _Op:
    __slots__ = ("eng", "fn", "deps", "signal", "sem", "val", "dma", "waits", "prev_val")

    def __init__(self, eng, fn, dma):
        self.eng = eng
        self.fn = fn
        self.dma = dma
        self.deps = set()
        self.signal = False
        self.sem = None
        self.val = 0
        self.prev_val = 0
        self.waits = []


class Sched:
    ENGS = ("pe", "act", "dve", "pool", "sp")

    def __init__(self):
        self.ops = {e: [] for e in self.ENGS}
        self.last_writer = {}
        self.readers = {}

    def add(self, eng, fn, reads=(), writes=(), dma=False):
        op = _Op(eng, fn, dma)
        deps = op.deps
        for r in reads:
            w = self.last_writer.get(r)
            if w is not None:
                deps.add(w)
        for w_ in writes:
            w = self.last_writer.get(w_)
            if w is not None:
                deps.add(w)
            for rd in self.readers.get(w_, ()):
                deps.add(rd)
        for r in reads:
            self.readers.setdefault(r, []).append(op)
        for w_ in writes:
            self.last_writer[w_] = op
            self.readers[w_] = []
        deps.discard(op)
        if eng == "pe" and not dma:
            op.deps = {d for d in deps if not (d.eng == "pe" and not d.dma)}
        for d in op.deps:
            d.signal = True
        self.ops[eng].append(op)
        return op

    def finalize(self, eng_sems, dma_sems):
        for e in self.ENGS:
            cnt = 0
            dcount = {}
            k = 0
            pool = dma_sems.get(e, [])
            for op in self.ops[e]:
                if op.dma:
                    s = pool[k % len(pool)]
                    k += 1
                    prev = dcount.get(id(s), 0)
                    op.sem = s
                    op.prev_val = prev
                    op.val = prev + 16
                    dcount[id(s)] = op.val
                elif op.signal:
                    cnt += 1
                    op.sem = eng_sems[e]
                    op.val = cnt
        for e in self.ENGS:
            waited = {}
            for op in self.ops[e]:
                need = {}
                for d in op.deps:
                    key = id(d.sem)
                    if d.val > need.get(key, (None, 0))[1]:
                        need[key] = (d.sem, d.val)
                if op.dma and op.prev_val > 0:
                    key = id(op.sem)
                    if op.prev_val > need.get(key, (None, 0))[1]:
                        need[key] = (op.sem, op.prev_val)
                for key, (s, v) in need.items():
                    if v > waited.get(key, 0):
                        waited[key] = v
                        op.waits.append((s, v))

    def emit(self, eng_name, e):
        for op in self.ops[eng_name]:
            for (s, v) in op.waits:
                e.wait_ge(s, v)
            ins = op.fn(e)
            if op.dma:
                ins.then_inc(op.sem, 16)
            elif op.signal:
                ins.then_inc(op.sem, 1)

    def final_waits(self, e):
        last = {}
        for q in self.ENGS:
            for op in self.ops[q]:
                if op.dma:
                    last[id(op.sem)] = (op.sem, op.val)
        for s, v in last.values():
            e.wait_ge(s, v)


def _build(NSEG, NB, STOP=9):
    assert NB % 4 == 0
    NST = NB // 4
    NBX = NB + 2
    TX = NBX * 128
    nc = bass.Bass("TRN2", target_bir_lowering=False)

    def din(name, shape, dt=F32):
        return nc.dram_tensor(name, shape, dt, kind="ExternalInput").ap()

    xe = din("xe", [NSEG, TX, D_MODEL])
    xmeta = din("xmeta", [128, D_MODEL])
    w_in = din("w_in", [D_MODEL, 2816])
    w_out = din("w_out", [D_MODEL, D_MODEL])
    normw_pk = din("normw_pk", [128, 8])
    qkw = din("qkw", [128, 2])
    convw_pk = din("convw_pk", [128, 4 * CONV_K])
    vec4 = din("vec4", [128, 12])
    sink_b = din("sink_b", [128, 8])
    hb_in = din("hb", [128, 2 * NSEG])
    ident_c = din("ident_c", [128, 128], BF16)
    bd_c = din("bd_c", [128, 128], BF16)
    onesln_c = din("onesln_c", [128, 128], BF16)
    bias_c = din("bias_c", [128, 3 * 8 * 128], BF16)
    y = nc.dram_tensor("y", [NSEG, NB * 128, D_MODEL], F32, kind="ExternalOutput").ap()

    S = Sched()
    with ExitStack() as st:
        E = st.enter_context

        def sb(name, shape, dt=F32):
            return E(nc.sbuf_tensor("sb_" + name, shape, dt))

        W = sb("W", [128, 8, WCOLS], BF16)
        WO = sb("WO", [128, 8, D_MODEL], BF16)
        DG = sb("DG", [128, 4, 15, 128], BF16)
        BIAS = sb("BIAS", [128, 3, 8, 128], BF16)
        IDN = sb("IDN", [128, 128], BF16)
        BD = sb("BD", [128, 128], BF16)
        ONL = sb("ONL", [128, 128], BF16)
        nwp = sb("nwp", [128, 8])
        qkv = sb("qkv", [128, 2])
        qks = sb("qks", [128, 2])
        cwp = sb("cwp", [128, 4 * CONV_K])
        v4 = sb("v4", [128, 12])
        v4h = sb("v4h", [128, 8])
        snk = sb("snk", [128, 8])
        esink = sb("esink", [128, 8])
        hb = sb("hb", [128, 2 * NSEG])
        mhalf = sb("mhalf", [128, 1])
        epsc = sb("epsc", [128, 1])
        kTd = sb("kTd", [128, 2, TX], BF16)
        vaug = sb("vaug", [128, NBX, 2, 66], BF16)
        uT = sb("uT", [128, 4, TX], BF16)
        kTm = sb("kTm", [128, 2, 128], BF16)
        vm = sb("vm", [128, 1, 2, 66], BF16)
        kTmZ = sb("kTmZ", [128, 2, 2, 48], BF16)
        hT = sb("hT", [128, 8, 512], BF16)
        qT = [sb(f"qT{i}", [128, 4, 4, 128], BF16) for i in range(2)]
        gaT = [sb(f"gaT{i}", [128, 4, 512], BF16) for i in range(2)]
        gcT = [sb(f"gcT{i}", [128, 4, 512], BF16) for i in range(2)]
        mixT = sb("mixT", [128, 4, 512], BF16)
        xt = [sb(f"xt{i}", [128, D_MODEL]) for i in range(3)]
        xres = xt
        htm = [sb(f"htm{i}", [128, D_MODEL], BF16) for i in range(4)]
        ss = [sb(f"ss{i}", [128, 1]) for i in range(4)]
        sq = [sb(f"sq{i}", [128, 512], BF16) for i in range(2)]
        tq = [sb(f"tq{i}", [128, 512]) for i in range(2)]
        th = [sb(f"th{i}", [128, 512]) for i in range(2)]
        NPT = 8
        PT = [sb(f"PT{i}", [128, 512], BF16) for i in range(NPT)]
        PTm = [sb(f"PTm{i}", [128, 512], BF16) for i in range(2)]
        On = [sb(f"On{i}", [128, 512], BF16) for i in range(2)]
        den = [sb(f"den{i}", [128, 8]) for i in range(2)]
        yb = sb("yb", [128, 4, 512], BF16)
        acc = sb("acc", [128, 4, 512])
        ysq = [sb(f"ysq{i}", [128, 512], BF16) for i in range(4)]
        lnr = sb("lnr", [128, 512])
        m2 = sb("m2", [128, 512])
        nmr = m2
        t2 = sb("t2", [128, 512])
        ta = sb("ta", [128, 512])

        banks = [E(nc.psum_tensor(f"bank{i}", [128, 512], F32)) for i in range(8)]
        sems = {e: E(nc.semaphore("s_" + e)) for e in Sched.ENGS}
        dsems = {"sp": [E(nc.semaphore(f"d{i}")) for i in range(24)],
                 "pool": [E(nc.semaphore(f"dp{i}")) for i in range(8)]}

        def BK(i):
            return ("bank", i)

        cnt = {"htm": 0, "xt": 0, "xres": 0, "sq": 0, "th": 0, "pt": 0, "ptm": 0, "on": 0, "ysq": 0, "proj": 0,
               "st2": 0, "cv": 0, "qslot": 0}

        def nxt(k, n):
            v = cnt[k] % n
            cnt[k] += 1
            return v

        def dma(out_ap, in_ap, reads=(), writes=()):
            S.add("sp", lambda e: e.dma_start(out=out_ap, in_=in_ap), reads=reads, writes=writes, dma=True)

        dma(nwp[:], normw_pk, writes=["nwp"])
        dma(qkv[:], qkw, writes=["qkv"])
        dma(cwp[:], convw_pk, writes=["cwp"])
        dma(v4[:], vec4, writes=["v4"])
        dma(snk[:], sink_b, writes=["snk"])
        dma(hb[:], hb_in, writes=["hb"])
        dma(IDN[:], ident_c, writes=["IDN"])
        dma(BD[:], bd_c, writes=["BD"])
        dma(ONL[:], onesln_c, writes=["ONL"])
        dma(BIAS[:].rearrange("p a h q -> p (a h q)"), bias_c, writes=["BIAS"])

        S.add("pool", lambda e: e.memset(mhalf[:], -0.5), writes=["mhalf"])
        S.add("pool", lambda e: e.memset(epsc[:], 1e-6), writes=["epsc"])
        S.add("pool", lambda e: e.memset(vaug[:, :, :, 64:66], 1.0), writes=[("vaug1",)])
        S.add("pool", lambda e: e.memset(vm[:, :, :, 64:66], 1.0), writes=[("vm1",)])
        for i in range(8):
            S.add("dve", (lambda i: lambda e: e.memset(banks[i][:], 0.0))(i), writes=[BK(i)])
        S.add("dve", lambda e: e.tensor_scalar(out=qks[:, 0:1], in0=qkv[:, 0:1], scalar1=0.125, scalar2=None,
                                               op0=ALU.mult), reads=["qkv"], writes=["qks0"])
        S.add("dve", lambda e: e.tensor_copy(out=qks[:, 1:2], in_=qkv[:, 1:2]), reads=["qkv"], writes=["qks1"])
        S.add("dve", lambda e: e.tensor_scalar(out=cwp[:], in0=cwp[:], scalar1=0.5, scalar2=None, op0=ALU.mult),
              reads=["cwp"], writes=["cwp"])
        S.add("dve", lambda e: e.tensor_scalar(out=v4h[:], in0=v4[:, 4:12], scalar1=0.5, scalar2=None,
                                               op0=ALU.mult), reads=["v4"], writes=["v4h"])
        S.add("act", lambda e: e.activation(out=esink[:], in_=snk[:], func=AF.Exp), reads=["snk"], writes=["esink"])

        def stage_view(slot):
            return xt[slot][:].rearrange("p (k n) -> p k n", k=8)

        nwb = nwp[:].unsqueeze(2).to_broadcast([128, 8, 128])

        def wpiece(c0):
            slot = nxt("xt", 3)
            sv = stage_view(slot)
            dma(sv, w_in[:, c0:c0 + 128].rearrange("(k p) n -> p k n", p=128), writes=[("xt", slot)])
            if c0 < 512:
                dsts = [(Q0 + c0, 0, 128)]
            elif c0 == 512:
                dsts = [(KD0, 0, 64), (KD0 + 64, 0, 64), (KD1, 64, 64), (KD1 + 64, 64, 64)]
            elif c0 == 640:
                dsts = [(V0, 0, 128)]
            elif c0 < 1280:
                dsts = [(GA0 + c0 - 768, 0, 128)]
            elif c0 < 1792:
                dsts = [(CV0 + c0 - 1280, 0, 128)]
            elif c0 < 2304:
                dsts = [(CG0 + c0 - 1792, 0, 128)]
            else:
                dsts = [(GC0 + c0 - 2304, 0, 128)]
            for (d0, s0, n) in dsts:
                S.add("dve", (lambda d0, s0, n, sv: lambda e: e.tensor_tensor(
                    out=W[:, :, d0:d0 + n], in0=sv[:, :, s0:s0 + n],
                    in1=nwp[:].unsqueeze(2).to_broadcast([128, 8, n]), op=ALU.mult))(d0, s0, n, sv),
                    reads=[("xt", slot), "nwp"], writes=[("W", d0)])

        for c0 in range(0, 2816, 128):
            wpiece(c0)

        def wopiece(c0):
            slot = nxt("xt", 3)
            sv = stage_view(slot)
            dma(sv, w_out[:, c0:c0 + 128].rearrange("(k p) n -> p k n", p=128), writes=[("xt", slot)])
            S.add("dve", lambda e: e.tensor_scalar(out=WO[:, 0:4, c0:c0 + 128], in0=sv[:, 0:4, :], scalar1=0.5,
                                                   scalar2=None, op0=ALU.mult),
                  reads=[("xt", slot)], writes=[("WO", c0, 0)])
            S.add("dve", lambda e: e.tensor_scalar(out=WO[:, 4:8, c0:c0 + 128], in0=sv[:, 4:8, :], scalar1=0.25,
                                                   scalar2=None, op0=ALU.mult),
                  reads=[("xt", slot)], writes=[("WO", c0, 1)])

        for c0 in range(0, D_MODEL, 128):
            wopiece(c0)

        def dgs(e):
            ins = None
            for cc in range(4):
                for k in range(16, CONV_K):
                    j = cc * CONV_K + k
                    ins = e.tensor_scalar(out=DG[:, cc, k - 16, :], in0=IDN[:], scalar1=cwp[:, j:j + 1], scalar2=None,
                                          op0=ALU.mult)
            return ins
        S.add("dve", dgs, reads=["IDN", "cwp"], writes=["DG"])
        WALL = [("W", d) for d in ([Q0 + i for i in range(0, 512, 128)] + [KD0, KD0 + 64, KD1, KD1 + 64, V0]
                                   + [b + i for b in (GA0, CV0, CG0, GC0) for i in range(0, 512, 128)])]
        WOALL = [("WO", c0, h) for c0 in range(0, D_MODEL, 128) for h in range(2)]

        def norm_pre(src, nblk):
            hs = []
            for j in range(nblk):
                slot = nxt("xt", 3)
                h_ = nxt("htm", 4)
                hs.append(h_)
                dma(xt[slot][:], src(j), writes=[("xt", slot)])
                S.add("act", (lambda slot, h_: lambda e: e.activation(out=htm[h_][:], in_=xt[slot][:], func=AF.Square,
                                                                       accum_out=ss[h_][:]))(slot, h_),
                      reads=[("xt", slot)], writes=[("htm", h_), ("ss", h_)])
                S.add("dve", (lambda h_: lambda e: e.tensor_scalar(out=ss[h_][:], in0=ss[h_][:], scalar1=1.0 / D_MODEL,
                                                                    scalar2=1e-6, op0=ALU.mult, op1=ALU.add))(h_),
                      reads=[("ss", h_)], writes=[("ss", h_)])
                S.add("pool", (lambda h_: lambda e: e.tensor_tensor(out=ss[h_][:], in0=ss[h_][:], in1=mhalf[:, 0:1],
                                                                     op=ALU.pow))(h_),
                      reads=[("ss", h_), "mhalf"], writes=[("ss", h_)])
                S.add("dve", (lambda slot, h_: lambda e: e.tensor_scalar(out=htm[h_][:], in0=xt[slot][:], scalar1=ss[h_][:],
                                                                          scalar2=None, op0=ALU.mult))(slot, h_),
                      reads=[("xt", slot), ("ss", h_)], writes=[("htm", h_)])
            return hs

        def norm_T(hs):
            for j, h_ in enumerate(hs):
                tb = 4 + (j % 2)
                trb = banks[tb][:].bitcast(BF16)

                def tr(e, h_=h_, trb=trb):
                    ins = None
                    for k in range(8):
                        ins = e.transpose(out=trb[:, k * 128:(k + 1) * 128], in_=htm[h_][:, k * 128:(k + 1) * 128],
                                          identity=IDN[:])
                    return ins
                S.add("pe", tr, reads=[("htm", h_), "IDN"], writes=[BK(tb)])
                S.add("act", (lambda j, trb: lambda e: e.activation(out=hT[:, :, j * 128:(j + 1) * 128],
                                                                     in_=trb.rearrange("p (k t) -> p k t", k=8),
                                                                     func=AF.Copy))(j, trb),
                      reads=[BK(tb)], writes=[("hT", j)])

        def proj_chunk(wcol, N, lo=0):
            b = nxt("proj", 3)

            def mm(e):
                ins = None
                for k in range(8):
                    ins = e.matmul(banks[b][:, 0:N], lhsT=W[:, k, wcol:wcol + 128], rhs=hT[:, k, lo:lo + N],
                                   start=(k == 0), stop=(k == 7))
                return ins
            S.add("pe", mm, reads=[("hT", j) for j in range((lo + N + 127) // 128)] + WALL, writes=[BK(b)])
            return b

        def silu2(b, N, out_ap, out_res):
            t = nxt("th", 2)
            S.add("act", lambda e: e.activation(out=th[t][:, 0:N], in_=banks[b][:, 0:N], func=AF.Tanh, scale=0.5),
                  reads=[BK(b)], writes=[("th", t)])
            S.add("dve", lambda e: e.scalar_tensor_tensor(out=out_ap, in0=th[t][:, 0:N], scalar=1.0,
                                                          in1=banks[b][:, 0:N], op0=ALU.add, op1=ALU.mult),
                  reads=[("th", t), BK(b)], writes=out_res)

        def qknorm(b, N, scol, out_ap, out_res, view3=False):
            s_ = nxt("sq", 2)
            stb = 3 if s_ == 0 else 5
            S.add("act", lambda e: e.activation(out=sq[s_][:, 0:N], in_=banks[b][:, 0:N], func=AF.Square),
                  reads=[BK(b)], writes=[("sq", s_)])
            S.add("pe", lambda e: e.matmul(banks[stb][:, 0:N], lhsT=BD[:], rhs=sq[s_][:, 0:N], start=True, stop=True),
                  reads=[("sq", s_), "BD"], writes=[BK(stb)])
            S.add("act", lambda e: e.activation(out=tq[s_][:, 0:N], in_=banks[stb][:, 0:N], func=AF.Ln, bias=epsc[:, 0:1]),
                  reads=[BK(stb), "epsc"], writes=[("tq", s_)])
            S.add("act", lambda e: e.activation(out=tq[s_][:, 0:N], in_=tq[s_][:, 0:N], func=AF.Exp, scale=-0.5),
                  reads=[("tq", s_)], writes=[("tq", s_)])
            v3 = (lambda ap: ap.rearrange("p (j t) -> p j t", t=128)) if view3 else (lambda ap: ap)
            S.add("dve", lambda e: e.scalar_tensor_tensor(out=out_ap, in0=v3(banks[b][:, 0:N]), scalar=qks[:, scol:scol + 1],
                                                          in1=v3(tq[s_][:, 0:N]), op0=ALU.mult, op1=ALU.mult),
                  reads=[BK(b), ("tq", s_), "qks0", "qks1"], writes=out_res)

        u_ready = set()

        def projA_jobs(b0, nblk, mode, qslot=None, seg=None):
            N = nblk * 128
            if mode == "meta":
                kdst, vdst, kres, vres = kTm, vm, lambda j: ("kTm",), lambda j: ("vm",)
                c0 = 0
            else:
                kdst, vdst, kres, vres = kTd, vaug, lambda j: ("kTd", b0 + j), lambda j: ("vaug", b0 + j)
                c0 = b0 * 128
            blks = range(nblk)
            jobs = []

            def job_v():
                def vmm(e):
                    ins = None
                    for j in range(nblk):
                        for k in range(8):
                            ins = e.matmul(banks[5][:, j * 128:(j + 1) * 128], lhsT=hT[:, k, j * 128:(j + 1) * 128],
                                           rhs=W[:, k, V0:V0 + 128], start=(k == 0), stop=(k == 7))
                    return ins
                S.add("pe", vmm, reads=[("hT", j) for j in blks] + WALL, writes=[BK(5)])
                vb0 = 0 if mode == "meta" else b0
                S.add("act", lambda e: e.activation(out=vdst[:, vb0:vb0 + nblk, :, 0:64],
                                                    in_=banks[5][:, 0:N].rearrange("p (j g d) -> p j g d", j=nblk, g=2),
                                                    func=AF.Copy),
                      reads=[BK(5)], writes=[vres(j) for j in blks])
            jobs.append(job_v)

            def job_k(g):
                b = proj_chunk(KD0 + 128 * g, N)
                qknorm(b, N, 1, kdst[:, g, c0:c0 + N], [kres(j) for j in blks])

            if mode == "meta":
                for g in range(2):
                    jobs.append((lambda g: lambda: job_k(g))(g))
                return jobs

            def job_glu(j):
                if mode == "halo":
                    lo = 96 if b0 == 0 else 0
                    n_ = 32
                else:
                    lo, n_ = 0, N
                bg = proj_chunk(CG0 + 128 * j, n_, lo)
                t = nxt("th", 2)
                S.add("act", lambda e: e.activation(out=th[t][:, 0:n_], in_=banks[bg][:, 0:n_], func=AF.Tanh, scale=0.5),
                      reads=[BK(bg)], writes=[("th", t)])
                bv = proj_chunk(CV0 + 128 * j, n_, lo)
                S.add("dve", lambda e: e.scalar_tensor_tensor(out=uT[:, j, c0 + lo:c0 + lo + n_], in0=th[t][:, 0:n_],
                                                              scalar=1.0, in1=banks[bv][:, 0:n_], op0=ALU.add, op1=ALU.mult),
                      reads=[("th", t), BK(bv)], writes=[("uT", b0 + jj) for jj in blks])
                if j == 3:
                    for jj in blks:
                        u_ready.add((seg, b0 + jj))

            def job_ga(j):
                b = proj_chunk(GA0 + 128 * j, N)
                silu2(b, N, gaT[qslot][:, j, 0:N], [("gaT", qslot, jj) for jj in blks])

            def job_gc(j):
                b = proj_chunk(GC0 + 128 * j, N)
                silu2(b, N, gcT[qslot][:, j, 0:N], [("gcT", qslot)])

            def job_q(j):
                b = proj_chunk(Q0 + 128 * j, N)
                qknorm(b, N, 0, qT[qslot][:, 0:nblk, j, :], [("qT", qslot, jj) for jj in blks], view3=True)

            for j in range(4):
                jobs.append((lambda j: lambda: job_glu(j))(j))
            if mode == "full":
                for j in range(4):
                    jobs.append((lambda j: lambda: job_ga(j))(j))
                for j in range(4):
                    jobs.append((lambda j: lambda: job_gc(j))(j))
            for g in range(2):
                jobs.append((lambda g: lambda: job_k(g))(g))
            if mode == "full":
                for j in range(4):
                    jobs.append((lambda j: lambda: job_q(j))(j))
            return jobs

        def att_scores(seg, qslot, jq, bq):
            qc = jq * 128
            ptiles = {}
            for g in range(2):
                for ty, kb in enumerate((bq - 1, bq, bq + 1)):
                    sb_ = 6 + nxt("st2", 2)

                    def smm(e, g=g, ty=ty, kb=kb, sb_=sb_):
                        e.matmul(banks[sb_][:, 0:256], lhsT=kTd[0:64, g, kb * 128:(kb + 1) * 128],
                                 rhs=qT[qslot][0:64, jq, 2 * g:2 * g + 2, :], start=True, stop=False,
                                 skip_group_check=True)
                        e.matmul(banks[sb_][:, 0:512], lhsT=IDN[:], rhs=BIAS[:, ty, 4 * g:4 * g + 4, :],
                                 start=False, stop=False, skip_group_check=True)
                        return e.matmul(banks[sb_][:, 256:512], lhsT=kTd[64:128, g, kb * 128:(kb + 1) * 128],
                                        rhs=qT[qslot][64:128, jq, 2 * g:2 * g + 2, :], start=False, stop=True,
                                        skip_group_check=True)
                    S.add("pe", smm, reads=[("kTd", kb), ("qT", qslot, jq), "IDN", "BIAS"], writes=[BK(sb_)])
                    p = nxt("pt", NPT)
                    ptiles[(g, ty)] = p
                    if kb == 0 or kb == NB + 1:
                        col = 2 * seg + (0 if kb == 0 else 1)
                        S.add("act", (lambda sb_, p, col: lambda e: e.activation(
                            out=PT[p][:], in_=banks[sb_][:], func=AF.Exp, bias=hb[:, col:col + 1]))(sb_, p, col),
                            reads=[BK(sb_), "hb"], writes=[("PT", p)])
                    else:
                        S.add("act", (lambda sb_, p: lambda e: e.activation(out=PT[p][:], in_=banks[sb_][:],
                                                                             func=AF.Exp))(sb_, p),
                              reads=[BK(sb_)], writes=[("PT", p)])
            sbm = 6 + nxt("st2", 2)

            def mmm(e):
                ins = None
                for g in range(2):
                    for a in range(2):
                        ins = e.matmul(banks[sbm][32 * g:32 * g + 16, a * 256:(a + 1) * 256],
                                       lhsT=kTmZ[:, g, a, 32 * g:32 * g + 16],
                                       rhs=qT[qslot][:, jq, 2 * g:2 * g + 2, :], start=True, stop=True)
                return ins
            S.add("pe", mmm, reads=[("kTmZ",), ("qT", qslot, jq)], writes=[BK(sbm)])
            pm = nxt("ptm", 2)
            S.add("act", lambda e: e.activation(out=PTm[pm][0:48, :], in_=banks[sbm][0:48, :], func=AF.Exp),
                  reads=[BK(sbm)], writes=[("PTm", pm)])
            return dict(seg=seg, qslot=qslot, jq=jq, bq=bq, qc=qc, ptiles=ptiles, pm=pm)

        def att_pv(cx):
            qslot, jq, bq, qc, ptiles, pm = cx["qslot"], cx["jq"], cx["bq"], cx["qc"], cx["ptiles"], cx["pm"]
            for g in range(2):
                pvb = g

                def pv(e, g=g, pvb=pvb):
                    ins = None
                    for hl in range(4):
                        a, j = hl % 2, hl // 2
                        c_ = a * 256 + j * 128
                        for ty, kb in enumerate((bq - 1, bq, bq + 1)):
                            e.matmul(banks[pvb][:, hl * 65:hl * 65 + 65], lhsT=PT[ptiles[(g, ty)]][:, c_:c_ + 128],
                                     rhs=vaug[:, kb, g, 0:65], start=(ty == 0), stop=False)
                        ins = e.matmul(banks[pvb][:, hl * 65:hl * 65 + 65], lhsT=PTm[pm][32 * g:32 * g + 16, c_:c_ + 128],
                                       rhs=vm[32 * g:32 * g + 16, 0, g, 0:65], start=False, stop=True)
                    return ins
                S.add("pe", pv, reads=[("PT", ptiles[(g, ty)]) for ty in range(3)] + [("PTm", pm), ("vm",), ("vm1",),
                                                                                      ("vaug1",)]
                      + [("vaug", kb) for kb in (bq - 1, bq, bq + 1)], writes=[BK(pvb)])
            o = nxt("on", 2)
            for g in range(2):
                S.add("dve", (lambda g: lambda e: e.tensor_tensor(
                    out=den[o][:, 4 * g:4 * g + 4], in0=banks[g][:, 64:260:65], in1=esink[:, 4 * g:4 * g + 4],
                    op=ALU.add))(g), reads=[BK(g), "esink"], writes=[("den", o, g)])
            S.add("dve", lambda e: e.reciprocal(out=den[o][:], in_=den[o][:]),
                  reads=[("den", o, 0), ("den", o, 1)], writes=[("den", o, 0), ("den", o, 1)])
            for g in range(2):
                S.add("dve", (lambda g: lambda e: e.tensor_tensor(
                    out=On[o][:, 256 * g:256 * g + 256].rearrange("p (h d) -> p h d", h=4),
                    in0=banks[g][:, 0:260].rearrange("p (h d) -> p h d", h=4)[:, :, 0:64],
                    in1=den[o][:, 4 * g:4 * g + 4].unsqueeze(2).to_broadcast([128, 4, 64]), op=ALU.mult))(g),
                    reads=[BK(g), ("den", o, 0), ("den", o, 1)], writes=[("On", o, g)])
            cx["o"] = o

        def att_out(cx):
            qslot, jq, qc, o = cx["qslot"], cx["jq"], cx["qc"], cx["o"]
            otb = banks[4][:].bitcast(BF16)

            def otr(e):
                ins = None
                for j in range(4):
                    ins = e.transpose(out=otb[:, j * 128:(j + 1) * 128], in_=On[o][:, j * 128:(j + 1) * 128],
                                      identity=IDN[:])
                return ins
            S.add("pe", otr, reads=[("On", o, 0), ("On", o, 1), "IDN"], writes=[BK(4)])
            S.add("dve", lambda e: e.tensor_tensor(out=mixT[:, 0:4, qc:qc + 128],
                                                   in0=otb[:, 0:512].rearrange("p (j t) -> p j t", j=4),
                                                   in1=gaT[qslot][:, 0:4, qc:qc + 128], op=ALU.mult),
                  reads=[BK(4), ("gaT", qslot, jq)], writes=[("mixA", jq)])

        def conv_mm(b0, cc):
            ublk = [("uT", b) for b in range(b0, b0 + 5)]
            c0 = b0 * 128
            cb = nxt("cv", 2)

            def cmm(e, cc=cc, cb=cb):
                ins = None
                for k in range(16, CONV_K):
                    ins = e.matmul(banks[cb][:], lhsT=DG[:, cc, k - 16, :], rhs=uT[:, cc, c0 + k - 15:c0 + k - 15 + 512],
                                   start=(k == 16), stop=(k == CONV_K - 1))
                return ins
            S.add("pe", cmm, reads=ublk + ["DG"], writes=[BK(cb)])
            S.add("dve", (lambda cc, cb: lambda e: e.tensor_tensor(out=acc[:, cc, :], in0=banks[cb][:], in1=acc[:, cc, :],
                                                                    op=ALU.add))(cc, cb),
                  reads=[BK(cb), ("acc", cc)], writes=[("acc", cc)])
            S.add("act", (lambda cc: lambda e: e.activation(out=yb[:, cc, :], in_=acc[:, cc, :], func=AF.Identity,
                                                             bias=v4[:, cc:cc + 1]))(cc),
                  reads=[("acc", cc), "v4"], writes=[("yb", cc)])
            ys = nxt("ysq", 4)
            S.add("act", (lambda cc, ys: lambda e: e.activation(out=ysq[ys][:], in_=acc[:, cc, :], func=AF.Square,
                                                                 bias=v4[:, cc:cc + 1]))(cc, ys),
                  reads=[("acc", cc), "v4"], writes=[("ysq", ys)])
            pending_stat.append((cc, ys))

        mac_q = []
        in_drain = [False]

        def conv_queue(seg, b0):
            c0 = b0 * 128
            for k in range(0, 16):
                for cc in range(4):
                    lo = c0 + k - 15
                    need = {(seg, b) for b in range(lo // 128, (lo + 511) // 128 + 1)}
                    j = cc * CONV_K + k
                    rd = [("uT", b) for (_, b) in need] + [("acc", cc), "cwp"]
                    if k == 0:
                        fn = (lambda cc, lo, j: lambda e: e.tensor_scalar(out=acc[:, cc, :], in0=uT[:, cc, lo:lo + 512],
                                                                           scalar1=cwp[:, j:j + 1], scalar2=None,
                                                                           op0=ALU.mult))(cc, lo, j)
                    else:
                        fn = (lambda cc, lo, j: lambda e: e.scalar_tensor_tensor(out=acc[:, cc, :], in0=uT[:, cc, lo:lo + 512],
                                                                                  scalar=cwp[:, j:j + 1], in1=acc[:, cc, :],
                                                                                  op0=ALU.mult, op1=ALU.add))(cc, lo, j)
                    mac_q.append((need, fn, rd, cc))

        def drain(n):
            if in_drain[0]:
                return
            in_drain[0] = True
            while n > 0 and mac_q and mac_q[0][0] <= u_ready:
                need, fn, rd, cc = mac_q.pop(0)
                S.add("dve", fn, reads=rd, writes=[("acc", cc)])
                n -= 1
            in_drain[0] = False

        _add = S.add

        def add_hook(eng, fn, reads=(), writes=(), dma=False):
            op = _add(eng, fn, reads=reads, writes=writes, dma=dma)
            if eng == "dve" and not dma and not in_drain[0]:
                drain(1)
            return op
        S.add = add_hook

        pending_stat = []

        def conv_statmm():
            while pending_stat:
                cc, ys = pending_stat.pop(0)
                S.add("pe", (lambda cc: lambda e: e.matmul(banks[2][:], lhsT=ONL[:], rhs=yb[:, cc, :], start=(cc == 0),
                                                            stop=(cc == 3)))(cc),
                      reads=[("yb", cc), "ONL"], writes=[BK(2)])
                S.add("pe", (lambda cc, ys: lambda e: e.matmul(banks[3][:], lhsT=ONL[:], rhs=ysq[ys][:], start=(cc == 0),
                                                                stop=(cc == 3)))(cc, ys),
                      reads=[("ysq", ys), "ONL"], writes=[BK(3)])

        def conv_stats():
            S.add("act", lambda e: e.activation(out=m2[:], in_=banks[2][:], func=AF.Square), reads=[BK(2)], writes=["m2"])
            S.add("dve", lambda e: e.scalar_tensor_tensor(out=lnr[:], in0=banks[3][:], scalar=1e-5, in1=m2[:],
                                                          op0=ALU.add, op1=ALU.subtract),
                  reads=[BK(3), "m2"], writes=["lnr"])
            S.add("act", lambda e: e.activation(out=lnr[:], in_=lnr[:], func=AF.Ln), reads=["lnr"], writes=["lnr"])
            S.add("act", lambda e: e.activation(out=lnr[:], in_=lnr[:], func=AF.Exp, scale=-0.5), reads=["lnr"], writes=["lnr"])
            S.add("dve", lambda e: e.scalar_tensor_tensor(out=nmr[:], in0=banks[2][:], scalar=-1.0, in1=lnr[:],
                                                          op0=ALU.mult, op1=ALU.mult),
                  reads=[BK(2), "lnr", "m2"], writes=["m2"])

        def conv_epi(qslot, cc):
            if True:
                S.add("dve", (lambda cc: lambda e: e.tensor_tensor(out=t2[:], in0=yb[:, cc, :], in1=lnr[:],
                                                                    op=ALU.mult))(cc),
                      reads=[("yb", cc), "lnr"], writes=["t2"])
                S.add("dve", lambda e: e.tensor_tensor(out=t2[:], in0=t2[:], in1=nmr[:], op=ALU.add),
                      reads=["t2", "m2"], writes=["t2"])
                t = nxt("th", 2)
                S.add("act", (lambda cc: lambda e: e.activation(out=ta[:], in_=t2[:], func=AF.Identity,
                                                                 scale=v4[:, 4 + cc:5 + cc], bias=v4[:, 8 + cc:9 + cc]))(cc),
                      reads=["t2", "v4"], writes=["ta"])
                S.add("act", (lambda cc, t: lambda e: e.activation(out=th[t][:], in_=t2[:], func=AF.Tanh,
                                                                    scale=v4h[:, cc:cc + 1], bias=v4h[:, 4 + cc:5 + cc]))(cc, t),
                      reads=["t2", "v4h"], writes=[("th", t)])
                S.add("dve", (lambda t: lambda e: e.scalar_tensor_tensor(out=ta[:], in0=th[t][:], scalar=1.0, in1=ta[:],
                                                                          op0=ALU.add, op1=ALU.mult))(t),
                      reads=[("th", t), "ta"], writes=["ta"])
                S.add("dve", (lambda cc: lambda e: e.tensor_tensor(out=yb[:, cc, :], in0=ta[:],
                                                                    in1=gcT[qslot][:, cc, :], op=ALU.mult))(cc),
                      reads=["ta", ("gcT", qslot)], writes=[("yb", cc)])

        def outproj_block(seg, jq, bq):
            qc = jq * 128
            r = nxt("xt", 3)
            dma(xres[r][:], xe[seg, bq * 128:(bq + 1) * 128, :], writes=[("xt", r)])
            for half in range(2):
                ob = 6 + half

                def omm(e, half=half, ob=ob):
                    ins = None
                    for k in range(8):
                        lt = mixT[:, k, qc:qc + 128] if k < 4 else yb[:, k - 4, qc:qc + 128]
                        ins = e.matmul(banks[ob][:], lhsT=lt,
                                       rhs=WO[:, k, half * 512:(half + 1) * 512], start=(k == 0), stop=(k == 7))
                    return ins
                S.add("pe", omm, reads=[("mixA", jq)] + [("yb", cc) for cc in range(4)] + WOALL, writes=[BK(ob)])
                S.add("dve", (lambda half, ob: lambda e: e.tensor_tensor(
                    out=xres[r][:, half * 512:(half + 1) * 512], in0=banks[ob][:],
                    in1=xres[r][:, half * 512:(half + 1) * 512], op=ALU.add))(half, ob),
                    reads=[BK(ob), ("xt", r)], writes=[("xt", r)])
            S.add("pool", lambda e: e.dma_start(out=y[seg, (bq - 1) * 128:bq * 128, :], in_=xres[r][:]),
                  reads=[("xt", r)], dma=True)

        if STOP >= 1:
            norm_T(norm_pre(lambda j: xmeta, 1))
            for jb in projA_jobs(0, 1, "meta"):
                jb()
        S.add("dve", lambda e: e.memset(kTmZ[:], 0.0), writes=[("kTmZ",)])
        for g in range(2):
            for a in range(2):
                S.add("dve", (lambda g, a: lambda e: e.tensor_copy(out=kTmZ[64 * a:64 * a + 64, g, a, :],
                                                                   in_=kTm[64 * a:64 * a + 64, g, 0:48]))(g, a),
                      reads=[("kTm",)], writes=[("kTmZ",)])

        A_list = []
        for seg in range(NSEG if STOP >= 2 else 0):
            A_list.append((seg, 0, "halo", 0, 1))
            for i in range(NST):
                A_list.append((seg, i + 1, "full", 1 + 4 * i, 4))
            A_list.append((seg, NST + 1, "halo", NB + 1, 1))
        B_list = [(seg, i) for seg in range(NSEG if STOP >= 2 else 0) for i in range(NST)]
        pa = [0]
        slot_of = {}

        def src_of(entry):
            seg, k, mode, b0, nblk = entry
            return (lambda seg, b0: lambda j: xe[seg, (b0 + j) * 128:(b0 + j + 1) * 128, :])(seg, b0)

        pre_slots = {}

        def do_pre(idx):
            if idx < len(A_list) and idx not in pre_slots:
                pre_slots[idx] = norm_pre(src_of(A_list[idx]), A_list[idx][4])

        def A_jobs(idx):
            seg, k, mode, b0, nblk = A_list[idx]
            qs = None
            if mode == "full":
                qs = nxt("qslot", 2)
                slot_of[(seg, k)] = qs
            pj = projA_jobs(b0, nblk, mode, qs, seg)

            def first():
                do_pre(idx)
                norm_T(pre_slots[idx])
            return [first, pj[0]] + pj[1:5] + [lambda: do_pre(idx + 1)] + pj[5:]

        def need_idx(seg, i):
            return seg * (NST + 2) + i + 2

        def first_slot(entry, seg, b0):
            eseg, k, mode, eb0, nblk = entry
            if eseg == seg:
                return 0
            last = -1
            for jq in range(4):
                rd = range(b0 + jq - 1, b0 + jq + 2)
                if any(eb0 <= r < eb0 + nblk for r in rd):
                    last = jq
            return last + 1

        for bi, (seg, i) in enumerate(B_list):
            while pa[0] <= need_idx(seg, i):
                for jb in A_jobs(pa[0]):
                    jb()
                pa[0] += 1
            qslot = slot_of[(seg, i + 1)]
            b0 = 1 + 4 * i
            slots = [[] for _ in range(5)]
            if bi + 1 < len(B_list):
                nseg, ni = B_list[bi + 1]
                tgt = need_idx(nseg, ni)
                pend = []
                while pa[0] <= tgt:
                    pend.append(pa[0])
                    pa[0] += 1
                queue = []
                post = False
                for idx in pend:
                    fs = first_slot(A_list[idx], seg, b0)
                    post = post or fs >= 4
                    for jb in A_jobs(idx):
                        queue.append((4 if post else fs, jb))
                n_in = sum(1 for fs, _ in queue if fs < 4)
                quota = max(1, -(-n_in // 4))
                cur, cnt_ = 0, 0
                for fs, jb in queue:
                    if fs >= 4:
                        slots[4].append(jb)
                        continue
                    if fs > cur:
                        cur, cnt_ = fs, 0
                    if cnt_ >= quota and cur < 3:
                        cur, cnt_ = cur + 1, 0
                    slots[cur].append(jb)
                    cnt_ += 1
            if bi == 0:
                conv_queue(seg, b0)
                drain(10 ** 9)
                assert not mac_q
            for cc in range(4):
                conv_mm(b0, cc)
                if cc >= 1:
                    pass
                if cc >= 1:
                    last = pending_stat.pop()
                    conv_statmm()
                    pending_stat.append(last)
            if bi + 1 < len(B_list):
                nseg, ni = B_list[bi + 1]
                conv_queue(nseg, 1 + 4 * ni)
            cxs = []
            for jq in range(4):
                cx = att_scores(seg, qslot, jq, b0 + jq)
                cxs.append(cx)
                if jq == 0:
                    conv_statmm()
                    conv_stats()
                if jq >= 1:
                    att_out(cxs[jq - 1])
                if jq == 3:
                    conv_epi(qslot, jq)
                for jb in slots[jq]:
                    jb()
                if jq < 3:
                    conv_epi(qslot, jq)
                att_pv(cx)
            for jq in range(2):
                outproj_block(seg, jq, b0 + jq)
            att_out(cxs[3])
            for jb in slots[4]:
                jb()
            for jq in range(2, 4):
                outproj_block(seg, jq, b0 + jq)
            drain(10 ** 9)
            assert not mac_q, "conv MACs left whose u blocks were never produced"

        S.finalize(sems, dsems)
        with nc.Block() as block:
            @block.tensor
            def _(e):
                S.emit("pe", e)

            @block.scalar
            def _(e):
                S.emit("act", e)

            @block.vector
            def _(e):
                S.emit("dve", e)

            @block.gpsimd
            def _(e):
                S.emit("pool", e)

            @block.sync
            def _(e):
                S.emit("sp", e)
                S.final_waits(e)
    return nc


def _const_tables():
    i = np.arange(128)
    jj, ii = np.meshgrid(i, i, indexing="ij")
    slopes = np.exp2(-8.0 * np.arange(1, N_HEADS + 1) / N_HEADS).astype(np.float32)
    tab = np.empty((128, 3, N_HEADS, 128), np.float32)
    dl, vl = 128 + ii - jj, jj >= ii
    dm = np.abs(ii - jj)
    dr, vr = 128 + jj - ii, jj <= ii
    order = [0, 2, 1, 3, 4, 6, 5, 7]
    for sl, h in enumerate(order):
        tab[:, 0, sl, :] = np.where(vl, -slopes[h] * dl, NEG)
        tab[:, 1, sl, :] = -slopes[h] * dm
        tab[:, 2, sl, :] = np.where(vr, -slopes[h] * dr, NEG)
    bf = ml_dtypes.bfloat16
    ident = np.eye(128, dtype=np.float32).astype(bf)
    bd = np.kron(np.eye(2, dtype=np.float32), np.full((64, 64), 1.0 / 64, np.float32)).astype(bf)
    onesln = np.full((128, 128), 1.0 / 512, np.float32).astype(bf)
    return tab.reshape(128, -1).astype(bf), ident, bd, onesln


def _segment(xseq, meta, t0, n):
    S_ = xseq.shape[0]
    if t0 > 0:
        left, hl = xseq[t0 - 128:t0], 0.0
    else:
        left = np.concatenate([np.zeros((128 - N_META, D_MODEL), np.float32), meta], 0)
        hl = NEG
    if t0 + n < S_:
        right, hr = xseq[t0 + n:t0 + n + 128], 0.0
    else:
        right, hr = np.zeros((128, D_MODEL), np.float32), NEG
    return np.concatenate([left, xseq[t0:t0 + n], right], 0), hl, hr


def _core_inputs(segs, meta, weights, consts):
    xs, hbv = [], []
    for (xseq, t0, n) in segs:
        x_, hl, hr = _segment(xseq, meta, t0, n)
        xs.append(x_)
        hbv += [hl, hr]
    xmeta = np.zeros((128, D_MODEL), np.float32)
    xmeta[0:N_META] = meta
    xmeta[32:32 + N_META] = meta
    d = dict(weights)
    d.update(consts)
    d["xe"] = np.ascontiguousarray(np.stack(xs, 0))
    d["xmeta"] = xmeta
    d["hb"] = np.ascontiguousarray(np.broadcast_to(np.asarray(hbv, np.float32)[None, :], (128, len(hbv))))
    return d


def _pack_weights(norm_w, w_in, q_norm_w, k_norm_w, sink_logits, conv_w, conv_b, conv_ln_w, conv_ln_b, w_out):
    f = np.float32
    return {
        "w_in": np.ascontiguousarray(w_in[0], f),
        "w_out": np.ascontiguousarray(w_out[0], f),
        "normw_pk": np.ascontiguousarray(norm_w[0].reshape(8, 128).T, f),
        "qkw": np.ascontiguousarray(np.stack([np.tile(q_norm_w[0], 2), np.tile(k_norm_w[0], 2)], 1), f),
        "convw_pk": np.ascontiguousarray(conv_w[0].reshape(CONV_K, 4, 128).transpose(2, 1, 0).reshape(128, 4 * CONV_K), f),
        "vec4": np.ascontiguousarray(np.concatenate([conv_b[0].reshape(4, 128).T, conv_ln_w[0].reshape(4, 128).T,
                                                     conv_ln_b[0].reshape(4, 128).T], 1), f),
        "sink_b": np.ascontiguousarray(np.broadcast_to(sink_logits[0][None, :], (128, N_HEADS)), f),
    }


_NC_CACHE = {}


def kernel(x_prompt, x_sample, meta_tokens, norm_w, w_in, q_norm_w, k_norm_w, sink_logits,
           conv_w, conv_b, conv_ln_w, conv_ln_b, w_out):
    NB, NSEG, NCORE = 8, 8, 8
    n = NB * 128
    xp = np.asarray(x_prompt, np.float32)
    xs = np.asarray(x_sample, np.float32)
    meta = np.asarray(meta_tokens, np.float32)
    weights = _pack_weights(*[np.asarray(a, np.float32) for a in
                              (norm_w, w_in, q_norm_w, k_norm_w, sink_logits, conv_w, conv_b, conv_ln_w, conv_ln_b, w_out)])
    tab, ident, bd, onesln = _const_tables()
    consts = {"bias_c": tab, "ident_c": ident, "bd_c": bd, "onesln_c": onesln}
    in_maps, place = [], []
    for c in range(NCORE):
        segs, pl = [], []
        for sq_ in (2 * c, 2 * c + 1):
            for t0 in (0, n):
                segs.append((xp[sq_], t0, n))
                pl.append((0, sq_, t0))
        sseq, base = c // 4, (c % 4) * 4 * n
        for q in range(4):
            segs.append((xs[sseq], base + q * n, n))
            pl.append((1, sseq, base + q * n))
        in_maps.append(_core_inputs(segs, meta, weights, consts))
        place.append(pl)
    key = (NSEG, NB)
    if key not in _NC_CACHE:
        _NC_CACHE[key] = _build(NSEG, NB)
    res = run_bass_kernel_spmd(_NC_CACHE[key], in_maps, core_ids=list(range(NCORE)))
    yp = np.empty(xp.shape, np.float32)
    ys = np.empty(xs.shape, np.float32)
    for c in range(NCORE):
        yc = res.results[c]["y"]
        for s_, (which, sq_, t0) in enumerate(place[c]):
            (yp if which == 0 else ys)[sq_, t0:t0 + n] = yc[s_]
    return (yp, ys)
```

```python
import math
from contextlib import ExitStack

import numpy as np
import ml_dtypes
import concourse.bass as bass
import concourse.mybir as mybir
from concourse.bass_utils import run_bass_kernel_spmd

F32 = mybir.dt.float32
BF16 = mybir.dt.bfloat16
AF = mybir.ActivationFunctionType
ALU = mybir.AluOpType

D_MODEL = 1024
N_META = 16
N_HEADS = 8
CONV_K = 31
NEG = -30000.0

Q0, KD0, KD1, V0, GA0, CV0, CG0, GC0, WCOLS = 0, 512, 640, 768, 896, 1408, 1920, 2432, 2944


class _Op:
    __slots__ = ("eng", "fn", "deps", "signal", "sem", "val", "dma", "waits", "prev_val")

    def __init__(self, eng, fn, dma):
        self.eng = eng
        self.fn = fn
        self.dma = dma
        self.deps = set()
        self.signal = False
        self.sem = None
        self.val = 0
        self.prev_val = 0
        self.waits = []


class Sched:
    ENGS = ("pe", "act", "dve", "pool", "sp")

    def __init__(self):
        self.ops = {e: [] for e in self.ENGS}
        self.last_writer = {}
        self.readers = {}

    def add(self, eng, fn, reads=(), writes=(), dma=False):
        op = _Op(eng, fn, dma)
        deps = op.deps
        for r in reads:
            w = self.last_writer.get(r)
            if w is not None:
                deps.add(w)
        for w_ in writes:
            w = self.last_writer.get(w_)
            if w is not None:
                deps.add(w)
            for rd in self.readers.get(w_, ()):
                deps.add(rd)
        for r in reads:
            self.readers.setdefault(r, []).append(op)
        for w_ in writes:
            self.last_writer[w_] = op
            self.readers[w_] = []
        deps.discard(op)
        if eng == "pe" and not dma:
            op.deps = {d for d in deps if not (d.eng == "pe" and not d.dma)}
        for d in op.deps:
            d.signal = True
        self.ops[eng].append(op)
        return op

    def finalize(self, eng_sems, dma_sems):
        for e in self.ENGS:
            cnt = 0
            dcount = {}
            k = 0
            pool = dma_sems.get(e, [])
            for op in self.ops[e]:
                if op.dma:
                    s = pool[k % len(pool)]
                    k += 1
                    prev = dcount.get(id(s), 0)
                    op.sem = s
                    op.prev_val = prev
                    op.val = prev + 16
                    dcount[id(s)] = op.val
                elif op.signal:
                    cnt += 1
                    op.sem = eng_sems[e]
                    op.val = cnt
        for e in self.ENGS:
            waited = {}
            for op in self.ops[e]:
                need = {}
                for d in op.deps:
                    key = id(d.sem)
                    if d.val > need.get(key, (None, 0))[1]:
                        need[key] = (d.sem, d.val)
                if op.dma and op.prev_val > 0:
                    key = id(op.sem)
                    if op.prev_val > need.get(key, (None, 0))[1]:
                        need[key] = (op.sem, op.prev_val)
                for key, (s, v) in need.items():
                    if v > waited.get(key, 0):
                        waited[key] = v
                        op.waits.append((s, v))

    def emit(self, eng_name, e):
        for op in self.ops[eng_name]:
            for (s, v) in op.waits:
                e.wait_ge(s, v)
            ins = op.fn(e)
            if op.dma:
                ins.then_inc(op.sem, 16)
            elif op.signal:
                ins.then_inc(op.sem, 1)

    def final_waits(self, e):
        last = {}
        for q in self.ENGS:
            for op in self.ops[q]:
                if op.dma:
                    last[id(op.sem)] = (op.sem, op.val)
        for s, v in last.values():
            e.wait_ge(s, v)


def _build(NSEG, NB, STOP=9):
    assert NB % 4 == 0
    NST = NB // 4
    NBX = NB + 2
    TX = NBX * 128
    nc = bass.Bass("TRN2", target_bir_lowering=False)

    def din(name, shape, dt=F32):
        return nc.dram_tensor(name, shape, dt, kind="ExternalInput").ap()

    xe = din("xe", [NSEG, TX, D_MODEL])
    xmeta = din("xmeta", [128, D_MODEL])
    w_in = din("w_in", [D_MODEL, 2816])
    w_out = din("w_out", [D_MODEL, D_MODEL])
    normw_pk = din("normw_pk", [128, 8])
    qkw = din("qkw", [128, 2])
    convw_pk = din("convw_pk", [128, 4 * CONV_K])
    vec4 = din("vec4", [128, 12])
    sink_b = din("sink_b", [128, 8])
    hb_in = din("hb", [128, 2 * NSEG])
    ident_c = din("ident_c", [128, 128], BF16)
    bd_c = din("bd_c", [128, 128], BF16)
    onesln_c = din("onesln_c", [128, 128], BF16)
    bias_c = din("bias_c", [128, 3 * 8 * 128], BF16)
    y = nc.dram_tensor("y", [NSEG, NB * 128, D_MODEL], F32, kind="ExternalOutput").ap()

    S = Sched()
    with ExitStack() as st:
        E = st.enter_context

        def sb(name, shape, dt=F32):
            return E(nc.sbuf_tensor("sb_" + name, shape, dt))

        W = sb("W", [128, 8, WCOLS], BF16)
        WO = sb("WO", [128, 8, D_MODEL], BF16)
        DG = sb("DG", [128, 4, 15, 128], BF16)
        BIAS = sb("BIAS", [128, 3, 8, 128], BF16)
        IDN = sb("IDN", [128, 128], BF16)
        BD = sb("BD", [128, 128], BF16)
        ONL = sb("ONL", [128, 128], BF16)
        nwp = sb("nwp", [128, 8])
        qkv = sb("qkv", [128, 2])
        qks = sb("qks", [128, 2])
        cwp = sb("cwp", [128, 4 * CONV_K])
        v4 = sb("v4", [128, 12])
        v4h = sb("v4h", [128, 8])
        snk = sb("snk", [128, 8])
        esink = sb("esink", [128, 8])
        hb = sb("hb", [128, 2 * NSEG])
        mhalf = sb("mhalf", [128, 1])
        epsc = sb("epsc", [128, 1])
        kTd = sb("kTd", [128, 2, TX], BF16)
        vaug = sb("vaug", [128, NBX, 2, 66], BF16)
        uT = sb("uT", [128, 4, TX], BF16)
        kTm = sb("kTm", [128, 2, 128], BF16)
        vm = sb("vm", [128, 1, 2, 66], BF16)
        kTmZ = sb("kTmZ", [128, 2, 2, 48], BF16)
        hT = sb("hT", [128, 8, 512], BF16)
        qT = [sb(f"qT{i}", [128, 4, 4, 128], BF16) for i in range(2)]
        gaT = [sb(f"gaT{i}", [128, 4, 512], BF16) for i in range(2)]
        gcT = [sb(f"gcT{i}", [128, 4, 512], BF16) for i in range(2)]
        mixT = sb("mixT", [128, 4, 512], BF16)
        xt = [sb(f"xt{i}", [128, D_MODEL]) for i in range(3)]
        xres = xt
        htm = [sb(f"htm{i}", [128, D_MODEL], BF16) for i in range(4)]
        ss = [sb(f"ss{i}", [128, 1]) for i in range(4)]
        sq = [sb(f"sq{i}", [128, 512], BF16) for i in range(2)]
        tq = [sb(f"tq{i}", [128, 512]) for i in range(2)]
        th = [sb(f"th{i}", [128, 512]) for i in range(2)]
        NPT = 8
        PT = [sb(f"PT{i}", [128, 512], BF16) for i in range(NPT)]
        PTm = [sb(f"PTm{i}", [128, 512], BF16) for i in range(2)]
        On = [sb(f"On{i}", [128, 512], BF16) for i in range(2)]
        den = [sb(f"den{i}", [128, 8]) for i in range(2)]
        yb = sb("yb", [128, 4, 512], BF16)
        acc = sb("acc", [128, 4, 512])
        ysq = [sb(f"ysq{i}", [128, 512], BF16) for i in range(4)]
        lnr = sb("lnr", [128, 512])
        m2 = sb("m2", [128, 512])
        nmr = m2
        t2 = sb("t2", [128, 512])
        ta = sb("ta", [128, 512])

        banks = [E(nc.psum_tensor(f"bank{i}", [128, 512], F32)) for i in range(8)]
        sems = {e: E(nc.semaphore("s_" + e)) for e in Sched.ENGS}
        dsems = {"sp": [E(nc.semaphore(f"d{i}")) for i in range(24)],
                 "pool": [E(nc.semaphore(f"dp{i}")) for i in range(8)]}

        def BK(i):
            return ("bank", i)

        cnt = {"htm": 0, "xt": 0, "xres": 0, "sq": 0, "th": 0, "pt": 0, "ptm": 0, "on": 0, "ysq": 0, "proj": 0,
               "st2": 0, "cv": 0, "qslot": 0}

        def nxt(k, n):
            v = cnt[k] % n
            cnt[k] += 1
            return v

        def dma(out_ap, in_ap, reads=(), writes=()):
            S.add("sp", lambda e: e.dma_start(out=out_ap, in_=in_ap), reads=reads, writes=writes, dma=True)

        dma(nwp[:], normw_pk, writes=["nwp"])
        dma(qkv[:], qkw, writes=["qkv"])
        dma(cwp[:], convw_pk, writes=["cwp"])
        dma(v4[:], vec4, writes=["v4"])
        dma(snk[:], sink_b, writes=["snk"])
        dma(hb[:], hb_in, writes=["hb"])
        dma(IDN[:], ident_c, writes=["IDN"])
        dma(BD[:], bd_c, writes=["BD"])
        dma(ONL[:], onesln_c, writes=["ONL"])
        dma(BIAS[:].rearrange("p a h q -> p (a h q)"), bias_c, writes=["BIAS"])

        S.add("pool", lambda e: e.memset(mhalf[:], -0.5), writes=["mhalf"])
        S.add("pool", lambda e: e.memset(epsc[:], 1e-6), writes=["epsc"])
        S.add("pool", lambda e: e.memset(vaug[:, :, :, 64:66], 1.0), writes=[("vaug1",)])
        S.add("pool", lambda e: e.memset(vm[:, :, :, 64:66], 1.0), writes=[("vm1",)])
        for i in range(8):
            S.add("dve", (lambda i: lambda e: e.memset(banks[i][:], 0.0))(i), writes=[BK(i)])
        S.add("dve", lambda e: e.tensor_scalar(out=qks[:, 0:1], in0=qkv[:, 0:1], scalar1=0.125, scalar2=None,
                                               op0=ALU.mult), reads=["qkv"], writes=["qks0"])
        S.add("dve", lambda e: e.tensor_copy(out=qks[:, 1:2], in_=qkv[:, 1:2]), reads=["qkv"], writes=["qks1"])
        S.add("dve", lambda e: e.tensor_scalar(out=cwp[:], in0=cwp[:], scalar1=0.5, scalar2=None, op0=ALU.mult),
              reads=["cwp"], writes=["cwp"])
        S.add("dve", lambda e: e.tensor_scalar(out=v4h[:], in0=v4[:, 4:12], scalar1=0.5, scalar2=None,
                                               op0=ALU.mult), reads=["v4"], writes=["v4h"])
        S.add("act", lambda e: e.activation(out=esink[:], in_=snk[:], func=AF.Exp), reads=["snk"], writes=["esink"])

        def stage_view(slot):
            return xt[slot][:].rearrange("p (k n) -> p k n", k=8)

        nwb = nwp[:].unsqueeze(2).to_broadcast([128, 8, 128])

        def wpiece(c0):
            slot = nxt("xt", 3)
            sv = stage_view(slot)
            dma(sv, w_in[:, c0:c0 + 128].rearrange("(k p) n -> p k n", p=128), writes=[("xt", slot)])
            if c0 < 512:
                dsts = [(Q0 + c0, 0, 128)]
            elif c0 == 512:
                dsts = [(KD0, 0, 64), (KD0 + 64, 0, 64), (KD1, 64, 64), (KD1 + 64, 64, 64)]
            elif c0 == 640:
                dsts = [(V0, 0, 128)]
            elif c0 < 1280:
                dsts = [(GA0 + c0 - 768, 0, 128)]
            elif c0 < 1792:
                dsts = [(CV0 + c0 - 1280, 0, 128)]
            elif c0 < 2304:
                dsts = [(CG0 + c0 - 1792, 0, 128)]
            else:
                dsts = [(GC0 + c0 - 2304, 0, 128)]
            for (d0, s0, n) in dsts:
                S.add("dve", (lambda d0, s0, n, sv: lambda e: e.tensor_tensor(
                    out=W[:, :, d0:d0 + n], in0=sv[:, :, s0:s0 + n],
                    in1=nwp[:].unsqueeze(2).to_broadcast([128, 8, n]), op=ALU.mult))(d0, s0, n, sv),
                    reads=[("xt", slot), "nwp"], writes=[("W", d0)])

        for c0 in range(0, 2816, 128):
            wpiece(c0)

        def wopiece(c0):
            slot = nxt("xt", 3)
            sv = stage_view(slot)
            dma(sv, w_out[:, c0:c0 + 128].rearrange("(k p) n -> p k n", p=128), writes=[("xt", slot)])
            S.add("dve", lambda e: e.tensor_scalar(out=WO[:, 0:4, c0:c0 + 128], in0=sv[:, 0:4, :], scalar1=0.5,
                                                   scalar2=None, op0=ALU.mult),
                  reads=[("xt", slot)], writes=[("WO", c0, 0)])
            S.add("dve", lambda e: e.tensor_scalar(out=WO[:, 4:8, c0:c0 + 128], in0=sv[:, 4:8, :], scalar1=0.25,
                                                   scalar2=None, op0=ALU.mult),
                  reads=[("xt", slot)], writes=[("WO", c0, 1)])

        for c0 in range(0, D_MODEL, 128):
            wopiece(c0)

        def dgs(e):
            ins = None
            for cc in range(4):
                for k in range(16, CONV_K):
                    j = cc * CONV_K + k
                    ins = e.tensor_scalar(out=DG[:, cc, k - 16, :], in0=IDN[:], scalar1=cwp[:, j:j + 1], scalar2=None,
                                          op0=ALU.mult)
            return ins
        S.add("dve", dgs, reads=["IDN", "cwp"], writes=["DG"])
        WALL = [("W", d) for d in ([Q0 + i for i in range(0, 512, 128)] + [KD0, KD0 + 64, KD1, KD1 + 64, V0]
                                   + [b + i for b in (GA0, CV0, CG0, GC0) for i in range(0, 512, 128)])]
        WOALL = [("WO", c0, h) for c0 in range(0, D_MODEL, 128) for h in range(2)]

        def norm_pre(src, nblk):
            hs = []
            for j in range(nblk):
                slot = nxt("xt", 3)
                h_ = nxt("htm", 4)
                hs.append(h_)
                dma(xt[slot][:], src(j), writes=[("xt", slot)])
                S.add("act", (lambda slot, h_: lambda e: e.activation(out=htm[h_][:], in_=xt[slot][:], func=AF.Square,
                                                                       accum_out=ss[h_][:]))(slot, h_),
                      reads=[("xt", slot)], writes=[("htm", h_), ("ss", h_)])
                S.add("dve", (lambda h_: lambda e: e.tensor_scalar(out=ss[h_][:], in0=ss[h_][:], scalar1=1.0 / D_MODEL,
                                                                    scalar2=1e-6, op0=ALU.mult, op1=ALU.add))(h_),
                      reads=[("ss", h_)], writes=[("ss", h_)])
                S.add("pool", (lambda h_: lambda e: e.tensor_tensor(out=ss[h_][:], in0=ss[h_][:], in1=mhalf[:, 0:1],
                                                                     op=ALU.pow))(h_),
                      reads=[("ss", h_), "mhalf"], writes=[("ss", h_)])
                S.add("dve", (lambda slot, h_: lambda e: e.tensor_scalar(out=htm[h_][:], in0=xt[slot][:], scalar1=ss[h_][:],
                                                                          scalar2=None, op0=ALU.mult))(slot, h_),
                      reads=[("xt", slot), ("ss", h_)], writes=[("htm", h_)])
            return hs

        def norm_T(hs):
            for j, h_ in enumerate(hs):
                tb = 4 + (j % 2)
                trb = banks[tb][:].bitcast(BF16)

                def tr(e, h_=h_, trb=trb):
                    ins = None
                    for k in range(8):
                        ins = e.transpose(out=trb[:, k * 128:(k + 1) * 128], in_=htm[h_][:, k * 128:(k + 1) * 128],
                                          identity=IDN[:])
                    return ins
                S.add("pe", tr, reads=[("htm", h_), "IDN"], writes=[BK(tb)])
                S.add("act", (lambda j, trb: lambda e: e.activation(out=hT[:, :, j * 128:(j + 1) * 128],
                                                                     in_=trb.rearrange("p (k t) -> p k t", k=8),
                                                                     func=AF.Copy))(j, trb),
                      reads=[BK(tb)], writes=[("hT", j)])

        def proj_chunk(wcol, N, lo=0):
            b = nxt("proj", 3)

            def mm(e):
                ins = None
                for k in range(8):
                    ins = e.matmul(banks[b][:, 0:N], lhsT=W[:, k, wcol:wcol + 128], rhs=hT[:, k, lo:lo + N],
                                   start=(k == 0), stop=(k == 7))
                return ins
            S.add("pe", mm, reads=[("hT", j) for j in range((lo + N + 127) // 128)] + WALL, writes=[BK(b)])
            return b

        def silu2(b, N, out_ap, out_res):
            t = nxt("th", 2)
            S.add("act", lambda e: e.activation(out=th[t][:, 0:N], in_=banks[b][:, 0:N], func=AF.Tanh, scale=0.5),
                  reads=[BK(b)], writes=[("th", t)])
            S.add("dve", lambda e: e.scalar_tensor_tensor(out=out_ap, in0=th[t][:, 0:N], scalar=1.0,
                                                          in1=banks[b][:, 0:N], op0=ALU.add, op1=ALU.mult),
                  reads=[("th", t), BK(b)], writes=out_res)

        def qknorm(b, N, scol, out_ap, out_res, view3=False):
            s_ = nxt("sq", 2)
            stb = 3
            S.add("act", lambda e: e.activation(out=sq[s_][:, 0:N], in_=banks[b][:, 0:N], func=AF.Square),
                  reads=[BK(b)], writes=[("sq", s_)])
            S.add("pe", lambda e: e.matmul(banks[stb][:, 0:N], lhsT=BD[:], rhs=sq[s_][:, 0:N], start=True, stop=True),
                  reads=[("sq", s_), "BD"], writes=[BK(stb)])
            S.add("act", lambda e: e.activation(out=tq[s_][:, 0:N], in_=banks[stb][:, 0:N], func=AF.Ln, bias=epsc[:, 0:1]),
                  reads=[BK(stb), "epsc"], writes=[("tq", s_)])
            S.add("act", lambda e: e.activation(out=tq[s_][:, 0:N], in_=tq[s_][:, 0:N], func=AF.Exp, scale=-0.5),
                  reads=[("tq", s_)], writes=[("tq", s_)])
            v3 = (lambda ap: ap.rearrange("p (j t) -> p j t", t=128)) if view3 else (lambda ap: ap)
            S.add("dve", lambda e: e.scalar_tensor_tensor(out=out_ap, in0=v3(banks[b][:, 0:N]), scalar=qks[:, scol:scol + 1],
                                                          in1=v3(tq[s_][:, 0:N]), op0=ALU.mult, op1=ALU.mult),
                  reads=[BK(b), ("tq", s_), "qks0", "qks1"], writes=out_res)

        u_ready = set()

        def projA_jobs(b0, nblk, mode, qslot=None, seg=None):
            N = nblk * 128
            if mode == "meta":
                kdst, vdst, kres, vres = kTm, vm, lambda j: ("kTm",), lambda j: ("vm",)
                c0 = 0
            else:
                kdst, vdst, kres, vres = kTd, vaug, lambda j: ("kTd", b0 + j), lambda j: ("vaug", b0 + j)
                c0 = b0 * 128
            blks = range(nblk)
            jobs = []

            def job_v():
                def vmm(e):
                    ins = None
                    for j in range(nblk):
                        for k in range(8):
                            ins = e.matmul(banks[5][:, j * 128:(j + 1) * 128], lhsT=hT[:, k, j * 128:(j + 1) * 128],
                                           rhs=W[:, k, V0:V0 + 128], start=(k == 0), stop=(k == 7))
                    return ins
                S.add("pe", vmm, reads=[("hT", j) for j in blks] + WALL, writes=[BK(5)])
                vb0 = 0 if mode == "meta" else b0
                S.add("act", lambda e: e.activation(out=vdst[:, vb0:vb0 + nblk, :, 0:64],
                                                    in_=banks[5][:, 0:N].rearrange("p (j g d) -> p j g d", j=nblk, g=2),
                                                    func=AF.Copy),
                      reads=[BK(5)], writes=[vres(j) for j in blks])
            jobs.append(job_v)

            def job_k(g):
                b = proj_chunk(KD0 + 128 * g, N)
                qknorm(b, N, 1, kdst[:, g, c0:c0 + N], [kres(j) for j in blks])

            if mode == "meta":
                for g in range(2):
                    jobs.append((lambda g: lambda: job_k(g))(g))
                return jobs

            def job_glu(j):
                if mode == "halo":
                    lo = 96 if b0 == 0 else 0
                    n_ = 32
                else:
                    lo, n_ = 0, N
                bg = proj_chunk(CG0 + 128 * j, n_, lo)
                t = nxt("th", 2)
                S.add("act", lambda e: e.activation(out=th[t][:, 0:n_], in_=banks[bg][:, 0:n_], func=AF.Tanh, scale=0.5),
                      reads=[BK(bg)], writes=[("th", t)])
                bv = proj_chunk(CV0 + 128 * j, n_, lo)
                S.add("dve", lambda e: e.scalar_tensor_tensor(out=uT[:, j, c0 + lo:c0 + lo + n_], in0=th[t][:, 0:n_],
                                                              scalar=1.0, in1=banks[bv][:, 0:n_], op0=ALU.add, op1=ALU.mult),
                      reads=[("th", t), BK(bv)], writes=[("uT", b0 + jj) for jj in blks])
                if j == 3:
                    for jj in blks:
                        u_ready.add((seg, b0 + jj))

            def job_ga(j):
                b = proj_chunk(GA0 + 128 * j, N)
                silu2(b, N, gaT[qslot][:, j, 0:N], [("gaT", qslot, jj) for jj in blks])

            def job_gc(j):
                b = proj_chunk(GC0 + 128 * j, N)
                silu2(b, N, gcT[qslot][:, j, 0:N], [("gcT", qslot)])

            def job_q(j):
                b = proj_chunk(Q0 + 128 * j, N)
                qknorm(b, N, 0, qT[qslot][:, 0:nblk, j, :], [("qT", qslot, jj) for jj in blks], view3=True)

            for j in range(4):
                jobs.append((lambda j: lambda: job_glu(j))(j))
            if mode == "full":
                for j in range(4):
                    jobs.append((lambda j: lambda: job_ga(j))(j))
                for j in range(4):
                    jobs.append((lambda j: lambda: job_gc(j))(j))
            for g in range(2):
                jobs.append((lambda g: lambda: job_k(g))(g))
            if mode == "full":
                for j in range(4):
                    jobs.append((lambda j: lambda: job_q(j))(j))
            return jobs

        def att_scores(seg, qslot, jq, bq):
            qc = jq * 128
            ptiles = {}
            for g in range(2):
                for ty, kb in enumerate((bq - 1, bq, bq + 1)):
                    sb_ = (5, 6, 7)[nxt("st2", 3)]

                    def smm(e, g=g, ty=ty, kb=kb, sb_=sb_):
                        e.matmul(banks[sb_][:, 0:256], lhsT=kTd[0:64, g, kb * 128:(kb + 1) * 128],
                                 rhs=qT[qslot][0:64, jq, 2 * g:2 * g + 2, :], start=True, stop=False,
                                 skip_group_check=True)
                        e.matmul(banks[sb_][:, 0:512], lhsT=IDN[:], rhs=BIAS[:, ty, 4 * g:4 * g + 4, :],
                                 start=False, stop=False, skip_group_check=True)
                        return e.matmul(banks[sb_][:, 256:512], lhsT=kTd[64:128, g, kb * 128:(kb + 1) * 128],
                                        rhs=qT[qslot][64:128, jq, 2 * g:2 * g + 2, :], start=False, stop=True,
                                        skip_group_check=True)
                    S.add("pe", smm, reads=[("kTd", kb), ("qT", qslot, jq), "IDN", "BIAS"], writes=[BK(sb_)])
                    p = nxt("pt", NPT)
                    ptiles[(g, ty)] = p
                    if kb == 0 or kb == NB + 1:
                        col = 2 * seg + (0 if kb == 0 else 1)
                        S.add("act", (lambda sb_, p, col: lambda e: e.activation(
                            out=PT[p][:], in_=banks[sb_][:], func=AF.Exp, bias=hb[:, col:col + 1]))(sb_, p, col),
                            reads=[BK(sb_), "hb"], writes=[("PT", p)])
                    else:
                        S.add("act", (lambda sb_, p: lambda e: e.activation(out=PT[p][:], in_=banks[sb_][:],
                                                                             func=AF.Exp))(sb_, p),
                              reads=[BK(sb_)], writes=[("PT", p)])
            sbm = (5, 6, 7)[nxt("st2", 3)]

            def mmm(e):
                ins = None
                for g in range(2):
                    for a in range(2):
                        ins = e.matmul(banks[sbm][32 * g:32 * g + 16, a * 256:(a + 1) * 256],
                                       lhsT=kTmZ[:, g, a, 32 * g:32 * g + 16],
                                       rhs=qT[qslot][:, jq, 2 * g:2 * g + 2, :], start=True, stop=True)
                return ins
            S.add("pe", mmm, reads=[("kTmZ",), ("qT", qslot, jq)], writes=[BK(sbm)])
            pm = nxt("ptm", 2)
            S.add("act", lambda e: e.activation(out=PTm[pm][0:48, :], in_=banks[sbm][0:48, :], func=AF.Exp),
                  reads=[BK(sbm)], writes=[("PTm", pm)])
            return dict(seg=seg, qslot=qslot, jq=jq, bq=bq, qc=qc, ptiles=ptiles, pm=pm)

        def att_pv(cx):
            qslot, jq, bq, qc, ptiles, pm = cx["qslot"], cx["jq"], cx["bq"], cx["qc"], cx["ptiles"], cx["pm"]
            for g in range(2):
                pvb = g

                def pv(e, g=g, pvb=pvb):
                    ins = None
                    for hl in range(4):
                        a, j = hl % 2, hl // 2
                        c_ = a * 256 + j * 128
                        for ty, kb in enumerate((bq - 1, bq, bq + 1)):
                            e.matmul(banks[pvb][:, hl * 65:hl * 65 + 65], lhsT=PT[ptiles[(g, ty)]][:, c_:c_ + 128],
                                     rhs=vaug[:, kb, g, 0:65], start=(ty == 0), stop=False)
                        ins = e.matmul(banks[pvb][:, hl * 65:hl * 65 + 65], lhsT=PTm[pm][32 * g:32 * g + 16, c_:c_ + 128],
                                       rhs=vm[32 * g:32 * g + 16, 0, g, 0:65], start=False, stop=True)
                    return ins
                S.add("pe", pv, reads=[("PT", ptiles[(g, ty)]) for ty in range(3)] + [("PTm", pm), ("vm",), ("vm1",),
                                                                                      ("vaug1",)]
                      + [("vaug", kb) for kb in (bq - 1, bq, bq + 1)], writes=[BK(pvb)])
            o = nxt("on", 2)
            for g in range(2):
                S.add("dve", (lambda g: lambda e: e.tensor_tensor(
                    out=den[o][:, 4 * g:4 * g + 4], in0=banks[g][:, 64:260:65], in1=esink[:, 4 * g:4 * g + 4],
                    op=ALU.add))(g), reads=[BK(g), "esink"], writes=[("den", o, g)])
            S.add("dve", lambda e: e.reciprocal(out=den[o][:], in_=den[o][:]),
                  reads=[("den", o, 0), ("den", o, 1)], writes=[("den", o, 0), ("den", o, 1)])
            for g in range(2):
                S.add("dve", (lambda g: lambda e: e.tensor_tensor(
                    out=On[o][:, 256 * g:256 * g + 256].rearrange("p (h d) -> p h d", h=4),
                    in0=banks[g][:, 0:260].rearrange("p (h d) -> p h d", h=4)[:, :, 0:64],
                    in1=den[o][:, 4 * g:4 * g + 4].unsqueeze(2).to_broadcast([128, 4, 64]), op=ALU.mult))(g),
                    reads=[BK(g), ("den", o, 0), ("den", o, 1)], writes=[("On", o, g)])
            cx["o"] = o

        def att_out(cx):
            qslot, jq, qc, o = cx["qslot"], cx["jq"], cx["qc"], cx["o"]
            otb = banks[4][:].bitcast(BF16)

            def otr(e):
                ins = None
                for j in range(4):
                    ins = e.transpose(out=otb[:, j * 128:(j + 1) * 128], in_=On[o][:, j * 128:(j + 1) * 128],
                                      identity=IDN[:])
                return ins
            S.add("pe", otr, reads=[("On", o, 0), ("On", o, 1), "IDN"], writes=[BK(4)])
            S.add("dve", lambda e: e.tensor_tensor(out=mixT[:, 0:4, qc:qc + 128],
                                                   in0=otb[:, 0:512].rearrange("p (j t) -> p j t", j=4),
                                                   in1=gaT[qslot][:, 0:4, qc:qc + 128], op=ALU.mult),
                  reads=[BK(4), ("gaT", qslot, jq)], writes=[("mixA", jq)])

        def conv_mm(b0, cc):
            ublk = [("uT", b) for b in range(b0, b0 + 5)]
            c0 = b0 * 128
            cb = 6 + nxt("cv", 2)

            def cmm(e, cc=cc, cb=cb):
                ins = None
                for k in range(16, CONV_K):
                    ins = e.matmul(banks[cb][:], lhsT=DG[:, cc, k - 16, :], rhs=uT[:, cc, c0 + k - 15:c0 + k - 15 + 512],
                                   start=(k == 16), stop=(k == CONV_K - 1))
                return ins
            S.add("pe", cmm, reads=ublk + ["DG"], writes=[BK(cb)])
            S.add("dve", (lambda cc, cb: lambda e: e.tensor_tensor(out=acc[:, cc, :], in0=banks[cb][:], in1=acc[:, cc, :],
                                                                    op=ALU.add))(cc, cb),
                  reads=[BK(cb), ("acc", cc)], writes=[("acc", cc)])
            S.add("act", (lambda cc: lambda e: e.activation(out=yb[:, cc, :], in_=acc[:, cc, :], func=AF.Identity,
                                                             bias=v4[:, cc:cc + 1]))(cc),
                  reads=[("acc", cc), "v4"], writes=[("yb", cc)])
            ys = nxt("ysq", 4)
            S.add("act", (lambda cc, ys: lambda e: e.activation(out=ysq[ys][:], in_=acc[:, cc, :], func=AF.Square,
                                                                 bias=v4[:, cc:cc + 1]))(cc, ys),
                  reads=[("acc", cc), "v4"], writes=[("ysq", ys)])
            pending_stat.append((cc, ys))

        mac_q = []
        in_drain = [False]

        def conv_queue(seg, b0):
            c0 = b0 * 128
            for k in range(0, 16):
                for cc in range(4):
                    lo = c0 + k - 15
                    need = {(seg, b) for b in range(lo // 128, (lo + 511) // 128 + 1)}
                    j = cc * CONV_K + k
                    rd = [("uT", b) for (_, b) in need] + [("acc", cc), "cwp"]
                    if k == 0:
                        fn = (lambda cc, lo, j: lambda e: e.tensor_scalar(out=acc[:, cc, :], in0=uT[:, cc, lo:lo + 512],
                                                                           scalar1=cwp[:, j:j + 1], scalar2=None,
                                                                           op0=ALU.mult))(cc, lo, j)
                    else:
                        fn = (lambda cc, lo, j: lambda e: e.scalar_tensor_tensor(out=acc[:, cc, :], in0=uT[:, cc, lo:lo + 512],
                                                                                  scalar=cwp[:, j:j + 1], in1=acc[:, cc, :],
                                                                                  op0=ALU.mult, op1=ALU.add))(cc, lo, j)
                    mac_q.append((need, fn, rd, cc))

        def drain(n):
            if in_drain[0]:
                return
            in_drain[0] = True
            while n > 0 and mac_q and mac_q[0][0] <= u_ready:
                need, fn, rd, cc = mac_q.pop(0)
                S.add("dve", fn, reads=rd, writes=[("acc", cc)])
                n -= 1
            in_drain[0] = False

        _add = S.add

        def add_hook(eng, fn, reads=(), writes=(), dma=False):
            op = _add(eng, fn, reads=reads, writes=writes, dma=dma)
            if eng == "dve" and not dma and not in_drain[0]:
                drain(1)
            return op
        S.add = add_hook

        pending_stat = []

        def conv_statmm():
            while pending_stat:
                cc, ys = pending_stat.pop(0)
                S.add("pe", (lambda cc: lambda e: e.matmul(banks[2][:], lhsT=ONL[:], rhs=yb[:, cc, :], start=(cc == 0),
                                                            stop=(cc == 3)))(cc),
                      reads=[("yb", cc), "ONL"], writes=[BK(2)])
                S.add("pe", (lambda cc, ys: lambda e: e.matmul(banks[3][:], lhsT=ONL[:], rhs=ysq[ys][:], start=(cc == 0),
                                                                stop=(cc == 3)))(cc, ys),
                      reads=[("ysq", ys), "ONL"], writes=[BK(3)])

        def conv_stats():
            S.add("act", lambda e: e.activation(out=m2[:], in_=banks[2][:], func=AF.Square), reads=[BK(2)], writes=["m2"])
            S.add("dve", lambda e: e.scalar_tensor_tensor(out=lnr[:], in0=banks[3][:], scalar=1e-5, in1=m2[:],
                                                          op0=ALU.add, op1=ALU.subtract),
                  reads=[BK(3), "m2"], writes=["lnr"])
            S.add("act", lambda e: e.activation(out=lnr[:], in_=lnr[:], func=AF.Ln), reads=["lnr"], writes=["lnr"])
            S.add("act", lambda e: e.activation(out=lnr[:], in_=lnr[:], func=AF.Exp, scale=-0.5), reads=["lnr"], writes=["lnr"])
            S.add("dve", lambda e: e.scalar_tensor_tensor(out=nmr[:], in0=banks[2][:], scalar=-1.0, in1=lnr[:],
                                                          op0=ALU.mult, op1=ALU.mult),
                  reads=[BK(2), "lnr", "m2"], writes=["m2"])

        def conv_epi(qslot, cc):
            if True:
                S.add("dve", (lambda cc: lambda e: e.tensor_tensor(out=t2[:], in0=yb[:, cc, :], in1=lnr[:],
                                                                    op=ALU.mult))(cc),
                      reads=[("yb", cc), "lnr"], writes=["t2"])
                S.add("dve", lambda e: e.tensor_tensor(out=t2[:], in0=t2[:], in1=nmr[:], op=ALU.add),
                      reads=["t2", "m2"], writes=["t2"])
                t = nxt("th", 2)
                S.add("act", (lambda cc: lambda e: e.activation(out=ta[:], in_=t2[:], func=AF.Identity,
                                                                 scale=v4[:, 4 + cc:5 + cc], bias=v4[:, 8 + cc:9 + cc]))(cc),
                      reads=["t2", "v4"], writes=["ta"])
                S.add("act", (lambda cc, t: lambda e: e.activation(out=th[t][:], in_=t2[:], func=AF.Tanh,
                                                                    scale=v4h[:, cc:cc + 1], bias=v4h[:, 4 + cc:5 + cc]))(cc, t),
                      reads=["t2", "v4h"], writes=[("th", t)])
                S.add("dve", (lambda t: lambda e: e.scalar_tensor_tensor(out=ta[:], in0=th[t][:], scalar=1.0, in1=ta[:],
                                                                          op0=ALU.add, op1=ALU.mult))(t),
                      reads=[("th", t), "ta"], writes=["ta"])
                S.add("dve", (lambda cc: lambda e: e.tensor_tensor(out=yb[:, cc, :], in0=ta[:],
                                                                    in1=gcT[qslot][:, cc, :], op=ALU.mult))(cc),
                      reads=["ta", ("gcT", qslot)], writes=[("yb", cc)])

        def outproj_block(seg, jq, bq):
            qc = jq * 128
            r = nxt("xt", 3)
            dma(xres[r][:], xe[seg, bq * 128:(bq + 1) * 128, :], writes=[("xt", r)])
            for half in range(2):
                ob = 6 + half

                def omm(e, half=half, ob=ob):
                    ins = None
                    for k in range(8):
                        lt = mixT[:, k, qc:qc + 128] if k < 4 else yb[:, k - 4, qc:qc + 128]
                        ins = e.matmul(banks[ob][:], lhsT=lt,
                                       rhs=WO[:, k, half * 512:(half + 1) * 512], start=(k == 0), stop=(k == 7))
                    return ins
                S.add("pe", omm, reads=[("mixA", jq)] + [("yb", cc) for cc in range(4)] + WOALL, writes=[BK(ob)])
                S.add("dve", (lambda half, ob: lambda e: e.tensor_tensor(
                    out=xres[r][:, half * 512:(half + 1) * 512], in0=banks[ob][:],
                    in1=xres[r][:, half * 512:(half + 1) * 512], op=ALU.add))(half, ob),
                    reads=[BK(ob), ("xt", r)], writes=[("xt", r)])
            S.add("pool", lambda e: e.dma_start(out=y[seg, (bq - 1) * 128:bq * 128, :], in_=xres[r][:]),
                  reads=[("xt", r)], dma=True)

        if STOP >= 1:
            norm_T(norm_pre(lambda j: xmeta, 1))
            for jb in projA_jobs(0, 1, "meta"):
                jb()
        S.add("dve", lambda e: e.memset(kTmZ[:], 0.0), writes=[("kTmZ",)])
        for g in range(2):
            for a in range(2):
                S.add("dve", (lambda g, a: lambda e: e.tensor_copy(out=kTmZ[64 * a:64 * a + 64, g, a, :],
                                                                   in_=kTm[64 * a:64 * a + 64, g, 0:48]))(g, a),
                      reads=[("kTm",)], writes=[("kTmZ",)])

        A_list = []
        for seg in range(NSEG if STOP >= 2 else 0):
            A_list.append((seg, 0, "halo", 0, 1))
            for i in range(NST):
                A_list.append((seg, i + 1, "full", 1 + 4 * i, 4))
            A_list.append((seg, NST + 1, "halo", NB + 1, 1))
        B_list = [(seg, i) for seg in range(NSEG if STOP >= 2 else 0) for i in range(NST)]
        pa = [0]
        slot_of = {}

        def src_of(entry):
            seg, k, mode, b0, nblk = entry
            return (lambda seg, b0: lambda j: xe[seg, (b0 + j) * 128:(b0 + j + 1) * 128, :])(seg, b0)

        pre_slots = {}

        def do_pre(idx):
            if idx < len(A_list) and idx not in pre_slots:
                pre_slots[idx] = norm_pre(src_of(A_list[idx]), A_list[idx][4])

        def A_jobs(idx):
            seg, k, mode, b0, nblk = A_list[idx]
            qs = None
            if mode == "full":
                qs = nxt("qslot", 2)
                slot_of[(seg, k)] = qs
            pj = projA_jobs(b0, nblk, mode, qs, seg)

            def first():
                do_pre(idx)
                norm_T(pre_slots[idx])
            return [first, pj[0]] + pj[1:5] + [lambda: do_pre(idx + 1)] + pj[5:]

        def need_idx(seg, i):
            return seg * (NST + 2) + i + 2

        def first_slot(entry, seg, b0):
            eseg, k, mode, eb0, nblk = entry
            if eseg == seg:
                return 0
            last = -1
            for jq in range(4):
                rd = range(b0 + jq - 1, b0 + jq + 2)
                if any(eb0 <= r < eb0 + nblk for r in rd):
                    last = jq
            return last + 1

        for bi, (seg, i) in enumerate(B_list):
            while pa[0] <= need_idx(seg, i):
                for jb in A_jobs(pa[0]):
                    jb()
                pa[0] += 1
            qslot = slot_of[(seg, i + 1)]
            b0 = 1 + 4 * i
            slots = [[] for _ in range(5)]
            if bi + 1 < len(B_list):
                nseg, ni = B_list[bi + 1]
                tgt = need_idx(nseg, ni)
                pend = []
                while pa[0] <= tgt:
                    pend.append(pa[0])
                    pa[0] += 1
                queue = []
                post = False
                for idx in pend:
                    fs = first_slot(A_list[idx], seg, b0)
                    post = post or fs >= 4
                    for jb in A_jobs(idx):
                        queue.append((4 if post else fs, jb))
                n_in = sum(1 for fs, _ in queue if fs < 4)
                quota = max(1, -(-n_in // 4))
                cur, cnt_ = 0, 0
                for fs, jb in queue:
                    if fs >= 4:
                        slots[4].append(jb)
                        continue
                    if fs > cur:
                        cur, cnt_ = fs, 0
                    if cnt_ >= quota and cur < 3:
                        cur, cnt_ = cur + 1, 0
                    slots[cur].append(jb)
                    cnt_ += 1
            if bi == 0:
                conv_queue(seg, b0)
                drain(10 ** 9)
                assert not mac_q
            for cc in range(4):
                conv_mm(b0, cc)
                if cc >= 1:
                    pass
                if cc >= 1:
                    last = pending_stat.pop()
                    conv_statmm()
                    pending_stat.append(last)
            if bi + 1 < len(B_list):
                nseg, ni = B_list[bi + 1]
                conv_queue(nseg, 1 + 4 * ni)
            cxs = []
            for jq in range(4):
                cx = att_scores(seg, qslot, jq, b0 + jq)
                cxs.append(cx)
                if jq == 0:
                    conv_statmm()
                    conv_stats()
                if jq >= 1:
                    att_out(cxs[jq - 1])
                if jq == 3:
                    conv_epi(qslot, jq)
                for jb in slots[jq]:
                    jb()
                if jq < 3:
                    conv_epi(qslot, jq)
                att_pv(cx)
            for jq in range(2):
                outproj_block(seg, jq, b0 + jq)
            att_out(cxs[3])
            for jb in slots[4]:
                jb()
            for jq in range(2, 4):
                outproj_block(seg, jq, b0 + jq)
            drain(10 ** 9)
            assert not mac_q, "conv MACs left whose u blocks were never produced"

        S.finalize(sems, dsems)
        with nc.Block() as block:
            @block.tensor
            def _(e):
                S.emit("pe", e)

            @block.scalar
            def _(e):
                S.emit("act", e)

            @block.vector
            def _(e):
                S.emit("dve", e)

            @block.gpsimd
            def _(e):
                S.emit("pool", e)

            @block.sync
            def _(e):
                S.emit("sp", e)
                S.final_waits(e)
    return nc


def _const_tables():
    i = np.arange(128)
    jj, ii = np.meshgrid(i, i, indexing="ij")
    slopes = np.exp2(-8.0 * np.arange(1, N_HEADS + 1) / N_HEADS).astype(np.float32)
    tab = np.empty((128, 3, N_HEADS, 128), np.float32)
    dl, vl = 128 + ii - jj, jj >= ii
    dm = np.abs(ii - jj)
    dr, vr = 128 + jj - ii, jj <= ii
    order = [0, 2, 1, 3, 4, 6, 5, 7]
    for sl, h in enumerate(order):
        tab[:, 0, sl, :] = np.where(vl, -slopes[h] * dl, NEG)
        tab[:, 1, sl, :] = -slopes[h] * dm
        tab[:, 2, sl, :] = np.where(vr, -slopes[h] * dr, NEG)
    bf = ml_dtypes.bfloat16
    ident = np.eye(128, dtype=np.float32).astype(bf)
    bd = np.kron(np.eye(2, dtype=np.float32), np.full((64, 64), 1.0 / 64, np.float32)).astype(bf)
    onesln = np.full((128, 128), 1.0 / 512, np.float32).astype(bf)
    return tab.reshape(128, -1).astype(bf), ident, bd, onesln


def _segment(xseq, meta, t0, n):
    S_ = xseq.shape[0]
    if t0 > 0:
        left, hl = xseq[t0 - 128:t0], 0.0
    else:
        left = np.concatenate([np.zeros((128 - N_META, D_MODEL), np.float32), meta], 0)
        hl = NEG
    if t0 + n < S_:
        right, hr = xseq[t0 + n:t0 + n + 128], 0.0
    else:
        right, hr = np.zeros((128, D_MODEL), np.float32), NEG
    return np.concatenate([left, xseq[t0:t0 + n], right], 0), hl, hr


def _core_inputs(segs, meta, weights, consts):
    xs, hbv = [], []
    for (xseq, t0, n) in segs:
        x_, hl, hr = _segment(xseq, meta, t0, n)
        xs.append(x_)
        hbv += [hl, hr]
    xmeta = np.zeros((128, D_MODEL), np.float32)
    xmeta[0:N_META] = meta
    xmeta[32:32 + N_META] = meta
    d = dict(weights)
    d.update(consts)
    d["xe"] = np.ascontiguousarray(np.stack(xs, 0))
    d["xmeta"] = xmeta
    d["hb"] = np.ascontiguousarray(np.broadcast_to(np.asarray(hbv, np.float32)[None, :], (128, len(hbv))))
    return d


def _pack_weights(norm_w, w_in, q_norm_w, k_norm_w, sink_logits, conv_w, conv_b, conv_ln_w, conv_ln_b, w_out):
    f = np.float32
    return {
        "w_in": np.ascontiguousarray(w_in[0], f),
        "w_out": np.ascontiguousarray(w_out[0], f),
        "normw_pk": np.ascontiguousarray(norm_w[0].reshape(8, 128).T, f),
        "qkw": np.ascontiguousarray(np.stack([np.tile(q_norm_w[0], 2), np.tile(k_norm_w[0], 2)], 1), f),
        "convw_pk": np.ascontiguousarray(conv_w[0].reshape(CONV_K, 4, 128).transpose(2, 1, 0).reshape(128, 4 * CONV_K), f),
        "vec4": np.ascontiguousarray(np.concatenate([conv_b[0].reshape(4, 128).T, conv_ln_w[0].reshape(4, 128).T,
                                                     conv_ln_b[0].reshape(4, 128).T], 1), f),
        "sink_b": np.ascontiguousarray(np.broadcast_to(sink_logits[0][None, :], (128, N_HEADS)), f),
    }


_NC_CACHE = {}


def kernel(x_prompt, x_sample, meta_tokens, norm_w, w_in, q_norm_w, k_norm_w, sink_logits,
           conv_w, conv_b, conv_ln_w, conv_ln_b, w_out):
    NB, NSEG, NCORE = 8, 8, 8
    n = NB * 128
    xp = np.asarray(x_prompt, np.float32)
    xs = np.asarray(x_sample, np.float32)
    meta = np.asarray(meta_tokens, np.float32)
    weights = _pack_weights(*[np.asarray(a, np.float32) for a in
                              (norm_w, w_in, q_norm_w, k_norm_w, sink_logits, conv_w, conv_b, conv_ln_w, conv_ln_b, w_out)])
    tab, ident, bd, onesln = _const_tables()
    consts = {"bias_c": tab, "ident_c": ident, "bd_c": bd, "onesln_c": onesln}
    in_maps, place = [], []
    for c in range(NCORE):
        segs, pl = [], []
        for sq_ in (2 * c, 2 * c + 1):
            for t0 in (0, n):
                segs.append((xp[sq_], t0, n))
                pl.append((0, sq_, t0))
        sseq, base = c // 4, (c % 4) * 4 * n
        for q in range(4):
            segs.append((xs[sseq], base + q * n, n))
            pl.append((1, sseq, base + q * n))
        in_maps.append(_core_inputs(segs, meta, weights, consts))
        place.append(pl)
    key = (NSEG, NB)
    if key not in _NC_CACHE:
        _NC_CACHE[key] = _build(NSEG, NB)
    res = run_bass_kernel_spmd(_NC_CACHE[key], in_maps, core_ids=list(range(NCORE)))
    yp = np.empty(xp.shape, np.float32)
    ys = np.empty(xs.shape, np.float32)
    for c in range(NCORE):
        yc = res.results[c]["y"]
        for s_, (which, sq_, t0) in enumerate(place[c]):
            (yp if which == 0 else ys)[sq_, t0:t0 + n] = yc[s_]
    return (yp, ys)
```

```python
import math
from contextlib import ExitStack

import numpy as np
import ml_dtypes
import concourse.bass as bass
import concourse.mybir as mybir
from concourse.bass_utils import run_bass_kernel_spmd

F32 = mybir.dt.float32
BF16 = mybir.dt.bfloat16
AF = mybir.ActivationFunctionType
ALU = mybir.AluOpType

D_MODEL = 1024
N_META = 16
N_HEADS = 8
CONV_K = 31
NEG = -30000.0

Q0, KD0, KD1, V0, GA0, CV0, CG0, GC0, WCOLS = 0, 512, 640, 768, 896, 1408, 1920, 2432, 2944


class _Op:
    __slots__ = ("eng", "fn", "deps", "signal", "sem", "val", "dma", "waits", "prev_val")

    def __init__(self, eng, fn, dma):
        self.eng = eng
        self.fn = fn
        self.dma = dma
        self.deps = set()
        self.signal = False
        self.sem = None
        self.val = 0
        self.prev_val = 0
        self.waits = []


class Sched:
    ENGS = ("pe", "act", "dve", "pool", "sp")

    def __init__(self):
        self.ops = {e: [] for e in self.ENGS}
        self.last_writer = {}
        self.readers = {}

    def add(self, eng, fn, reads=(), writes=(), dma=False):
        op = _Op(eng, fn, dma)
        deps = op.deps
        for r in reads:
            w = self.last_writer.get(r)
            if w is not None:
                deps.add(w)
        for w_ in writes:
            w = self.last_writer.get(w_)
            if w is not None:
                deps.add(w)
            for rd in self.readers.get(w_, ()):
                deps.add(rd)
        for r in reads:
            self.readers.setdefault(r, []).append(op)
        for w_ in writes:
            self.last_writer[w_] = op
            self.readers[w_] = []
        deps.discard(op)
        if eng == "pe" and not dma:
            op.deps = {d for d in deps if not (d.eng == "pe" and not d.dma)}
        for d in op.deps:
            d.signal = True
        self.ops[eng].append(op)
        return op

    def finalize(self, eng_sems, dma_sems):
        for e in self.ENGS:
            cnt = 0
            dcount = {}
            k = 0
            pool = dma_sems.get(e, [])
            for op in self.ops[e]:
                if op.dma:
                    s = pool[k % len(pool)]
                    k += 1
                    prev = dcount.get(id(s), 0)
                    op.sem = s
                    op.prev_val = prev
                    op.val = prev + 16
                    dcount[id(s)] = op.val
                elif op.signal:
                    cnt += 1
                    op.sem = eng_sems[e]
                    op.val = cnt
        for e in self.ENGS:
            waited = {}
            for op in self.ops[e]:
                need = {}
                for d in op.deps:
                    key = id(d.sem)
                    if d.val > need.get(key, (None, 0))[1]:
                        need[key] = (d.sem, d.val)
                if op.dma and op.prev_val > 0:
                    key = id(op.sem)
                    if op.prev_val > need.get(key, (None, 0))[1]:
                        need[key] = (op.sem, op.prev_val)
                for key, (s, v) in need.items():
                    if v > waited.get(key, 0):
                        waited[key] = v
                        op.waits.append((s, v))

    def emit(self, eng_name, e):
        for op in self.ops[eng_name]:
            for (s, v) in op.waits:
                e.wait_ge(s, v)
            ins = op.fn(e)
            if op.dma:
                ins.then_inc(op.sem, 16)
            elif op.signal:
                ins.then_inc(op.sem, 1)

    def final_waits(self, e):
        last = {}
        for q in self.ENGS:
            for op in self.ops[q]:
                if op.dma:
                    last[id(op.sem)] = (op.sem, op.val)
        for s, v in last.values():
            e.wait_ge(s, v)


def _build(NSEG, NB, STOP=9):
    assert NB % 4 == 0
    NST = NB // 4
    NBX = NB + 2
    TX = NBX * 128
    nc = bass.Bass("TRN2", target_bir_lowering=False)

    def din(name, shape, dt=F32):
        return nc.dram_tensor(name, shape, dt, kind="ExternalInput").ap()

    xe = din("xe", [NSEG, TX, D_MODEL])
    xmeta = din("xmeta", [128, D_MODEL])
    w_in = din("w_in", [D_MODEL, 2816])
    w_out = din("w_out", [D_MODEL, D_MODEL])
    normw_pk = din("normw_pk", [128, 8])
    qkw = din("qkw", [128, 2])
    convw_pk = din("convw_pk", [128, 4 * CONV_K])
    vec4 = din("vec4", [128, 12])
    sink_b = din("sink_b", [128, 8])
    hb_in = din("hb", [128, 2 * NSEG])
    ident_c = din("ident_c", [128, 128], BF16)
    bd_c = din("bd_c", [128, 128], BF16)
    onesln_c = din("onesln_c", [128, 128], BF16)
    bias_c = din("bias_c", [128, 3 * 8 * 128], BF16)
    y = nc.dram_tensor("y", [NSEG, NB * 128, D_MODEL], F32, kind="ExternalOutput").ap()

    S = Sched()
    with ExitStack() as st:
        E = st.enter_context

        def sb(name, shape, dt=F32):
            return E(nc.sbuf_tensor("sb_" + name, shape, dt))

        W = sb("W", [128, 8, WCOLS], BF16)
        WO = sb("WO", [128, 8, D_MODEL], BF16)
        DG = sb("DG", [128, 4, 15, 128], BF16)
        BIAS = sb("BIAS", [128, 3, 8, 128], BF16)
        IDN = sb("IDN", [128, 128], BF16)
        BD = sb("BD", [128, 128], BF16)
        ONL = sb("ONL", [128, 128], BF16)
        nwp = sb("nwp", [128, 8])
        qkv = sb("qkv", [128, 2])
        qks = sb("qks", [128, 2])
        cwp = sb("cwp", [128, 4 * CONV_K])
        v4 = sb("v4", [128, 12])
        v4h = sb("v4h", [128, 8])
        snk = sb("snk", [128, 8])
        esink = sb("esink", [128, 8])
        hb = sb("hb", [128, 2 * NSEG])
        mhalf = sb("mhalf", [128, 1])
        epsc = sb("epsc", [128, 1])
        kTd = sb("kTd", [128, 2, TX], BF16)
        vaug = sb("vaug", [128, NBX, 2, 66], BF16)
        uT = sb("uT", [128, 4, TX], BF16)
        kTm = sb("kTm", [128, 2, 128], BF16)
        vm = sb("vm", [128, 1, 2, 66], BF16)
        kTmZ = sb("kTmZ", [128, 2, 2, 48], BF16)
        hT = sb("hT", [128, 8, 512], BF16)
        qT = [sb(f"qT{i}", [128, 4, 4, 128], BF16) for i in range(2)]
        gaT = [sb(f"gaT{i}", [128, 4, 512], BF16) for i in range(2)]
        gcT = [sb(f"gcT{i}", [128, 4, 512], BF16) for i in range(2)]
        mixT = sb("mixT", [128, 4, 512], BF16)
        xt = [sb(f"xt{i}", [128, D_MODEL]) for i in range(3)]
        xres = xt
        htm = [sb(f"htm{i}", [128, D_MODEL], BF16) for i in range(4)]
        ss = [sb(f"ss{i}", [128, 1]) for i in range(4)]
        sq = [sb(f"sq{i}", [128, 512], BF16) for i in range(2)]
        tq = [sb(f"tq{i}", [128, 512]) for i in range(2)]
        th = [sb(f"th{i}", [128, 512]) for i in range(2)]
        NPT = 8
        PT = [sb(f"PT{i}", [128, 512], BF16) for i in range(NPT)]
        PTm = [sb(f"PTm{i}", [128, 512], BF16) for i in range(2)]
        On = [sb(f"On{i}", [128, 512], BF16) for i in range(2)]
        den = [sb(f"den{i}", [128, 8]) for i in range(2)]
        yb = sb("yb", [128, 4, 512], BF16)
        acc = sb("acc", [128, 4, 512])
        ysq = [sb(f"ysq{i}", [128, 512], BF16) for i in range(4)]
        lnr = sb("lnr", [128, 512])
        m2 = sb("m2", [128, 512])
        nmr = m2
        t2 = sb("t2", [128, 512])
        ta = sb("ta", [128, 512])

        banks = [E(nc.psum_tensor(f"bank{i}", [128, 512], F32)) for i in range(8)]
        sems = {e: E(nc.semaphore("s_" + e)) for e in Sched.ENGS}
        dsems = {"sp": [E(nc.semaphore(f"d{i}")) for i in range(24)],
                 "pool": [E(nc.semaphore(f"dp{i}")) for i in range(8)]}

        def BK(i):
            return ("bank", i)

        cnt = {"htm": 0, "xt": 0, "xres": 0, "sq": 0, "th": 0, "pt": 0, "ptm": 0, "on": 0, "ysq": 0, "proj": 0,
               "st2": 0, "cv": 0, "qslot": 0}

        def nxt(k, n):
            v = cnt[k] % n
            cnt[k] += 1
            return v

        def dma(out_ap, in_ap, reads=(), writes=()):
            S.add("sp", lambda e: e.dma_start(out=out_ap, in_=in_ap), reads=reads, writes=writes, dma=True)

        dma(nwp[:], normw_pk, writes=["nwp"])
        dma(qkv[:], qkw, writes=["qkv"])
        dma(cwp[:], convw_pk, writes=["cwp"])
        dma(v4[:], vec4, writes=["v4"])
        dma(snk[:], sink_b, writes=["snk"])
        dma(hb[:], hb_in, writes=["hb"])
        dma(IDN[:], ident_c, writes=["IDN"])
        dma(BD[:], bd_c, writes=["BD"])
        dma(ONL[:], onesln_c, writes=["ONL"])
        dma(BIAS[:].rearrange("p a h q -> p (a h q)"), bias_c, writes=["BIAS"])

        S.add("pool", lambda e: e.memset(mhalf[:], -0.5), writes=["mhalf"])
        S.add("pool", lambda e: e.memset(epsc[:], 1e-6), writes=["epsc"])
        S.add("pool", lambda e: e.memset(vaug[:, :, :, 64:66], 1.0), writes=[("vaug1",)])
        S.add("pool", lambda e: e.memset(vm[:, :, :, 64:66], 1.0), writes=[("vm1",)])
        for i in range(8):
            S.add("dve", (lambda i: lambda e: e.memset(banks[i][:], 0.0))(i), writes=[BK(i)])
        S.add("dve", lambda e: e.tensor_scalar(out=qks[:, 0:1], in0=qkv[:, 0:1], scalar1=0.125, scalar2=None,
                                               op0=ALU.mult), reads=["qkv"], writes=["qks0"])
        S.add("dve", lambda e: e.tensor_copy(out=qks[:, 1:2], in_=qkv[:, 1:2]), reads=["qkv"], writes=["qks1"])
        S.add("dve", lambda e: e.tensor_scalar(out=cwp[:], in0=cwp[:], scalar1=0.5, scalar2=None, op0=ALU.mult),
              reads=["cwp"], writes=["cwp"])
        S.add("dve", lambda e: e.tensor_scalar(out=v4h[:], in0=v4[:, 4:12], scalar1=0.5, scalar2=None,
                                               op0=ALU.mult), reads=["v4"], writes=["v4h"])
        S.add("act", lambda e: e.activation(out=esink[:], in_=snk[:], func=AF.Exp), reads=["snk"], writes=["esink"])

        def stage_view(slot):
            return xt[slot][:].rearrange("p (k n) -> p k n", k=8)

        nwb = nwp[:].unsqueeze(2).to_broadcast([128, 8, 128])

        def wpiece(c0):
            slot = nxt("xt", 3)
            sv = stage_view(slot)
            dma(sv, w_in[:, c0:c0 + 128].rearrange("(k p) n -> p k n", p=128), writes=[("xt", slot)])
            if c0 < 512:
                dsts = [(Q0 + c0, 0, 128)]
            elif c0 == 512:
                dsts = [(KD0, 0, 64), (KD0 + 64, 0, 64), (KD1, 64, 64), (KD1 + 64, 64, 64)]
            elif c0 == 640:
                dsts = [(V0, 0, 128)]
            elif c0 < 1280:
                dsts = [(GA0 + c0 - 768, 0, 128)]
            elif c0 < 1792:
                dsts = [(CV0 + c0 - 1280, 0, 128)]
            elif c0 < 2304:
                dsts = [(CG0 + c0 - 1792, 0, 128)]
            else:
                dsts = [(GC0 + c0 - 2304, 0, 128)]
            for (d0, s0, n) in dsts:
                S.add("dve", (lambda d0, s0, n, sv: lambda e: e.tensor_tensor(
                    out=W[:, :, d0:d0 + n], in0=sv[:, :, s0:s0 + n],
                    in1=nwp[:].unsqueeze(2).to_broadcast([128, 8, n]), op=ALU.mult))(d0, s0, n, sv),
                    reads=[("xt", slot), "nwp"], writes=[("W", d0)])

        for c0 in range(0, 2816, 128):
            wpiece(c0)

        def wopiece(c0):
            slot = nxt("xt", 3)
            sv = stage_view(slot)
            dma(sv, w_out[:, c0:c0 + 128].rearrange("(k p) n -> p k n", p=128), writes=[("xt", slot)])
            S.add("dve", lambda e: e.tensor_scalar(out=WO[:, 0:4, c0:c0 + 128], in0=sv[:, 0:4, :], scalar1=0.5,
                                                   scalar2=None, op0=ALU.mult),
                  reads=[("xt", slot)], writes=[("WO", c0, 0)])
            S.add("dve", lambda e: e.tensor_scalar(out=WO[:, 4:8, c0:c0 + 128], in0=sv[:, 4:8, :], scalar1=0.25,
                                                   scalar2=None, op0=ALU.mult),
                  reads=[("xt", slot)], writes=[("WO", c0, 1)])

        for c0 in range(0, D_MODEL, 128):
            wopiece(c0)

        def dgs(e):
            ins = None
            for cc in range(4):
                for k in range(16, CONV_K):
                    j = cc * CONV_K + k
                    ins = e.tensor_scalar(out=DG[:, cc, k - 16, :], in0=IDN[:], scalar1=cwp[:, j:j + 1], scalar2=None,
                                          op0=ALU.mult)
            return ins
        S.add("dve", dgs, reads=["IDN", "cwp"], writes=["DG"])
        WALL = [("W", d) for d in ([Q0 + i for i in range(0, 512, 128)] + [KD0, KD0 + 64, KD1, KD1 + 64, V0]
                                   + [b + i for b in (GA0, CV0, CG0, GC0) for i in range(0, 512, 128)])]
        WOALL = [("WO", c0, h) for c0 in range(0, D_MODEL, 128) for h in range(2)]

        def norm_pre(src, nblk):
            hs = []
            for j in range(nblk):
                slot = nxt("xt", 3)
                h_ = nxt("htm", 4)
                hs.append(h_)
                dma(xt[slot][:], src(j), writes=[("xt", slot)])
                S.add("act", (lambda slot, h_: lambda e: e.activation(out=htm[h_][:], in_=xt[slot][:], func=AF.Square,
                                                                       accum_out=ss[h_][:]))(slot, h_),
                      reads=[("xt", slot)], writes=[("htm", h_), ("ss", h_)])
                S.add("dve", (lambda h_: lambda e: e.tensor_scalar(out=ss[h_][:], in0=ss[h_][:], scalar1=1.0 / D_MODEL,
                                                                    scalar2=1e-6, op0=ALU.mult, op1=ALU.add))(h_),
                      reads=[("ss", h_)], writes=[("ss", h_)])
                S.add("pool", (lambda h_: lambda e: e.tensor_tensor(out=ss[h_][:], in0=ss[h_][:], in1=mhalf[:, 0:1],
                                                                     op=ALU.pow))(h_),
                      reads=[("ss", h_), "mhalf"], writes=[("ss", h_)])
                S.add("dve", (lambda slot, h_: lambda e: e.tensor_scalar(out=htm[h_][:], in0=xt[slot][:], scalar1=ss[h_][:],
                                                                          scalar2=None, op0=ALU.mult))(slot, h_),
                      reads=[("xt", slot), ("ss", h_)], writes=[("htm", h_)])
            return hs

        def norm_T(hs):
            for j, h_ in enumerate(hs):
                tb = (4, 3)[j % 2]
                trb = banks[tb][:].bitcast(BF16)

                def tr(e, h_=h_, trb=trb):
                    ins = None
                    for k in range(8):
                        ins = e.transpose(out=trb[:, k * 128:(k + 1) * 128], in_=htm[h_][:, k * 128:(k + 1) * 128],
                                          identity=IDN[:])
                    return ins
                S.add("pe", tr, reads=[("htm", h_), "IDN"], writes=[BK(tb)])
                S.add("act", (lambda j, trb: lambda e: e.activation(out=hT[:, :, j * 128:(j + 1) * 128],
                                                                     in_=trb.rearrange("p (k t) -> p k t", k=8),
                                                                     func=AF.Copy))(j, trb),
                      reads=[BK(tb)], writes=[("hT", j)])

        def proj_chunk(wcol, N, lo=0):
            b = nxt("proj", 3)

            def mm(e):
                ins = None
                for k in range(8):
                    ins = e.matmul(banks[b][:, 0:N], lhsT=W[:, k, wcol:wcol + 128], rhs=hT[:, k, lo:lo + N],
                                   start=(k == 0), stop=(k == 7))
                return ins
            S.add("pe", mm, reads=[("hT", j) for j in range((lo + N + 127) // 128)] + WALL, writes=[BK(b)])
            return b

        def silu2(b, N, out_ap, out_res):
            t = nxt("th", 2)
            S.add("act", lambda e: e.activation(out=th[t][:, 0:N], in_=banks[b][:, 0:N], func=AF.Tanh, scale=0.5),
                  reads=[BK(b)], writes=[("th", t)])
            S.add("dve", lambda e: e.scalar_tensor_tensor(out=out_ap, in0=th[t][:, 0:N], scalar=1.0,
                                                          in1=banks[b][:, 0:N], op0=ALU.add, op1=ALU.mult),
                  reads=[("th", t), BK(b)], writes=out_res)

        def qknorm(b, N, scol, out_ap, out_res, view3=False):
            s_ = nxt("sq", 2)
            stb = 3
            S.add("act", lambda e: e.activation(out=sq[s_][:, 0:N], in_=banks[b][:, 0:N], func=AF.Square),
                  reads=[BK(b)], writes=[("sq", s_)])
            S.add("pe", lambda e: e.matmul(banks[stb][:, 0:N], lhsT=BD[:], rhs=sq[s_][:, 0:N], start=True, stop=True),
                  reads=[("sq", s_), "BD"], writes=[BK(stb)])
            S.add("act", lambda e: e.activation(out=tq[s_][:, 0:N], in_=banks[stb][:, 0:N], func=AF.Ln, bias=epsc[:, 0:1]),
                  reads=[BK(stb), "epsc"], writes=[("tq", s_)])
            S.add("act", lambda e: e.activation(out=tq[s_][:, 0:N], in_=tq[s_][:, 0:N], func=AF.Exp, scale=-0.5),
                  reads=[("tq", s_)], writes=[("tq", s_)])
            v3 = (lambda ap: ap.rearrange("p (j t) -> p j t", t=128)) if view3 else (lambda ap: ap)
            S.add("dve", lambda e: e.scalar_tensor_tensor(out=out_ap, in0=v3(banks[b][:, 0:N]), scalar=qks[:, scol:scol + 1],
                                                          in1=v3(tq[s_][:, 0:N]), op0=ALU.mult, op1=ALU.mult),
                  reads=[BK(b), ("tq", s_), "qks0", "qks1"], writes=out_res)

        u_ready = set()

        def projA_jobs(b0, nblk, mode, qslot=None, seg=None):
            N = nblk * 128
            if mode == "meta":
                kdst, vdst, kres, vres = kTm, vm, lambda j: ("kTm",), lambda j: ("vm",)
                c0 = 0
            else:
                kdst, vdst, kres, vres = kTd, vaug, lambda j: ("kTd", b0 + j), lambda j: ("vaug", b0 + j)
                c0 = b0 * 128
            blks = range(nblk)
            jobs = []

            def job_v():
                def vmm(e):
                    ins = None
                    for j in range(nblk):
                        for k in range(8):
                            ins = e.matmul(banks[5][:, j * 128:(j + 1) * 128], lhsT=hT[:, k, j * 128:(j + 1) * 128],
                                           rhs=W[:, k, V0:V0 + 128], start=(k == 0), stop=(k == 7))
                    return ins
                S.add("pe", vmm, reads=[("hT", j) for j in blks] + WALL, writes=[BK(5)])
                vb0 = 0 if mode == "meta" else b0
                S.add("act", lambda e: e.activation(out=vdst[:, vb0:vb0 + nblk, :, 0:64],
                                                    in_=banks[5][:, 0:N].rearrange("p (j g d) -> p j g d", j=nblk, g=2),
                                                    func=AF.Copy),
                      reads=[BK(5)], writes=[vres(j) for j in blks])
            jobs.append(job_v)

            def job_k(g):
                b = proj_chunk(KD0 + 128 * g, N)
                qknorm(b, N, 1, kdst[:, g, c0:c0 + N], [kres(j) for j in blks])

            if mode == "meta":
                for g in range(2):
                    jobs.append((lambda g: lambda: job_k(g))(g))
                return jobs

            def job_glu(j):
                if mode == "halo":
                    lo = 96 if b0 == 0 else 0
                    n_ = 32
                else:
                    lo, n_ = 0, N
                bg = proj_chunk(CG0 + 128 * j, n_, lo)
                t = nxt("th", 2)
                S.add("act", lambda e: e.activation(out=th[t][:, 0:n_], in_=banks[bg][:, 0:n_], func=AF.Tanh, scale=0.5),
                      reads=[BK(bg)], writes=[("th", t)])
                bv = proj_chunk(CV0 + 128 * j, n_, lo)
                S.add("dve", lambda e: e.scalar_tensor_tensor(out=uT[:, j, c0 + lo:c0 + lo + n_], in0=th[t][:, 0:n_],
                                                              scalar=1.0, in1=banks[bv][:, 0:n_], op0=ALU.add, op1=ALU.mult),
                      reads=[("th", t), BK(bv)], writes=[("uT", b0 + jj) for jj in blks])
                if j == 3:
                    for jj in blks:
                        u_ready.add((seg, b0 + jj))

            def job_ga(j):
                b = proj_chunk(GA0 + 128 * j, N)
                silu2(b, N, gaT[qslot][:, j, 0:N], [("gaT", qslot, jj) for jj in blks])

            def job_gc(j):
                b = proj_chunk(GC0 + 128 * j, N)
                silu2(b, N, gcT[qslot][:, j, 0:N], [("gcT", qslot)])

            def job_q(j):
                b = proj_chunk(Q0 + 128 * j, N)
                qknorm(b, N, 0, qT[qslot][:, 0:nblk, j, :], [("qT", qslot, jj) for jj in blks], view3=True)

            for j in range(4):
                jobs.append((lambda j: lambda: job_glu(j))(j))
            if mode == "full":
                for j in range(4):
                    jobs.append((lambda j: lambda: job_ga(j))(j))
                for j in range(4):
                    jobs.append((lambda j: lambda: job_gc(j))(j))
            for g in range(2):
                jobs.append((lambda g: lambda: job_k(g))(g))
            if mode == "full":
                for j in range(4):
                    jobs.append((lambda j: lambda: job_q(j))(j))
            return jobs

        def att_scores(seg, qslot, jq, bq):
            qc = jq * 128
            ptiles = {}
            for g in range(2):
                for ty, kb in enumerate((bq - 1, bq, bq + 1)):
                    sb_ = (5, 6, 7)[nxt("st2", 3)]

                    def smm(e, g=g, ty=ty, kb=kb, sb_=sb_):
                        e.matmul(banks[sb_][:, 0:256], lhsT=kTd[0:64, g, kb * 128:(kb + 1) * 128],
                                 rhs=qT[qslot][0:64, jq, 2 * g:2 * g + 2, :], start=True, stop=False,
                                 skip_group_check=True)
                        e.matmul(banks[sb_][:, 0:512], lhsT=IDN[:], rhs=BIAS[:, ty, 4 * g:4 * g + 4, :],
                                 start=False, stop=False, skip_group_check=True)
                        return e.matmul(banks[sb_][:, 256:512], lhsT=kTd[64:128, g, kb * 128:(kb + 1) * 128],
                                        rhs=qT[qslot][64:128, jq, 2 * g:2 * g + 2, :], start=False, stop=True,
                                        skip_group_check=True)
                    S.add("pe", smm, reads=[("kTd", kb), ("qT", qslot, jq), "IDN", "BIAS"], writes=[BK(sb_)])
                    p = nxt("pt", NPT)
                    ptiles[(g, ty)] = p
                    if kb == 0 or kb == NB + 1:
                        col = 2 * seg + (0 if kb == 0 else 1)
                        S.add("act", (lambda sb_, p, col: lambda e: e.activation(
                            out=PT[p][:], in_=banks[sb_][:], func=AF.Exp, bias=hb[:, col:col + 1]))(sb_, p, col),
                            reads=[BK(sb_), "hb"], writes=[("PT", p)])
                    else:
                        S.add("act", (lambda sb_, p: lambda e: e.activation(out=PT[p][:], in_=banks[sb_][:],
                                                                             func=AF.Exp))(sb_, p),
                              reads=[BK(sb_)], writes=[("PT", p)])
            sbm = (5, 6, 7)[nxt("st2", 3)]

            def mmm(e):
                ins = None
                for g in range(2):
                    for a in range(2):
                        ins = e.matmul(banks[sbm][32 * g:32 * g + 16, a * 256:(a + 1) * 256],
                                       lhsT=kTmZ[:, g, a, 32 * g:32 * g + 16],
                                       rhs=qT[qslot][:, jq, 2 * g:2 * g + 2, :], start=True, stop=True)
                return ins
            S.add("pe", mmm, reads=[("kTmZ",), ("qT", qslot, jq)], writes=[BK(sbm)])
            pm = nxt("ptm", 2)
            S.add("act", lambda e: e.activation(out=PTm[pm][0:48, :], in_=banks[sbm][0:48, :], func=AF.Exp),
                  reads=[BK(sbm)], writes=[("PTm", pm)])
            return dict(seg=seg, qslot=qslot, jq=jq, bq=bq, qc=qc, ptiles=ptiles, pm=pm)

        def att_pv(cx):
            qslot, jq, bq, qc, ptiles, pm = cx["qslot"], cx["jq"], cx["bq"], cx["qc"], cx["ptiles"], cx["pm"]
            for g in range(2):
                pvb = g

                def pv(e, g=g, pvb=pvb):
                    ins = None
                    for hl in range(4):
                        a, j = hl % 2, hl // 2
                        c_ = a * 256 + j * 128
                        for ty, kb in enumerate((bq - 1, bq, bq + 1)):
                            e.matmul(banks[pvb][:, hl * 65:hl * 65 + 65], lhsT=PT[ptiles[(g, ty)]][:, c_:c_ + 128],
                                     rhs=vaug[:, kb, g, 0:65], start=(ty == 0), stop=False)
                        ins = e.matmul(banks[pvb][:, hl * 65:hl * 65 + 65], lhsT=PTm[pm][32 * g:32 * g + 16, c_:c_ + 128],
                                       rhs=vm[32 * g:32 * g + 16, 0, g, 0:65], start=False, stop=True)
                    return ins
                S.add("pe", pv, reads=[("PT", ptiles[(g, ty)]) for ty in range(3)] + [("PTm", pm), ("vm",), ("vm1",),
                                                                                      ("vaug1",)]
                      + [("vaug", kb) for kb in (bq - 1, bq, bq + 1)], writes=[BK(pvb)])
            o = nxt("on", 2)
            for g in range(2):
                S.add("dve", (lambda g: lambda e: e.tensor_tensor(
                    out=den[o][:, 4 * g:4 * g + 4], in0=banks[g][:, 64:260:65], in1=esink[:, 4 * g:4 * g + 4],
                    op=ALU.add))(g), reads=[BK(g), "esink"], writes=[("den", o, g)])
            S.add("dve", lambda e: e.reciprocal(out=den[o][:], in_=den[o][:]),
                  reads=[("den", o, 0), ("den", o, 1)], writes=[("den", o, 0), ("den", o, 1)])
            for g in range(2):
                S.add("dve", (lambda g: lambda e: e.tensor_tensor(
                    out=On[o][:, 256 * g:256 * g + 256].rearrange("p (h d) -> p h d", h=4),
                    in0=banks[g][:, 0:260].rearrange("p (h d) -> p h d", h=4)[:, :, 0:64],
                    in1=den[o][:, 4 * g:4 * g + 4].unsqueeze(2).to_broadcast([128, 4, 64]), op=ALU.mult))(g),
                    reads=[BK(g), ("den", o, 0), ("den", o, 1)], writes=[("On", o, g)])
            cx["o"] = o

        def att_out(cx):
            qslot, jq, qc, o = cx["qslot"], cx["jq"], cx["qc"], cx["o"]
            otb = banks[4][:].bitcast(BF16)

            def otr(e):
                ins = None
                for j in range(4):
                    ins = e.transpose(out=otb[:, j * 128:(j + 1) * 128], in_=On[o][:, j * 128:(j + 1) * 128],
                                      identity=IDN[:])
                return ins
            S.add("pe", otr, reads=[("On", o, 0), ("On", o, 1), "IDN"], writes=[BK(4)])
            S.add("dve", lambda e: e.tensor_tensor(out=mixT[:, 0:4, qc:qc + 128],
                                                   in0=otb[:, 0:512].rearrange("p (j t) -> p j t", j=4),
                                                   in1=gaT[qslot][:, 0:4, qc:qc + 128], op=ALU.mult),
                  reads=[BK(4), ("gaT", qslot, jq)], writes=[("mixA", jq)])

        def conv_mm(b0, cc):
            ublk = [("uT", b) for b in range(b0, b0 + 5)]
            c0 = b0 * 128
            cb = 6 + nxt("cv", 2)

            def cmm(e, cc=cc, cb=cb):
                ins = None
                for k in range(16, CONV_K):
                    ins = e.matmul(banks[cb][:], lhsT=DG[:, cc, k - 16, :], rhs=uT[:, cc, c0 + k - 15:c0 + k - 15 + 512],
                                   start=(k == 16), stop=(k == CONV_K - 1))
                return ins
            S.add("pe", cmm, reads=ublk + ["DG"], writes=[BK(cb)])
            S.add("dve", (lambda cc, cb: lambda e: e.tensor_tensor(out=acc[:, cc, :], in0=banks[cb][:], in1=acc[:, cc, :],
                                                                    op=ALU.add))(cc, cb),
                  reads=[BK(cb), ("acc", cc)], writes=[("acc", cc)])
            S.add("act", (lambda cc: lambda e: e.activation(out=yb[:, cc, :], in_=acc[:, cc, :], func=AF.Identity,
                                                             bias=v4[:, cc:cc + 1]))(cc),
                  reads=[("acc", cc), "v4"], writes=[("yb", cc)])
            ys = nxt("ysq", 4)
            S.add("act", (lambda cc, ys: lambda e: e.activation(out=ysq[ys][:], in_=acc[:, cc, :], func=AF.Square,
                                                                 bias=v4[:, cc:cc + 1]))(cc, ys),
                  reads=[("acc", cc), "v4"], writes=[("ysq", ys)])
            pending_stat.append((cc, ys))

        mac_q = []
        in_drain = [False]

        def conv_queue(seg, b0):
            c0 = b0 * 128
            for k in range(0, 16):
                for cc in range(4):
                    lo = c0 + k - 15
                    need = {(seg, b) for b in range(lo // 128, (lo + 511) // 128 + 1)}
                    j = cc * CONV_K + k
                    rd = [("uT", b) for (_, b) in need] + [("acc", cc), "cwp"]
                    if k == 0:
                        fn = (lambda cc, lo, j: lambda e: e.tensor_scalar(out=acc[:, cc, :], in0=uT[:, cc, lo:lo + 512],
                                                                           scalar1=cwp[:, j:j + 1], scalar2=None,
                                                                           op0=ALU.mult))(cc, lo, j)
                    else:
                        fn = (lambda cc, lo, j: lambda e: e.scalar_tensor_tensor(out=acc[:, cc, :], in0=uT[:, cc, lo:lo + 512],
                                                                                  scalar=cwp[:, j:j + 1], in1=acc[:, cc, :],
                                                                                  op0=ALU.mult, op1=ALU.add))(cc, lo, j)
                    mac_q.append((need, fn, rd, cc))

        def drain(n):
            if in_drain[0]:
                return
            in_drain[0] = True
            while n > 0 and mac_q and mac_q[0][0] <= u_ready:
                need, fn, rd, cc = mac_q.pop(0)
                S.add("dve", fn, reads=rd, writes=[("acc", cc)])
                n -= 1
            in_drain[0] = False

        _add = S.add

        def add_hook(eng, fn, reads=(), writes=(), dma=False):
            op = _add(eng, fn, reads=reads, writes=writes, dma=dma)
            if eng == "dve" and not dma and not in_drain[0]:
                drain(1)
            return op
        S.add = add_hook

        pending_stat = []

        def conv_statmm():
            while pending_stat:
                cc, ys = pending_stat.pop(0)
                S.add("pe", (lambda cc: lambda e: e.matmul(banks[2][:], lhsT=ONL[:], rhs=yb[:, cc, :], start=(cc == 0),
                                                            stop=(cc == 3)))(cc),
                      reads=[("yb", cc), "ONL"], writes=[BK(2)])
                S.add("pe", (lambda cc, ys: lambda e: e.matmul(banks[3][:], lhsT=ONL[:], rhs=ysq[ys][:], start=(cc == 0),
                                                                stop=(cc == 3)))(cc, ys),
                      reads=[("ysq", ys), "ONL"], writes=[BK(3)])

        def conv_stats():
            S.add("act", lambda e: e.activation(out=m2[:], in_=banks[2][:], func=AF.Square), reads=[BK(2)], writes=["m2"])
            S.add("dve", lambda e: e.scalar_tensor_tensor(out=lnr[:], in0=banks[3][:], scalar=1e-5, in1=m2[:],
                                                          op0=ALU.add, op1=ALU.subtract),
                  reads=[BK(3), "m2"], writes=["lnr"])
            S.add("act", lambda e: e.activation(out=lnr[:], in_=lnr[:], func=AF.Ln), reads=["lnr"], writes=["lnr"])
            S.add("act", lambda e: e.activation(out=lnr[:], in_=lnr[:], func=AF.Exp, scale=-0.5), reads=["lnr"], writes=["lnr"])
            S.add("dve", lambda e: e.scalar_tensor_tensor(out=nmr[:], in0=banks[2][:], scalar=-1.0, in1=lnr[:],
                                                          op0=ALU.mult, op1=ALU.mult),
                  reads=[BK(2), "lnr", "m2"], writes=["m2"])

        def conv_epi(qslot, cc):
            if True:
                S.add("dve", (lambda cc: lambda e: e.tensor_tensor(out=t2[:], in0=yb[:, cc, :], in1=lnr[:],
                                                                    op=ALU.mult))(cc),
                      reads=[("yb", cc), "lnr"], writes=["t2"])
                S.add("dve", lambda e: e.tensor_tensor(out=t2[:], in0=t2[:], in1=nmr[:], op=ALU.add),
                      reads=["t2", "m2"], writes=["t2"])
                t = nxt("th", 2)
                S.add("act", (lambda cc: lambda e: e.activation(out=ta[:], in_=t2[:], func=AF.Identity,
                                                                 scale=v4[:, 4 + cc:5 + cc], bias=v4[:, 8 + cc:9 + cc]))(cc),
                      reads=["t2", "v4"], writes=["ta"])
                S.add("act", (lambda cc, t: lambda e: e.activation(out=th[t][:], in_=t2[:], func=AF.Tanh,
                                                                    scale=v4h[:, cc:cc + 1], bias=v4h[:, 4 + cc:5 + cc]))(cc, t),
                      reads=["t2", "v4h"], writes=[("th", t)])
                S.add("dve", (lambda t: lambda e: e.scalar_tensor_tensor(out=ta[:], in0=th[t][:], scalar=1.0, in1=ta[:],
                                                                          op0=ALU.add, op1=ALU.mult))(t),
                      reads=[("th", t), "ta"], writes=["ta"])
                S.add("dve", (lambda cc: lambda e: e.tensor_tensor(out=yb[:, cc, :], in0=ta[:],
                                                                    in1=gcT[qslot][:, cc, :], op=ALU.mult))(cc),
                      reads=["ta", ("gcT", qslot)], writes=[("yb", cc)])

        def outproj_block(seg, jq, bq):
            qc = jq * 128
            r = nxt("xt", 3)
            dma(xres[r][:], xe[seg, bq * 128:(bq + 1) * 128, :], writes=[("xt", r)])
            for half in range(2):
                ob = 6 + half

                def omm(e, half=half, ob=ob):
                    ins = None
                    for k in range(8):
                        lt = mixT[:, k, qc:qc + 128] if k < 4 else yb[:, k - 4, qc:qc + 128]
                        ins = e.matmul(banks[ob][:], lhsT=lt,
                                       rhs=WO[:, k, half * 512:(half + 1) * 512], start=(k == 0), stop=(k == 7))
                    return ins
                S.add("pe", omm, reads=[("mixA", jq)] + [("yb", cc) for cc in range(4)] + WOALL, writes=[BK(ob)])
                S.add("dve", (lambda half, ob: lambda e: e.tensor_tensor(
                    out=xres[r][:, half * 512:(half + 1) * 512], in0=banks[ob][:],
                    in1=xres[r][:, half * 512:(half + 1) * 512], op=ALU.add))(half, ob),
                    reads=[BK(ob), ("xt", r)], writes=[("xt", r)])
            S.add("pool", lambda e: e.dma_start(out=y[seg, (bq - 1) * 128:bq * 128, :], in_=xres[r][:]),
                  reads=[("xt", r)], dma=True)

        if STOP >= 1:
            norm_T(norm_pre(lambda j: xmeta, 1))
            for jb in projA_jobs(0, 1, "meta"):
                jb()
        S.add("dve", lambda e: e.memset(kTmZ[:], 0.0), writes=[("kTmZ",)])
        for g in range(2):
            for a in range(2):
                S.add("dve", (lambda g, a: lambda e: e.tensor_copy(out=kTmZ[64 * a:64 * a + 64, g, a, :],
                                                                   in_=kTm[64 * a:64 * a + 64, g, 0:48]))(g, a),
                      reads=[("kTm",)], writes=[("kTmZ",)])

        A_list = []
        for seg in range(NSEG if STOP >= 2 else 0):
            A_list.append((seg, 0, "halo", 0, 1))
            for i in range(NST):
                A_list.append((seg, i + 1, "full", 1 + 4 * i, 4))
            A_list.append((seg, NST + 1, "halo", NB + 1, 1))
        B_list = [(seg, i) for seg in range(NSEG if STOP >= 2 else 0) for i in range(NST)]
        pa = [0]
        slot_of = {}

        def src_of(entry):
            seg, k, mode, b0, nblk = entry
            return (lambda seg, b0: lambda j: xe[seg, (b0 + j) * 128:(b0 + j + 1) * 128, :])(seg, b0)

        pre_slots = {}

        def do_pre(idx):
            if idx < len(A_list) and idx not in pre_slots:
                pre_slots[idx] = norm_pre(src_of(A_list[idx]), A_list[idx][4])

        def A_jobs(idx):
            seg, k, mode, b0, nblk = A_list[idx]
            qs = None
            if mode == "full":
                qs = nxt("qslot", 2)
                slot_of[(seg, k)] = qs
            pj = projA_jobs(b0, nblk, mode, qs, seg)

            def first():
                do_pre(idx)
                norm_T(pre_slots[idx])
            return [first, pj[0]] + pj[1:5] + [lambda: do_pre(idx + 1)] + pj[5:]

        def need_idx(seg, i):
            return seg * (NST + 2) + i + 2

        def first_slot(entry, seg, b0):
            eseg, k, mode, eb0, nblk = entry
            if eseg == seg:
                return 0
            last = -1
            for jq in range(4):
                rd = range(b0 + jq - 1, b0 + jq + 2)
                if any(eb0 <= r < eb0 + nblk for r in rd):
                    last = jq
            return last + 1

        for bi, (seg, i) in enumerate(B_list):
            while pa[0] <= need_idx(seg, i):
                for jb in A_jobs(pa[0]):
                    jb()
                pa[0] += 1
            qslot = slot_of[(seg, i + 1)]
            b0 = 1 + 4 * i
            slots = [[] for _ in range(5)]
            if bi + 1 < len(B_list):
                nseg, ni = B_list[bi + 1]
                tgt = need_idx(nseg, ni)
                pend = []
                while pa[0] <= tgt:
                    pend.append(pa[0])
                    pa[0] += 1
                queue = []
                post = False
                for idx in pend:
                    fs = first_slot(A_list[idx], seg, b0)
                    post = post or fs >= 4
                    for jb in A_jobs(idx):
                        queue.append((4 if post else fs, jb))
                n_in = sum(1 for fs, _ in queue if fs < 4)
                quota = max(1, -(-n_in // 4))
                cur, cnt_ = 0, 0
                for fs, jb in queue:
                    if fs >= 4:
                        slots[4].append(jb)
                        continue
                    if fs > cur:
                        cur, cnt_ = fs, 0
                    if cnt_ >= quota and cur < 3:
                        cur, cnt_ = cur + 1, 0
                    slots[cur].append(jb)
                    cnt_ += 1
            if bi == 0:
                conv_queue(seg, b0)
                drain(10 ** 9)
                assert not mac_q
            for cc in range(4):
                conv_mm(b0, cc)
                if cc >= 1:
                    pass
                if cc >= 1:
                    last = pending_stat.pop()
                    conv_statmm()
                    pending_stat.append(last)
            if bi + 1 < len(B_list):
                nseg, ni = B_list[bi + 1]
                conv_queue(nseg, 1 + 4 * ni)
            cxs = []
            for jq in range(4):
                cx = att_scores(seg, qslot, jq, b0 + jq)
                cxs.append(cx)
                if jq == 0:
                    conv_statmm()
                    conv_stats()
                if jq >= 1:
                    att_out(cxs[jq - 1])
                if jq == 3:
                    conv_epi(qslot, jq)
                for jb in slots[jq]:
                    jb()
                if jq < 3:
                    conv_epi(qslot, jq)
                att_pv(cx)
            for jq in range(2):
                outproj_block(seg, jq, b0 + jq)
            att_out(cxs[3])
            for jb in slots[4]:
                jb()
            for jq in range(2, 4):
                outproj_block(seg, jq, b0 + jq)
            drain(10 ** 9)
            assert not mac_q, "conv MACs left whose u blocks were never produced"

        S.finalize(sems, dsems)
        with nc.Block() as block:
            @block.tensor
            def _(e):
                S.emit("pe", e)

            @block.scalar
            def _(e):
                S.emit("act", e)

            @block.vector
            def _(e):
                S.emit("dve", e)

            @block.gpsimd
            def _(e):
                S.emit("pool", e)

            @block.sync
            def _(e):
                S.emit("sp", e)
                S.final_waits(e)
    return nc


def _const_tables():
    i = np.arange(128)
    jj, ii = np.meshgrid(i, i, indexing="ij")
    slopes = np.exp2(-8.0 * np.arange(1, N_HEADS + 1) / N_HEADS).astype(np.float32)
    tab = np.empty((128, 3, N_HEADS, 128), np.float32)
    dl, vl = 128 + ii - jj, jj >= ii
    dm = np.abs(ii - jj)
    dr, vr = 128 + jj - ii, jj <= ii
    order = [0, 2, 1, 3, 4, 6, 5, 7]
    for sl, h in enumerate(order):
        tab[:, 0, sl, :] = np.where(vl, -slopes[h] * dl, NEG)
        tab[:, 1, sl, :] = -slopes[h] * dm
        tab[:, 2, sl, :] = np.where(vr, -slopes[h] * dr, NEG)
    bf = ml_dtypes.bfloat16
    ident = np.eye(128, dtype=np.float32).astype(bf)
    bd = np.kron(np.eye(2, dtype=np.float32), np.full((64, 64), 1.0 / 64, np.float32)).astype(bf)
    onesln = np.full((128, 128), 1.0 / 512, np.float32).astype(bf)
    return tab.reshape(128, -1).astype(bf), ident, bd, onesln


def _segment(xseq, meta, t0, n):
    S_ = xseq.shape[0]
    if t0 > 0:
        left, hl = xseq[t0 - 128:t0], 0.0
    else:
        left = np.concatenate([np.zeros((128 - N_META, D_MODEL), np.float32), meta], 0)
        hl = NEG
    if t0 + n < S_:
        right, hr = xseq[t0 + n:t0 + n + 128], 0.0
    else:
        right, hr = np.zeros((128, D_MODEL), np.float32), NEG
    return np.concatenate([left, xseq[t0:t0 + n], right], 0), hl, hr


def _core_inputs(segs, meta, weights, consts):
    xs, hbv = [], []
    for (xseq, t0, n) in segs:
        x_, hl, hr = _segment(xseq, meta, t0, n)
        xs.append(x_)
        hbv += [hl, hr]
    xmeta = np.zeros((128, D_MODEL), np.float32)
    xmeta[0:N_META] = meta
    xmeta[32:32 + N_META] = meta
    d = dict(weights)
    d.update(consts)
    d["xe"] = np.ascontiguousarray(np.stack(xs, 0))
    d["xmeta"] = xmeta
    d["hb"] = np.ascontiguousarray(np.broadcast_to(np.asarray(hbv, np.float32)[None, :], (128, len(hbv))))
    return d


def _pack_weights(norm_w, w_in, q_norm_w, k_norm_w, sink_logits, conv_w, conv_b, conv_ln_w, conv_ln_b, w_out):
    f = np.float32
    return {
        "w_in": np.ascontiguousarray(w_in[0], f),
        "w_out": np.ascontiguousarray(w_out[0], f),
        "normw_pk": np.ascontiguousarray(norm_w[0].reshape(8, 128).T, f),
        "qkw": np.ascontiguousarray(np.stack([np.tile(q_norm_w[0], 2), np.tile(k_norm_w[0], 2)], 1), f),
        "convw_pk": np.ascontiguousarray(conv_w[0].reshape(CONV_K, 4, 128).transpose(2, 1, 0).reshape(128, 4 * CONV_K), f),
        "vec4": np.ascontiguousarray(np.concatenate([conv_b[0].reshape(4, 128).T, conv_ln_w[0].reshape(4, 128).T,
                                                     conv_ln_b[0].reshape(4, 128).T], 1), f),
        "sink_b": np.ascontiguousarray(np.broadcast_to(sink_logits[0][None, :], (128, N_HEADS)), f),
    }


_NC_CACHE = {}


def kernel(x_prompt, x_sample, meta_tokens, norm_w, w_in, q_norm_w, k_norm_w, sink_logits,
           conv_w, conv_b, conv_ln_w, conv_ln_b, w_out):
    NB, NSEG, NCORE = 8, 8, 8
    n = NB * 128
    xp = np.asarray(x_prompt, np.float32)
    xs = np.asarray(x_sample, np.float32)
    meta = np.asarray(meta_tokens, np.float32)
    weights = _pack_weights(*[np.asarray(a, np.float32) for a in
                              (norm_w, w_in, q_norm_w, k_norm_w, sink_logits, conv_w, conv_b, conv_ln_w, conv_ln_b, w_out)])
    tab, ident, bd, onesln = _const_tables()
    consts = {"bias_c": tab, "ident_c": ident, "bd_c": bd, "onesln_c": onesln}
    in_maps, place = [], []
    for c in range(NCORE):
        segs, pl = [], []
        for sq_ in (2 * c, 2 * c + 1):
            for t0 in (0, n):
                segs.append((xp[sq_], t0, n))
                pl.append((0, sq_, t0))
        sseq, base = c // 4, (c % 4) * 4 * n
        for q in range(4):
            segs.append((xs[sseq], base + q * n, n))
            pl.append((1, sseq, base + q * n))
        in_maps.append(_core_inputs(segs, meta, weights, consts))
        place.append(pl)
    key = (NSEG, NB)
    if key not in _NC_CACHE:
        _NC_CACHE[key] = _build(NSEG, NB)
    res = run_bass_kernel_spmd(_NC_CACHE[key], in_maps, core_ids=list(range(NCORE)))
    yp = np.empty(xp.shape, np.float32)
    ys = np.empty(xs.shape, np.float32)
    for c in range(NCORE):
        yc = res.results[c]["y"]
        for s_, (which, sq_, t0) in enumerate(place[c]):
            (yp if which == 0 else ys)[sq_, t0:t0 + n] = yc[s_]
    return (yp, ys)
```
